# Optimizing a Trainium2 kernel written in Bass

```python
import jax, jax.numpy as jnp
from jax import lax
import numpy as np

D_MODEL = 1024
BATCH = 1
SEQ = 16384
DEPTH = 1
DEC_BATCH = 32
DEC_SEQ = 1
PAST_LEN = 16384
PAGE_SIZE = 128

HEAD_DIM_A = 64
HEADS_PER_GROUP_A = 4
DILATED_GROUPS = ((128, 1), (512, 4), (2048, 16))
N_GROUPS_A = len(DILATED_GROUPS)
N_HEADS_A = HEADS_PER_GROUP_A * N_GROUPS_A
WIDTH_A = N_HEADS_A * HEAD_DIM_A
COMB_WIDTH_A = HEADS_PER_GROUP_A * HEAD_DIM_A
CHUNK = 128
N_GROUPS_B = 4
WIDTH_B = 768
GROUP_DIM_B = WIDTH_B // N_GROUPS_B
N_MEM = 256
N_HEADS_M = 4
HEAD_DIM_M = 128
WIDTH_M = N_HEADS_M * HEAD_DIM_M
N_BRANCH = 3
IN_WIDTH = 3 * WIDTH_A + 2 * WIDTH_B + WIDTH_M + N_BRANCH * D_MODEL
D_FF = 2816
CONV_W = 3
LN_EPS = 1e-5
ALPHA = (2.0 * DEPTH) ** 0.25
BETA = (8.0 * DEPTH) ** -0.25
NEG = -1e30

kernel_name = "hybrid_dilated_gmlp_memory_decoder_step"


def layer_norm(x, g, b):
    xf = x.astype(jnp.float32)
    mu = jnp.mean(xf, axis=-1, keepdims=True)
    var = jnp.mean(jnp.square(xf - mu), axis=-1, keepdims=True)
    return ((xf - mu) * lax.rsqrt(var + LN_EPS) * g + b).astype(x.dtype)


def split_in(h):
    bounds = np.cumsum([WIDTH_A, WIDTH_A, WIDTH_A, WIDTH_B, WIDTH_B, WIDTH_M]).tolist()
    return jnp.split(h, bounds, axis=-1)


def heads(t, n_heads, head_dim):
    return t.reshape(t.shape[:-1] + (n_heads, head_dim))


def group_slice(t, g):
    return t[:, :, g * HEADS_PER_GROUP_A:(g + 1) * HEADS_PER_GROUP_A]


def dilated_attn_prompt(q, k, v, window, dilation):
    B, S, H, E = q.shape
    nk = window // dilation
    unit = dilation * nk
    Sp = -(-S // unit) * unit
    nb = Sp // unit
    pad = ((0, 0), (0, Sp - S), (0, 0), (0, 0))

    def to_sub(t):
        return jnp.pad(t, pad).reshape(B, nb, nk, dilation, H, E)

    def with_prev(t):
        prev = jnp.pad(t, ((0, 0), (1, 0), (0, 0), (0, 0), (0, 0), (0, 0)))[:, :-1]
        return jnp.concatenate([prev, t], axis=2)

    qs = to_sub(q)
    kk = with_prev(to_sub(k))
    vv = with_prev(to_sub(v))
    s = jnp.einsum('bnidhe,bnjdhe->bndhij', qs, kk,
                   preferred_element_type=jnp.float32) * (E ** -0.5)
    i = jnp.arange(nk)[:, None]
    j = jnp.arange(2 * nk)[None, :]
    rel = i + nk - j
    blk = jnp.arange(nb)[:, None, None]
    valid = (rel >= 0) & (rel <= nk) & (blk * nk + j - nk >= 0)
    s = jnp.where(valid[None, :, None, None], s, NEG)
    lse = jax.nn.logsumexp(s, axis=-1)
    p = jnp.exp(s - lse[..., None]).astype(v.dtype)
    o = jnp.einsum('bndhij,bnjdhe->bnidhe', p, vv)
    o = o.reshape(B, Sp, H, E)[:, :S]
    lse = lse.transpose(0, 1, 4, 2, 3).reshape(B, Sp, H)[:, :S]
    return o, lse


def dilated_attn_sample(q, k_new, v_new, k_buf, v_buf, window, dilation):
    T, E = q.shape[1], q.shape[-1]
    Wb = k_buf.shape[1]
    nk = window // dilation
    keys = jnp.concatenate([k_buf, k_new], axis=1)
    vals = jnp.concatenate([v_buf, v_new], axis=1)
    i = jnp.arange(T)[:, None]
    kk = jnp.arange(nk + 1)[None, :]
    pos = PAST_LEN + i - kk * dilation
    idx = pos - (PAST_LEN - Wb)
    valid = (pos >= 0) & (idx >= 0)
    idx = jnp.clip(idx, 0, Wb + T - 1)
    kg = jnp.take(keys, idx, axis=1)
    vg = jnp.take(vals, idx, axis=1)
    s = jnp.einsum('bthe,btkhe->bthk', q, kg,
                   preferred_element_type=jnp.float32) * (E ** -0.5)
    s = jnp.where(valid[None, :, None, :], s, NEG)
    lse = jax.nn.logsumexp(s, axis=-1)
    p = jnp.exp(s - lse[..., None]).astype(v_new.dtype)
    o = jnp.einsum('bthk,btkhe->bthe', p, vg)
    return o, lse


def combine_dilations(outs, lses):
    wts = jax.nn.softmax(jnp.stack(lses, axis=0), axis=0)
    o = jnp.einsum('gbsh,gbshe->bshe', wts, jnp.stack(outs, axis=0).astype(jnp.float32))
    B, S = o.shape[:2]
    return o.reshape(B, S, COMB_WIDTH_A).astype(outs[0].dtype)


def gmlp_branch(u, v, ln_g, ln_b, w_s, b_s, rows):
    u = jax.nn.gelu(u)
    v = layer_norm(jax.nn.gelu(v), ln_g, ln_b)
    B, S, W = u.shape
    n = S // rows
    mask = jnp.tril(jnp.ones((rows, rows), dtype=bool))
    ws = jnp.where(mask, w_s[:, :rows, :rows], 0.0).astype(v.dtype)
    vs = v.reshape(B, n, rows, N_GROUPS_B, GROUP_DIM_B)
    mixed = jnp.einsum('gij,bnjgc->bnigc', ws, vs) + b_s[:, :rows].T[:, :, None]
    return u * mixed.reshape(B, S, W), v


def memory_attn(q, mk, mv):
    B, S, H, E = q.shape
    s = jnp.einsum('bshe,bmhe->bshm', q, mk, preferred_element_type=jnp.float32) * (E ** -0.5)
    p = jax.nn.softmax(s, axis=-1).astype(mv.dtype)
    return jnp.einsum('bshm,bmhe->bshe', p, mv).reshape(B, S, WIDTH_M)


def merge_and_ffn(x, o_a, o_b, o_m, gates, conv_state, b_gate, w_ba, w_bb, w_bm, w_out,
                  ln1_g, ln1_b, w_up, conv_w, conv_b, w_down, ln2_g, ln2_b):
    g = jax.nn.sigmoid(gates.reshape(gates.shape[:-1] + (N_BRANCH, D_MODEL)) + b_gate)
    mixed = g[..., 0, :] * (o_a @ w_ba) + g[..., 1, :] * (o_b @ w_bb) + g[..., 2, :] * (o_m @ w_bm)
    x1 = layer_norm(ALPHA * x + mixed @ w_out, ln1_g, ln1_b)
    a, val = jnp.split(x1 @ w_up, 2, axis=-1)
    a_ext = jnp.concatenate([conv_state.astype(a.dtype), a], axis=1)
    S = a.shape[1]
    conv = conv_b + sum(conv_w[k] * a_ext[:, k:k + S] for k in range(CONV_W))
    h = jax.nn.gelu(conv) * val
    y = layer_norm(ALPHA * x1 + h @ w_down, ln2_g, ln2_b)
    return y, a_ext[:, a_ext.shape[1] - (CONV_W - 1):]


def setup_inputs(seed: int = 0) -> dict:
    key = jax.random.key(seed)
    ks = jax.random.split(key, 32)
    f32 = jnp.float32
    nrm = lambda k, shape, scale: jax.random.normal(k, shape, f32) * scale
    win_lens = [min(w, PAST_LEN) for (w, _) in DILATED_GROUPS]
    return {
        "x_prompt": nrm(ks[0], (BATCH, SEQ, D_MODEL), 1.0),
        "x_sample": nrm(ks[1], (DEC_BATCH, DEC_SEQ, D_MODEL), 1.0),
        "mem_prompt": nrm(ks[2], (BATCH, N_MEM, D_MODEL), 1.0),
        "cache_win128_kv": nrm(ks[3], (DEPTH, DEC_BATCH, win_lens[0], 2, HEADS_PER_GROUP_A, HEAD_DIM_A), 1.0),
        "cache_win512_kv": nrm(ks[4], (DEPTH, DEC_BATCH, win_lens[1], 2, HEADS_PER_GROUP_A, HEAD_DIM_A), 1.0),
        "cache_win2048_kv": nrm(ks[5], (DEPTH, DEC_BATCH, win_lens[2], 2, HEADS_PER_GROUP_A, HEAD_DIM_A), 1.0),
        "cache_mem_kv": nrm(ks[6], (DEPTH, DEC_BATCH, N_MEM, 2, N_HEADS_M, HEAD_DIM_M), 1.0),
        "state_ffn_conv": nrm(ks[7], (DEPTH, DEC_BATCH, CONV_W - 1, D_FF), 1.0),
        "w_in": nrm(ks[8], (DEPTH, D_MODEL, IN_WIDTH), D_MODEL ** -0.5),
        "b_gate": nrm(ks[9], (DEPTH, N_BRANCH, D_MODEL), 0.02),
        "ln_v_g": 1.0 + nrm(ks[10], (DEPTH, WIDTH_B), 0.02),
        "ln_v_b": nrm(ks[11], (DEPTH, WIDTH_B), 0.02),
        "w_spatial": nrm(ks[12], (DEPTH, N_GROUPS_B, CHUNK, CHUNK), CHUNK ** -0.5),
        "b_spatial": 1.0 + nrm(ks[13], (DEPTH, N_GROUPS_B, CHUNK), 0.02),
        "w_mem_kv": nrm(ks[14], (DEPTH, D_MODEL, 2 * WIDTH_M), D_MODEL ** -0.5),
        "w_branch_a": nrm(ks[15], (DEPTH, COMB_WIDTH_A, D_MODEL), COMB_WIDTH_A ** -0.5),
        "w_branch_b": nrm(ks[16], (DEPTH, WIDTH_B, D_MODEL), WIDTH_B ** -0.5),
        "w_branch_m": nrm(ks[17], (DEPTH, WIDTH_M, D_MODEL), WIDTH_M ** -0.5),
        "w_out": nrm(ks[18], (DEPTH, D_MODEL, D_MODEL), BETA * D_MODEL ** -0.5),
        "ln1_g": 1.0 + nrm(ks[19], (DEPTH, D_MODEL), 0.02),
        "ln1_b": nrm(ks[20], (DEPTH, D_MODEL), 0.02),
        "w_up": nrm(ks[21], (DEPTH, D_MODEL, 2 * D_FF), D_MODEL ** -0.5),
        "conv_w": nrm(ks[22], (DEPTH, CONV_W, D_FF), CONV_W ** -0.5),
        "conv_b": nrm(ks[23], (DEPTH, D_FF), 0.02),
        "w_down": nrm(ks[24], (DEPTH, D_FF, D_MODEL), BETA * D_FF ** -0.5),
        "ln2_g": 1.0 + nrm(ks[25], (DEPTH, D_MODEL), 0.02),
        "ln2_b": nrm(ks[26], (DEPTH, D_MODEL), 0.02),
    }


def reference(x_prompt, x_sample, mem_prompt, cache_win128_kv, cache_win512_kv, cache_win2048_kv,
              cache_mem_kv, state_ffn_conv, w_in, b_gate, ln_v_g, ln_v_b, w_spatial, b_spatial,
              w_mem_kv, w_branch_a, w_branch_b, w_branch_m, w_out, ln1_g, ln1_b, w_up, conv_w,
              conv_b, w_down, ln2_g, ln2_b):
    win_caches = (cache_win128_kv, cache_win512_kv, cache_win2048_kv)
    yp, ys = x_prompt, x_sample
    win_p = [[] for _ in range(N_GROUPS_A)]
    win_s = [[] for _ in range(N_GROUPS_A)]
    mem_p, conv_p_list, gmlp_s, conv_s_list = [], [], [], []
    for l in range(DEPTH):
        B, S = yp.shape[:2]
        qa, ka, va, ub, vb, qm, gp = split_in(yp @ w_in[l])
        qa, ka, va = (heads(t, N_HEADS_A, HEAD_DIM_A) for t in (qa, ka, va))
        outs, lses = [], []
        for g, (win, dil) in enumerate(DILATED_GROUPS):
            qg, kg, vg = group_slice(qa, g), group_slice(ka, g), group_slice(va, g)
            o, lse = dilated_attn_prompt(qg, kg, vg, win, dil)
            outs.append(o)
            lses.append(lse)
            keep = min(win, S)
            win_p[g].append(jnp.stack([kg[:, S - keep:], vg[:, S - keep:]], axis=2))
        o_a = combine_dilations(outs, lses)
        o_b, _ = gmlp_branch(ub, vb, ln_v_g[l], ln_v_b[l], w_spatial[l], b_spatial[l], CHUNK)
        mkv = (mem_prompt @ w_mem_kv[l]).reshape(B, N_MEM, 2, N_HEADS_M, HEAD_DIM_M)
        mem_p.append(mkv)
        o_m = memory_attn(heads(qm, N_HEADS_M, HEAD_DIM_M), mkv[:, :, 0], mkv[:, :, 1])
        conv0 = jnp.zeros((B, CONV_W - 1, D_FF), yp.dtype)
        yp_next, conv_p = merge_and_ffn(yp, o_a, o_b, o_m, gp, conv0, b_gate[l], w_branch_a[l],
                                        w_branch_b[l], w_branch_m[l], w_out[l], ln1_g[l], ln1_b[l],
                                        w_up[l], conv_w[l], conv_b[l], w_down[l], ln2_g[l], ln2_b[l])
        conv_p_list.append(conv_p)
        T = ys.shape[1]
        qa, ka, va, ub, vb, qm, gs = split_in(ys @ w_in[l])
        qa, ka, va = (heads(t, N_HEADS_A, HEAD_DIM_A) for t in (qa, ka, va))
        outs, lses = [], []
        for g, (win, dil) in enumerate(DILATED_GROUPS):
            qg, kg, vg = group_slice(qa, g), group_slice(ka, g), group_slice(va, g)
            buf = win_caches[g][l]
            o, lse = dilated_attn_sample(qg, kg, vg, buf[:, :, 0], buf[:, :, 1], win, dil)
            outs.append(o)
            lses.append(lse)
            win_s[g].append(jnp.stack([kg, vg], axis=2))
        o_a = combine_dilations(outs, lses)
        o_b, v_rows = gmlp_branch(ub, vb, ln_v_g[l], ln_v_b[l], w_spatial[l], b_spatial[l], T)
        gmlp_s.append(v_rows)
        mkv = cache_mem_kv[l]
        o_m = memory_attn(heads(qm, N_HEADS_M, HEAD_DIM_M), mkv[:, :, 0], mkv[:, :, 1])
        ys_next, conv_s = merge_and_ffn(ys, o_a, o_b, o_m, gs, state_ffn_conv[l], b_gate[l], w_branch_a[l],
                                        w_branch_b[l], w_branch_m[l], w_out[l], ln1_g[l], ln1_b[l],
                                        w_up[l], conv_w[l], conv_b[l], w_down[l], ln2_g[l], ln2_b[l])
        conv_s_list.append(conv_s)
        yp, ys = yp_next, ys_next
    new_win128_kv_prompt = jnp.stack(win_p[0])
    new_win512_kv_prompt = jnp.stack(win_p[1])
    new_win2048_kv_prompt = jnp.stack(win_p[2])
    new_mem_kv_prompt = jnp.stack(mem_p)
    new_ffn_conv_prompt = jnp.stack(conv_p_list)
    new_win128_kv_sample = jnp.stack(win_s[0])
    new_win512_kv_sample = jnp.stack(win_s[1])
    new_win2048_kv_sample = jnp.stack(win_s[2])
    new_gmlp_v_sample = jnp.stack(gmlp_s)
    new_ffn_conv_sample = jnp.stack(conv_s_list)
    return (yp, ys, new_win128_kv_prompt, new_win512_kv_prompt, new_win2048_kv_prompt,
            new_mem_kv_prompt, new_ffn_conv_prompt, new_win128_kv_sample, new_win512_kv_sample,
            new_win2048_kv_sample, new_gmlp_v_sample, new_ffn_conv_sample)
```

```python
import os
import numpy as np
from contextlib import ExitStack
import concourse.bass as bass
import concourse.mybir as mybir
from concourse.bass_utils import run_bass_kernel_spmd

F32 = mybir.dt.float32
BF16 = mybir.dt.bfloat16
AF = mybir.ActivationFunctionType
ALU = mybir.AluOpType
AX = mybir.AxisListType

NCORES = 8
D = 1024
S = 16384
T = S // NCORES
NT = T // 128
TW = T + 8
INW = 7424
DFF = 2816
NJ = DFF // 128
ALPHA = 2.0 ** 0.25
EPS = 1e-5
NEG = -30000.0
GROUPS = ((128, 1), (512, 4), (2048, 16))
DBG = os.environ.get("MK_DEBUG", "")


class Op:
    __slots__ = ("eng", "fn", "reads", "writes", "dma", "deps", "signal", "sigval", "waits", "waitall")

    def __init__(self, eng, fn, reads, writes, dma, waitall):
        self.eng = eng
        self.fn = fn
        self.reads = tuple(reads)
        self.writes = tuple(writes)
        self.dma = dma
        self.deps = ()
        self.signal = False
        self.sigval = None
        self.waits = ()
        self.waitall = waitall


class Sched:
    def __init__(self, nc, stack):
        self.nc = nc
        self.stack = stack
        self.engines = {"pe": nc.tensor, "act": nc.scalar, "dve": nc.vector, "pool": nc.gpsimd, "sp": nc.sync}
        self.esem = {e: stack.enter_context(nc.semaphore("s_" + e)) for e in self.engines}
        self.ecnt = {e: 0 for e in self.engines}
        self.dsem = {}
        self.dcnt = {}
        self.waited = {e: {} for e in self.engines}
        self.ops = []
        self.nins = 0

    enabled = True

    def add(self, eng, fn, reads=(), writes=(), dma=None, waitall=False):
        if self.enabled:
            self.ops.append(Op(eng, fn, reads, writes, dma, waitall))

    def stop_at(self, name):
        if os.environ.get("MK_STOP") == name:
            self.enabled = False

    def _sem(self, key):
        if key[0] == "e":
            return self.esem[key[1]]
        return self.dsem[key[1]]

    def flush(self, barrier=True):
        ops = self.ops
        self.ops = []
        last_w = {}
        readers = {}
        for i, op in enumerate(ops):
            deps = set()
            for k in op.reads:
                if k in last_w:
                    deps.add(last_w[k])
            for k in op.writes:
                if k in last_w:
                    deps.add(last_w[k])
                deps.update(readers.get(k, ()))
            deps.discard(i)
            for k in op.reads:
                readers.setdefault(k, []).append(i)
            for k in op.writes:
                last_w[k] = i
                readers[k] = []
            op.deps = deps

        def skip(pj, op):
            return pj.dma is None and op.dma is None and pj.eng == "pe" and op.eng == "pe"

        for op in ops:
            for j in op.deps:
                pj = ops[j]
                if pj.dma is None and not skip(pj, op):
                    pj.signal = True
        if barrier:
            lastc = {}
            for i, op in enumerate(ops):
                if op.dma is None and op.fn is not None:
                    lastc[op.eng] = i
            for i in lastc.values():
                ops[i].signal = True
        dfinal = dict(self.dcnt)
        for op in ops:
            if op.dma is not None:
                dfinal[op.dma] = dfinal.get(op.dma, 0) + 16
        for op in ops:
            if op.dma is not None:
                if op.dma not in self.dsem:
                    self.dsem[op.dma] = self.stack.enter_context(self.nc.semaphore("d_" + op.dma))
                    self.dcnt[op.dma] = 0
                self.dcnt[op.dma] += 16
                op.sigval = (("d", op.dma), dfinal[op.dma] if op.waitall else self.dcnt[op.dma])
            elif op.signal:
                self.ecnt[op.eng] += 1
                op.sigval = (("e", op.eng), self.ecnt[op.eng])
        for op in ops:
            need = {}
            for j in op.deps:
                pj = ops[j]
                if skip(pj, op):
                    continue
                k, v = pj.sigval
                if need.get(k, 0) < v:
                    need[k] = v
            eng = self.engines[op.eng]
            wd = self.waited[op.eng]
            for k, v in need.items():
                if wd.get(k, 0) < v:
                    eng.wait_ge(self._sem(k), v)
                    wd[k] = v
                    self.nins += 1
            if op.fn is None:
                continue
            ins = op.fn(eng)
            self.nins += 1
            if op.dma is not None:
                ins.then_inc(self.dsem[op.dma], 16)
            elif op.signal:
                ins.then_inc(self.esem[op.eng], 1)
        if barrier:
            for e, eng in self.engines.items():
                wd = self.waited[e]
                for e2 in self.engines:
                    if e2 != e and wd.get(("e", e2), 0) < self.ecnt[e2]:
                        eng.wait_ge(self.esem[e2], self.ecnt[e2])
                        wd[("e", e2)] = self.ecnt[e2]
                for dk, dv in self.dcnt.items():
                    if wd.get(("d", dk), 0) < dv:
                        eng.wait_ge(self.dsem[dk], dv)
                        wd[("d", dk)] = dv


def build_program(stage=99):
    nc = bass.Bass("TRN2", target_bir_lowering=False)
    dram = {}

    def din(name, shape):
        dram[name] = nc.dram_tensor(name, list(shape), F32, kind="ExternalInput").ap()
        return dram[name]

    def dout(name, shape):
        dram[name] = nc.dram_tensor(name, list(shape), F32, kind="ExternalOutput").ap()
        return dram[name]

    xh = din("xh", [2 * T, D])
    hbias_d = din("hbias", [128, 1])
    w_in = din("w_in", [D, INW])
    out_win = [dout("win%d_p" % g, [GROUPS[g][0], 512]) for g in range(3)]
    din("mem", [256, D])
    din("w_mem_kv", [D, D])
    din("ln_v_g", [768])
    din("ln_v_b", [768])
    din("w_spatial", [4, 128, 128])
    din("b_spatial", [4, 128])
    din("w_branch_a", [256, D])
    din("w_branch_b", [768, D])
    din("w_branch_m", [512, D])
    din("b_gate", [3, D])
    din("w_out", [D, D])
    din("ln1_g", [D])
    din("ln1_b", [D])
    dout("new_mem_kv", [256, D])
    din("w_up", [D, 2 * DFF])
    din("conv_w", [3, DFF])
    din("conv_b", [DFF])
    din("w_down", [DFF, D])
    din("ln2_g", [D])
    din("ln2_b", [D])
    din("xd", [8, D])
    din("xbk", [2, 3, 128, D])
    din("bkmask", [128, 6])
    din("bflag", [128, 1])
    din("cwin0", [4, 128, 512])
    din("cwin1", [4, 512, 512])
    din("cwin2", [4, 2048, 512])
    din("cmem", [4, 256, D])
    din("cstate", [4, 2, DFF])
    dout("y_s", [4, D])
    for g_ in range(3):
        dout("win%d_s" % g_, [4, 512])
    dout("gmlp_v_s", [4, 768])
    dout("conv_s", [4, 2, DFF])
    dbg_d = dout("dbg_d", [128, 64]) if "dd" in DBG else None
    dout("y_p", [T, D])
    dout("conv_p", [2, DFF])
    x1s = nc.dram_tensor("x1s", [T + 128, D], F32, kind="Internal").ap()
    dbg_x1 = dout("dbg_x1", [T, D]) if "x1" in DBG else None
    dbg_oa = dout("dbg_oa", [256, T]) if "oa" in DBG else None

    with ExitStack() as st:
        S_ = Sched(nc, st)
        add = S_.add

        def sb(name, shape, dt, stack=st):
            return stack.enter_context(nc.sbuf_tensor(name, list(shape), dt))

        banks = [st.enter_context(nc.psum_tensor("bank%d" % i, [128, 512], F32)) for i in range(7)]
        bankT = st.enter_context(nc.psum_tensor("bankT", [128, 1024], BF16))
        bankT2 = banks[6][:].bitcast(BF16)
        identf = sb("identf", [128, 128], F32)
        ident = sb("ident", [128, 128], BF16)
        zerosb = sb("zerosb", [128, 512], BF16)
        maskP = sb("maskP", [128, 512], BF16)
        maskC = sb("maskC", [128, 512], BF16)
        maskPH = sb("maskPH", [128, 512], BF16)
        onespad = sb("onespad", [128, 2, 128], BF16)
        hbias = sb("hbias_sb", [128, 1], F32)
        mhalf = sb("mhalf", [128, 1], F32)
        xT = sb("xT", [128, 8, TW], BF16)

        add("pool", lambda e: e.memset(identf[:], 1.0), writes=["identf"])
        add("pool", lambda e: e.affine_select(out=identf[:], in_=identf[:], pattern=[[-1, 128]], compare_op=ALU.is_equal,
                                              fill=0.0, base=0, channel_multiplier=1), reads=["identf"], writes=["identf"])
        add("dve", lambda e: e.tensor_copy(out=ident[:], in_=identf[:]), reads=["identf"], writes=["ident"])
        add("pool", lambda e: e.memset(zerosb[:], 0.0), writes=["zerosb"])
        add("pool", lambda e: e.affine_select(out=maskP[:], in_=zerosb[:], pattern=[[0, 4], [-1, 128]], compare_op=ALU.is_ge,
                                              fill=NEG, base=0, channel_multiplier=1), reads=["zerosb"], writes=["maskP"])
        add("pool", lambda e: e.affine_select(out=maskC[:], in_=zerosb[:], pattern=[[0, 4], [1, 128]], compare_op=ALU.is_ge,
                                              fill=NEG, base=0, channel_multiplier=-1), reads=["zerosb"], writes=["maskC"])
        add("sp", lambda e: e.dma_start(out=hbias[:], in_=hbias_d), writes=["hbias"], dma="c_hb")
        with ExitStack() as lc:
            maskPf = sb("maskPf", [128, 512], F32, lc)
            add("dve", lambda e: e.tensor_copy(out=maskPf[:], in_=maskP[:]), reads=["maskP"], writes=["maskPf"])
            add("dve", lambda e: e.tensor_scalar(out=maskPH[:], in0=maskPf[:], scalar1=hbias[:, 0:1], scalar2=None, op0=ALU.add),
                reads=["maskPf", "hbias"], writes=["maskPH"])
            S_.flush(barrier=True)
        add("pool", lambda e: e.memset(mhalf[:], -0.5), writes=["mhalf"])
        add("pool", lambda e: e.memset(onespad[:], 0.0), writes=["onespad"])
        add("pool", lambda e: e.memset(onespad[:, 0, 0:64], 1.0), reads=["onespad"], writes=["onespad"])
        add("pool", lambda e: e.memset(onespad[:, 1, 64:128], 1.0), reads=["onespad"], writes=["onespad"])

        S_.stop_at("consts")
        w_in_v = w_in.rearrange("(k p) c -> p k c", p=128)
        xh_v = xh.rearrange("(t p) c -> p t c", p=128)

        class Rot:
            def __init__(self, ids):
                self.ids = ids
                self.i = 0

            def next(self):
                b = self.ids[self.i % len(self.ids)]
                self.i += 1
                return b

        evac_tog = [0]

        def evac_eng():
            evac_tog[0] ^= 1
            return "act" if evac_tog[0] else "dve"

        def copy_op(eng, out, in_, reads, writes):
            if eng == "act":
                add("act", lambda e: e.activation(out=out, in_=in_, func=AF.Identity), reads=reads, writes=writes)
            else:
                add(eng, lambda e: e.tensor_copy(out=out, in_=in_), reads=reads, writes=writes)

        with ExitStack() as l1:
            o_aT = sb("o_aT", [128, 2, TW], BF16, l1)
            with ExitStack() as l2:
                xb = [sb("xb%d" % i, [128, 2, D], BF16, l2) for i in range(2)]
                xTh = sb("xTh", [128, 8, 512], BF16, l2)
                wqkv = [sb("wqkv%d" % i, [128, 8, 768], BF16, l2) for i in range(2)]
                qpad = sb("qpad", [128, 2, 2, T], BF16, l2)
                kT = sb("kT", [128, 2, 2 * T], BF16, l2)
                vpad = sb("vpad", [128, 32, 4, 128], BF16, l2)
                acc = sb("acc", [128, 4, T], F32, l2)
                PT = [sb("PT%d" % i, [128, 2, 512], BF16, l2) for i in range(2)]
                kvst = [sb("kvst0", [128, 512], F32, l2), None]

                kvst[1] = PT[1][:].rearrange("p a b -> p (a b)").bitcast(F32)
                xdb = PT[0][:].rearrange("p a b -> p (a b)")
                xdTp = sb("xdTp", [128, 8, 128], BF16, l2)
                dfm = sb("dfm", [128, 3, 6, 8], F32, l2)
                dqpad = sb("dqpad", [128, 3, 2, 2, 8], BF16, l2)
                ck = sb("ck", [128, 6, 512], BF16, l2)
                ckT = sb("ckT", [128, 12, 128], BF16, l2)
                cvpad = sb("cvpad", [128, 6, 4, 128], BF16, l2)
                dP = sb("dP", [128, 32], BF16, l2)
                dacc = sb("dacc", [128, 2, 2, 8], F32, l2)
                dprod = sb("dprod", [128, 2, 8], F32, l2)
                dpo = sb("dpo", [128, 2, 8], F32, l2)
                dov = sb("dov", [128, 2, 8], F32, l2)
                headsel = sb("headsel", [128, 128], F32, l2)
                bkm = sb("bkm", [128, 6], F32, l2)
                def kvst_ap(i):
                    return kvst[0][:] if i == 0 else kvst[1]
                xb_i = [0]

                def load_xT(rows_ap_fn, ntiles, dst, dst_key):
                    for t0 in range(0, ntiles, 2):
                        slot = xb_i[0] % 2
                        xb_i[0] += 1
                        src = rows_ap_fn(t0, 2)
                        add("pool", lambda e, slot=slot, src=src: e.dma_start(out=xb[slot][:], in_=src),
                            writes=["xb%d" % slot], dma="xb%d" % slot)
                        if zero_jobs:
                            zj = zero_jobs.pop(0)
                            add("pool", lambda e, zj=zj: e.memset(zj[0], 0.0), writes=[zj[1]])
                        for tt in range(2):
                            t = t0 + tt
                            bt, btk = (bankT[:], "bankT") if tt == 0 else (bankT2, "bank6")
                            for k in range(8):
                                add("pe", lambda e, slot=slot, tt=tt, k=k, bt=bt: e.transpose(
                                    out=bt[:, k * 128:(k + 1) * 128], in_=xb[slot][:, tt, k * 128:(k + 1) * 128], identity=ident[:]),
                                    reads=["xb%d" % slot, "ident"], writes=[btk])
                            copy_op(evac_eng(), dst[:, :, t * 128:(t + 1) * 128], bt.rearrange("p (k c) -> p k c", k=8),
                                    [], ["%s:%d" % (dst_key, t), btk])

                zero_jobs = [(qpad[:, c], "qpad_z%d" % c) for c in range(2)] + [(vpad[:, 8 * i:8 * (i + 1)], "vpad_z%d" % i) for i in range(4)]
                def load_wqkv(g):
                    ws_ = g % 2
                    for part, c0 in enumerate((g * 256, 768 + g * 256, 1536 + g * 256)):
                        add("pool", lambda e, part=part, c0=c0, ws_=ws_: e.dma_start(
                            out=wqkv[ws_][:, :, part * 256:(part + 1) * 256], in_=w_in_v[:, :, c0:c0 + 256]),
                            writes=["wqkv%d:%d" % (ws_, part)], dma="wqkv%d_%d" % (ws_, part))
                load_wqkv(2)
                load_xT(lambda t0, n: xh_v[:, 16 + t0:16 + t0 + n, :], 16, xT, "xT")


                add("dve", lambda e: e.memset(xdb, 0.0), writes=["xdb"])
                add("pool", lambda e: e.dma_start(out=xdb[0:8, :], in_=dram["xd"]), reads=["xdb"], writes=["xdb2"], dma="xdb")
                add("pool", lambda e: e.memset(dqpad[:], 0.0), writes=["dqpad_z"])
                add("dve", lambda e: e.memset(cvpad[:], 0.0), writes=["cvpad_z"])
                add("pool", lambda e: e.memset(headsel[:], 0.0), writes=["headsel"])
                add("pool", lambda e: e.memset(headsel[0:64, 0:64], 1.0), reads=["headsel"], writes=["headsel"])
                add("pool", lambda e: e.memset(headsel[64:128, 64:128], 1.0), reads=["headsel"], writes=["headsel"])
                add("sp", lambda e: e.dma_start(out=bkm[:], in_=dram["bkmask"]), writes=["bkm"], dma="c_bkm")
                for k in range(8):
                    add("pe", lambda e, k=k: e.transpose(out=bankT[:, k * 128:(k + 1) * 128], in_=xdb[:, k * 128:(k + 1) * 128], identity=ident[:]),
                        reads=["xdb", "xdb2", "ident"], writes=["bankT"])
                copy_op("dve", xdTp[:], bankT[:].rearrange("p (k c) -> p k c", k=8), [], ["xdTp", "bankT"])
                add("pool", lambda e: e.tensor_copy(out=xT[:, :, T:TW], in_=xdTp[:, :, 0:8]), reads=["xdTp"], writes=["xTd"])

                S_.stop_at("xT")
                rotP = Rot([0, 1, 2, 3])
                wq_i = [0]
                first_group = [True]
                kv_i = [0]
                pt_i = [0]

                for g in (2, 1, 0):
                    win, dil = GROUPS[g]
                    ws = g % 2
                    W = wqkv[ws]
                    if g > 0:
                        load_wqkv(g - 1)
                    wkeys = ["wqkv%d:%d" % (ws, p) for p in range(3)]
                    S_.stop_at("wload%d" % g)
                    nblk_halo = 1
                    if g == 0:
                        halo_tok = 128
                    elif g == 1:
                        halo_tok = 512
                    else:
                        halo_tok = 2048
                    nb = T // win
                    kstride = (nb + 1) * 128

                    def kcol(r, n):
                        return r * kstride + (n + 1) * 128

                    def qcol(r, n):
                        return r * nb * 128 + n * 128

                    def vblk(r, n):
                        return r * (nb + 1) + (n + 1)

                    nh_tiles = halo_tok // 128
                    hbase = T - halo_tok

                    def halo_rows(t0, n, dil=dil, hbase=hbase):
                        if dil == 1:
                            return xh_v[:, (hbase // 128) + t0:(hbase // 128) + t0 + n, :]
                        v = xh[hbase:hbase + 128 * dil, :].rearrange("(i r) c -> i r c", r=dil)
                        return v[:, t0:t0 + n, :]
                    S_.stop_at("halo%d" % g)
                    def perm_dst(buf2d, tt, width_per_r, col_of):
                        if g == 0:
                            return buf2d[:, col_of(0, 0) + tt * 512:col_of(0, 0) + (tt + 1) * 512], None
                        if g == 1:
                            v = buf2d.rearrange("p (r m) -> p r m", r=4)
                            off = col_of(0, tt)
                            return v[:, :, off:off + 128], 4
                        v = buf2d.rearrange("p (r m) -> p r m", r=16)
                        off = col_of(0, 0) + 32 * tt
                        return v[:, :, off:off + 32], 16

                    for c in range(2):
                        for what in ("k", "q"):
                            for tt in range(4):
                                b = rotP.next()
                                wc0 = (256 if what == "k" else 0) + c * 128
                                for k in range(8):
                                    add("pe", lambda e, b=b, k=k, tt=tt, wc0=wc0, W=W: e.matmul(
                                        banks[b][:], lhsT=W[:, k, wc0:wc0 + 128], rhs=xT[:, k, tt * 512:(tt + 1) * 512],
                                        start=(k == 0), stop=(k == 7)),
                                        reads=wkeys + ["xT:%d" % t for t in range(tt * 4, tt * 4 + 4)], writes=["bank%d" % b])
                                if what == "k":
                                    dstv, rr = perm_dst(kT[:, c, 0:dil * kstride], tt, kstride, kcol)
                                    srcv = banks[b][:] if rr is None else banks[b][:].rearrange("p (i r) -> p r i", r=rr)
                                    copy_op(evac_eng(), dstv, srcv, ["bank%d" % b], ["kT:%d:own%d" % (c, tt)])
                                else:
                                    ve = evac_eng()
                                    for hp in range(2):
                                        ps = slice(hp * 64, (hp + 1) * 64)
                                        dstv, rr = perm_dst(qpad[:, c, hp, :], tt, nb * 128, qcol)
                                        dstv = dstv[ps]
                                        srcv = banks[b][ps, :] if rr is None else banks[b][ps, :].rearrange("p (i r) -> p r i", r=rr)
                                        copy_op(ve, dstv, srcv, ["bank%d" % b, "qpad_z0", "qpad_z1"], ["qpad:%d:%d:%d" % (c, hp, tt)])
                    kown = ["kT:%d:own%d" % (c, tt) for c in range(2) for tt in range(4)]
                    qown = ["qpad:%d:%d:%d" % (c, hp, tt) for c in range(2) for hp in range(2) for tt in range(4)]

                    S_.stop_at("kq%d" % g)
                    for r in range(dil):
                        for n in range(nb):
                            b = rotP.next()
                            base = n * win + r
                            for k in range(8):
                                add("pe", lambda e, b=b, k=k, base=base, W=W, dil=dil: e.matmul(
                                    banks[b][:, 0:256], lhsT=xT[:, k, base:base + 127 * dil + 1:dil], rhs=W[:, k, 512:768],
                                    start=(k == 0), stop=(k == 7)),
                                    reads=wkeys + ["xT:%d" % t for t in range(n * win // 128, (n + 1) * win // 128)], writes=["bank%d" % b])
                            blk = vblk(r, n)
                            ve = evac_eng()
                            for hp in range(2):
                                dstv = vpad[:, blk, hp::2, hp * 64:(hp + 1) * 64]
                                srcv = banks[b][:, 0:256].rearrange("p (c q x) -> p c q x", c=2, q=2)[:, :, hp, :]
                                copy_op(ve, dstv, srcv, ["bank%d" % b, "vpad_z0", "vpad_z1", "vpad_z2", "vpad_z3"], ["vpad:%d:%d" % (blk, hp)])

                    S_.stop_at("v%d" % g)
                    for t in range(NT - win // 128, NT):
                        b = rotP.next()
                        for k in range(8):
                            add("pe", lambda e, b=b, k=k, t=t, W=W: e.matmul(
                                banks[b][:, 0:256], lhsT=xT[:, k, t * 128:(t + 1) * 128], rhs=W[:, k, 256:512],
                                start=(k == 0), stop=(k == 7)), reads=wkeys + ["xT:%d" % t], writes=["bank%d" % b])
                        for k in range(8):
                            add("pe", lambda e, b=b, k=k, t=t, W=W: e.matmul(
                                banks[b][:, 256:512], lhsT=xT[:, k, t * 128:(t + 1) * 128], rhs=W[:, k, 512:768],
                                start=(k == 0), stop=(k == 7)), reads=wkeys + ["xT:%d" % t], writes=["bank%d" % b])
                        ks = kv_i[0] % 2
                        kv_i[0] += 1
                        copy_op(evac_eng(), kvst_ap(ks), banks[b][:], ["bank%d" % b], ["kvst%d" % ks])
                        row0 = (t - (NT - win // 128)) * 128
                        add("sp", lambda e, ks=ks, g=g, row0=row0: e.dma_start(out=out_win[g][row0:row0 + 128, :], in_=kvst_ap(ks)),
                            reads=["kvst%d" % ks], writes=["out_kvst%d" % ks], dma="kvst%d" % ks)

                    for hc0 in range(0, nh_tiles, 4):
                        hn = min(4, nh_tiles - hc0)
                        if hn >= 2:
                            load_xT(lambda t0, n, hc0=hc0: halo_rows(hc0 + t0, n), hn, xTh, "xTh")
                        else:
                            slot = xb_i[0] % 2
                            xb_i[0] += 1
                            src = halo_rows(0, 1)
                            add("pool", lambda e, slot=slot, src=src: e.dma_start(out=xb[slot][:, 0:1, :], in_=src),
                                writes=["xb%d" % slot], dma="xb%d" % slot)
                            for k in range(8):
                                add("pe", lambda e, slot=slot, k=k: e.transpose(
                                    out=bankT[:, k * 128:(k + 1) * 128], in_=xb[slot][:, 0, k * 128:(k + 1) * 128], identity=ident[:]),
                                    reads=["xb%d" % slot, "ident"], writes=["bankT"])
                            copy_op(evac_eng(), xTh[:, :, 0:128], bankT[:].rearrange("p (k c) -> p k c", k=8), ["bankT"], ["xTh:0"])
                        S_.stop_at("hload%d" % g)
                        for c in range(2):
                            for h0 in range(0, hn * 128, 512):
                                n = min(512, hn * 128 - h0)
                                b = rotP.next()
                                for k in range(8):
                                    add("pe", lambda e, b=b, c=c, k=k, h0=h0, n=n, W=W: e.matmul(
                                        banks[b][:, 0:n], lhsT=W[:, k, 256 + c * 128:256 + (c + 1) * 128], rhs=xTh[:, k, h0:h0 + n],
                                        start=(k == 0), stop=(k == 7)),
                                        reads=wkeys + ["xTh:%d" % t for t in range(h0 // 128, (h0 + n) // 128)], writes=["bank%d" % b])
                                nr = n // 128
                                r0 = hc0 + h0 // 128
                                dstv = kT[:, c, 0:dil * kstride].rearrange("p (r m) -> p r m", m=kstride)[:, r0:r0 + nr, 0:128]
                                copy_op(evac_eng(), dstv, banks[b][:, 0:n].rearrange("p (r i) -> p r i", i=128),
                                        ["bank%d" % b], ["kT:%d:%d" % (c, r) for r in range(r0, r0 + nr)])
                        S_.stop_at("hk%d" % g)
                        for rl in range(hn):
                            r = hc0 + rl
                            b = rotP.next()
                            for k in range(8):
                                add("pe", lambda e, b=b, k=k, rl=rl, W=W: e.matmul(
                                    banks[b][:, 0:256], lhsT=xTh[:, k, rl * 128:(rl + 1) * 128], rhs=W[:, k, 512:768],
                                    start=(k == 0), stop=(k == 7)), reads=wkeys + ["xTh:%d" % rl], writes=["bank%d" % b])
                            blk = vblk(r, -1)
                            ve = evac_eng()
                            for hp in range(2):
                                dstv = vpad[:, blk, hp::2, hp * 64:(hp + 1) * 64]
                                srcv = banks[b][:, 0:256].rearrange("p (c q x) -> p c q x", c=2, q=2)[:, :, hp, :]
                                copy_op(ve, dstv, srcv, ["bank%d" % b, "vpad_z0", "vpad_z1", "vpad_z2", "vpad_z3"], ["vpad:%d:%d" % (blk, hp)])
                                S_.stop_at("hv%d_%d_%d" % (g, rl, hp))
                        S_.stop_at("hchunk%d" % g)

                    ckk = ["ck:s", "ck:4", "ck:5"]
                    dpk = ["dP:s", "dP:4", "dP:5"]
                    def dec_p1():
                        for part in range(3):
                            for c in range(2):
                                pc = part * 2 + c
                                for k in range(8):
                                    add("pe", lambda e, pc=pc, part=part, c=c, k=k, W=W: e.matmul(
                                        banks[6][:, pc * 8:(pc + 1) * 8], lhsT=W[:, k, part * 256 + c * 128:part * 256 + (c + 1) * 128], rhs=xT[:, k, T:TW],
                                        start=(k == 0), stop=(k == 7)), reads=wkeys + ["xTd"], writes=["bank6"])
                        add("dve", lambda e, g=g: e.tensor_copy(out=dfm[:, g, :, :], in_=banks[6][:, 0:48].rearrange("p (a n) -> p a n", a=6)),
                            writes=["dfm:%d" % g, "bank6"])
                        for c in range(2):
                            for hp in range(2):
                                ps = slice(hp * 64, (hp + 1) * 64)
                                add("pool", lambda e, g=g, c=c, hp=hp, ps=ps: e.tensor_copy(out=dqpad[ps, g, c, hp, :], in_=dfm[ps, g, c, :]),
                                    reads=["dfm:%d" % g, "dqpad_z"], writes=["dqpad:%d" % g])
                    def dec_p2():
                        b = rotP.next()
                        for (part, c0) in ((1, 0), (2, 256)):
                            for k in range(8):
                                add("pe", lambda e, b=b, k=k, part=part, c0=c0, W=W: e.matmul(
                                    banks[b][:, c0:c0 + 256], lhsT=xdTp[:, k, :], rhs=W[:, k, part * 256:(part + 1) * 256],
                                    start=(k == 0), stop=(k == 7)), reads=wkeys + ["xdTp"], writes=["bank%d" % b])
                        ks = 0
                        copy_op(evac_eng(), kvst_ap(ks), banks[b][:], [], ["kvst%d" % ks, "bank%d" % b])
                        add("sp", lambda e, ks=ks, g=g: e.dma_start(out=dram["win%d_s" % g], in_=kvst_ap(ks)[0:4, :]),
                            reads=["kvst%d" % ks], writes=["out_kvst%d" % ks], dma="kvst%d" % ks)
                    def dec_ck():
                        csrc = dram["cwin%d" % g][:, 0:win - dil + 1:dil, :].rearrange("b r x -> r b x")
                        add("pool", lambda e, csrc=csrc: e.dma_start(out=ck[:, 0:4, :], in_=csrc), writes=["ck:s"], dma="ck_s")
                    bslots = []

                    def dec_p3a():
                        for nb_ in range(2):
                            slot = xb_i[0] % 2
                            xb_i[0] += 1
                            bslots.append(slot)
                            add("pool", lambda e, slot=slot, nb_=nb_, g=g: e.dma_start(out=xb[slot][:, 0, :], in_=dram["xbk"][nb_, g]),
                                writes=["xb%d" % slot], dma="xb%d" % slot)

                    def dec_p3():
                        for nb_ in range(2):
                            slot = bslots[nb_]
                            for k in range(8):
                                add("pe", lambda e, slot=slot, k=k: e.transpose(
                                    out=bankT[:, k * 128:(k + 1) * 128], in_=xb[slot][:, 0, k * 128:(k + 1) * 128], identity=ident[:]),
                                    reads=["xb%d" % slot, "ident"], writes=["bankT"])
                            copy_op(evac_eng(), xTh[:, :, 0:128], bankT[:].rearrange("p (k c) -> p k c", k=8), [], ["xTh:0", "bankT"])
                            b = rotP.next()
                            for (part, c0) in ((1, 0), (2, 256)):
                                for k in range(8):
                                    add("pe", lambda e, b=b, k=k, part=part, c0=c0, W=W: e.matmul(
                                        banks[b][:, c0:c0 + 256], lhsT=xTh[:, k, 0:128], rhs=W[:, k, part * 256:(part + 1) * 256],
                                        start=(k == 0), stop=(k == 7)), reads=wkeys + ["xTh:0"], writes=["bank%d" % b])
                            copy_op(evac_eng(), ck[:, 4 + nb_, :], banks[b][:], [], ["ck:%d" % (4 + nb_), "bank%d" % b])
                    def dec_p4():
                        for i8 in range(0, 12, 8):
                            cnt = min(8, 12 - i8)
                            for ii in range(cnt):
                                idx = i8 + ii
                                n_, c_ = idx // 2, idx % 2
                                add("pe", lambda e, ii=ii, n_=n_, c_=c_: e.transpose(
                                    out=bankT[:, ii * 128:(ii + 1) * 128], in_=ck[:, n_, c_ * 128:(c_ + 1) * 128], identity=ident[:]),
                                    reads=ckk + ["ident"], writes=["bankT"])
                            copy_op("dve", ckT[:, i8:i8 + cnt, :], bankT[:, 0:cnt * 128].rearrange("p (k c) -> p k c", k=cnt), [], ["ckT:%d" % i8, "bankT"])
                        for hp in range(2):
                            add("pool", lambda e, hp=hp: e.tensor_copy(
                                out=cvpad[:, :, hp::2, hp * 64:(hp + 1) * 64],
                                in_=ck[:, :, 256:512].rearrange("p n (c q x) -> p n c q x", c=2, q=2)[:, :, :, hp, :]),
                                reads=ckk + ["cvpad_z"], writes=["cvpad:%d" % hp])
                    def dec_p5():
                        for n_ in range(6):
                            for c_ in range(2):
                                for hp in range(2):
                                    col = 64 + n_ * 4 + c_ * 2 + hp
                                    add("pe", lambda e, n_=n_, c_=c_, hp=hp, col=col, g=g: e.matmul(
                                        banks[6][:, col:col + 1], lhsT=ckT[:, n_ * 2 + c_, :], rhs=dqpad[:, g, c_, hp, n_:n_ + 1], start=True, stop=True),
                                        reads=["ckT:0", "ckT:8", "dqpad:%d" % g, "dqpad_z"], writes=["bank6"])
                        add("act", lambda e: e.activation(out=dP[:, 0:16], in_=banks[6][:, 64:80], func=AF.Exp, scale=0.125), writes=["dP:s", "bank6"])
                        for n_ in (4, 5):
                            add("act", lambda e, n_=n_, g=g: e.activation(out=dP[:, n_ * 4:(n_ + 1) * 4], in_=banks[6][:, 64 + n_ * 4:64 + (n_ + 1) * 4], func=AF.Exp,
                                                                           scale=0.125, bias=bkm[:, (n_ - 4) * 3 + g:(n_ - 4) * 3 + g + 1]),
                                reads=["bkm"], writes=["dP:%d" % n_, "bank6"])
                    def dec_p6():
                        for n_ in range(6):
                            for c_ in range(2):
                                for (base, which) in ((128, "o"), (160, "d")):
                                    for hp in range(2):
                                        col = n_ * 4 + c_ * 2 + hp
                                        lhs = cvpad[:, n_, 2 * c_ + hp, :] if which == "o" else onespad[:, hp, :]
                                        add("pe", lambda e, n_=n_, c_=c_, hp=hp, col=col, base=base, lhs=lhs: e.matmul(
                                            banks[6][:, base + c_ * 8 + n_:base + c_ * 8 + n_ + 1], lhsT=lhs, rhs=dP[:, col:col + 1],
                                            start=(hp == 0), stop=(hp == 1)), reads=dpk + ["cvpad:0", "cvpad:1", "cvpad_z", "onespad"], writes=["bank6"])
                    def dec_p7():
                        add("dve", lambda e, g=g: e.tensor_tensor(out=dprod[:], in0=dfm[:, g, 0:2, :], in1=dfm[:, g, 2:4, :], op=ALU.mult),
                            reads=["dfm:%d" % g], writes=["dprod"])
                        add("pe", lambda e: e.matmul(banks[5][:, 0:16], lhsT=headsel[:], rhs=dprod[:].rearrange("p c n -> p (c n)"), start=True, stop=True),
                            reads=["headsel", "dprod"], writes=["bank5"])
                        add("act", lambda e: e.activation(out=dpo[:].rearrange("p c n -> p (c n)"), in_=banks[5][:, 0:16], func=AF.Exp, scale=0.125),
                            writes=["dpo", "bank5"])
                        add("dve", lambda e, g=g: e.tensor_tensor(out=dov[:], in0=dpo[:], in1=dfm[:, g, 4:6, :], op=ALU.mult),
                            reads=["dpo", "dfm:%d" % g], writes=["dov"])
                        add("dve", lambda e: e.tensor_tensor(out=dov[:, :, 0:6], in0=dov[:, :, 0:6], in1=banks[6][:, 128:144].rearrange("p (c n) -> p c n", c=2)[:, :, 0:6], op=ALU.add),
                            reads=["dov"], writes=["dov", "bank6"])
                        add("dve", lambda e: e.tensor_tensor(out=dpo[:, :, 0:6], in0=dpo[:, :, 0:6], in1=banks[6][:, 160:176].rearrange("p (c n) -> p c n", c=2)[:, :, 0:6], op=ALU.add),
                            reads=["dpo"], writes=["dpo", "bank6"])
                        if first_group[0]:
                            add("dve", lambda e: e.tensor_copy(out=dacc[:, 0], in_=dov[:]), reads=["dov"], writes=["dacc"])
                            add("dve", lambda e: e.tensor_copy(out=dacc[:, 1], in_=dpo[:]), reads=["dpo"], writes=["dacc"])
                        else:
                            add("dve", lambda e: e.tensor_tensor(out=dacc[:, 0], in0=dacc[:, 0], in1=dov[:], op=ALU.add), reads=["dov", "dacc"], writes=["dacc"])
                            add("dve", lambda e: e.tensor_tensor(out=dacc[:, 1], in0=dacc[:, 1], in1=dpo[:], op=ALU.add), reads=["dpo", "dacc"], writes=["dacc"])
                    dec_ck()
                    dec_sched = {1: dec_p1, 2: (lambda: (dec_p2(), dec_p3a())), 4: dec_p3, 7: dec_p4, 9: dec_p5, 11: dec_p6, 13: dec_p7}
                    S_.stop_at("kvout%d" % g)
                    rotS = Rot([0, 1, 2, 3])
                    rotO = Rot([4, 5])
                    tiles = [(r, n) for r in range(dil) for n in range(nb)]
                    tinfo = {}

                    def emit_S(ti):
                        r, n = tiles[ti]
                        sbk = [rotS.next(), rotS.next()]
                        ob = rotO.next()
                        pslot = pt_i[0] % 2
                        pt_i[0] += 1
                        tinfo[ti] = (ob, pslot)
                        halo_prev = (n == 0)
                        for bi, (kn, mask) in enumerate(((n - 1, maskPH if halo_prev else maskP), (n, maskC))):
                            b = sbk[bi]
                            mk = "maskPH" if (bi == 0 and halo_prev) else ("maskP" if bi == 0 else "maskC")
                            add("pe", lambda e, b=b, mask=mask: e.matmul(banks[b][:], lhsT=ident[:], rhs=mask[:], start=True, stop=False),
                                reads=["ident", mk], writes=["bank%d" % b])
                            kc0 = kcol(r, kn)
                            kkeys = kown + (["kT:%d:%d" % (c, r) for c in range(2)] if kn < 0 else [])
                            for hp in range(2):
                                for c in range(2):
                                    j = hp * 2 + c
                                    add("pe", lambda e, b=b, c=c, hp=hp, j=j, kc0=kc0, q0=qcol(r, n): e.matmul(
                                        banks[b][:, j * 128:(j + 1) * 128], lhsT=kT[:, c, kc0:kc0 + 128], rhs=qpad[:, c, hp, q0:q0 + 128],
                                        start=False, stop=(j == 3)), reads=kkeys + qown + ["qpad_z0", "qpad_z1"], writes=["bank%d" % b])
                            add("act", lambda e, b=b, pslot=pslot, bi=bi: e.activation(
                                out=PT[pslot][:, bi, :], in_=banks[b][:], func=AF.Exp, scale=0.125),
                                reads=["bank%d" % b] + (["out_kvst1", "kvst1"] if pslot == 1 else []), writes=["PT%d:%d" % (pslot, bi)])

                    def emit_PV(ti):
                        r, n = tiles[ti]
                        ob, pslot = tinfo[ti]
                        pkeys = ["PT%d:0" % pslot, "PT%d:1" % pslot]
                        for c in range(2):
                            i = 0
                            for hp in range(2):
                                h = 2 * c + hp
                                j = hp * 2 + c
                                for bi, kn in enumerate((n - 1, n)):
                                    blk = vblk(r, kn)
                                    add("pe", lambda e, ob=ob, c=c, h=h, j=j, bi=bi, blk=blk, pslot=pslot, i=i: e.matmul(
                                        banks[ob][:, c * 128:(c + 1) * 128], lhsT=vpad[:, blk, h, :], rhs=PT[pslot][:, bi, j * 128:(j + 1) * 128],
                                        start=(i == 0), stop=(i == 3)),
                                        reads=pkeys + ["vpad_z0", "vpad_z1", "vpad_z2", "vpad_z3", "vpad:%d:%d" % (blk, hp)], writes=["bank%d" % ob])
                                    i += 1
                        i = 0
                        for hp in range(2):
                            for bi in range(2):
                                add("pe", lambda e, ob=ob, hp=hp, bi=bi, pslot=pslot, i=i: e.matmul(
                                    banks[ob][:, 256:512], lhsT=onespad[:, hp, :], rhs=PT[pslot][:, bi, hp * 256:(hp + 1) * 256],
                                    start=(i == 0), stop=(i == 3)), reads=pkeys + ["onespad"], writes=["bank%d" % ob])
                                i += 1
                        p0 = n * win + r
                        accv = acc[:, :, p0:p0 + 127 * dil + 1:dil]
                        srcv = banks[ob][:].rearrange("p (a q) -> p a q", a=4)
                        tkeys = ["acc:%d" % t for t in range(n * win // 128, (n + 1) * win // 128)]
                        if first_group[0]:
                            add("dve", lambda e, accv=accv, srcv=srcv: e.tensor_copy(out=accv, in_=srcv),
                                reads=["bank%d" % ob], writes=tkeys)
                        else:
                            add("dve", lambda e, accv=accv, srcv=srcv: e.tensor_tensor(out=accv, in0=accv, in1=srcv, op=ALU.add),
                                reads=["bank%d" % ob] + tkeys, writes=tkeys)
                        if g == 0:
                            sl = slice(n * 128, (n + 1) * 128)
                            add("dve", lambda e, sl=sl: e.reciprocal(out=acc[:, 2:4, sl], in_=acc[:, 2:4, sl]), reads=tkeys, writes=tkeys)
                            add("pool", lambda e, sl=sl: e.tensor_tensor(out=o_aT[:, :, sl], in0=acc[:, 0:2, sl], in1=acc[:, 2:4, sl], op=ALU.mult),
                                reads=tkeys, writes=["o_aT:%d" % n])
                    emit_S(0)
                    for ti in range(len(tiles)):
                        if ti + 1 < len(tiles):
                            emit_S(ti + 1)
                        emit_PV(ti)
                        if ti in dec_sched:
                            dec_sched[ti]()
                    first_group[0] = False
                    S_.stop_at("attn%d" % g)
                    S_.flush(barrier=True)

                add("dve", lambda e: e.reciprocal(out=dacc[:, 1], in_=dacc[:, 1]), reads=["dacc"], writes=["dacc"])
                add("dve", lambda e: e.tensor_tensor(out=o_aT[:, :, T:TW], in0=dacc[:, 0], in1=dacc[:, 1], op=ALU.mult), reads=["dacc"], writes=["o_aT:d"])
                if dbg_d is not None:
                    add("dve", lambda e: e.tensor_copy(out=dprod[:], in_=o_aT[:, :, T:TW]), reads=["o_aT:d"], writes=["dprod"])
                    add("sp", lambda e: e.dma_start(out=dbg_d[:, 0:16], in_=dprod[:].rearrange("p c n -> p (c n)")), reads=["dprod"], writes=["dbg_d"], dma="dbgd")
                if dbg_oa is not None:
                    add("dve", lambda e: e.tensor_copy(out=acc[:, 0:2, :], in_=o_aT[:, :, 0:T]),
                        reads=["o_aT:%d" % n for n in range(16)],
                        writes=["acc:%d" % t for t in range(16)])
                    add("sp", lambda e: e.dma_start(out=dbg_oa.rearrange("(c p) t -> p c t", p=128), in_=acc[:, 0:2, :]),
                        reads=["acc:%d" % t for t in range(16)], writes=["dbg_oa"], dma="dbg")
                S_.flush(barrier=True)
            S_.stop_at("gmlp_start")
            with ExitStack() as l2b:
                o_bT = sb("o_bT", [128, 8, TW], BF16, l2b)
                onesb = sb("onesb", [128, 128], BF16, l2b)
                add("pool", lambda e: e.memset(onesb[:], 1.0), writes=["onesb"])
                UST = [0, 128, 192, 320, 384, 512, 576, 704]
                with ExitStack() as l3:
                    vtok = sb("vtok", [128, NT, 832], BF16, l3)
                    wvb = sb("wvb", [128, 8, 768], BF16, l3)
                    wub = sb("wub", [128, 8, 896], BF16, l3)
                    gtmp = [sb("gtmp%d" % i, [128, 768], F32, l3) for i in range(2)]
                    gtmp3 = gtmp + [sb("gtmp2", [128, 768], F32, l3)]
                    stats3 = [sb("stats3_%d" % i, [128, 2, 6], F32, l3) for i in range(3)]
                    mv3 = [sb("mv3_%d" % i, [128, 4], F32, l3) for i in range(3)]
                    ntmp = [sb("ntmp%d" % i, [128, 768], F32, l3) for i in range(2)]
                    lnvg = sb("lnvg", [128, 768], F32, l3)
                    lnvb = sb("lnvb", [128, 768], F32, l3)
                    stats = [sb("stats%d" % i, [128, 2, 6], F32, l3) for i in range(2)]
                    mv_ = [sb("mv%d" % i, [128, 4], F32, l3) for i in range(2)]
                    wsn = sb("wsn", [128, 4, 128], F32, l3)
                    wsm = sb("wsm", [128, 4, 128], BF16, l3)
                    wsT = sb("wsT", [128, 4, 128], BF16, l3)
                    bs2 = sb("bs2", [128, 512], F32, l3)
                    bshi = sb("bshi", [128, 512], BF16, l3)
                    bshf = sb("bshf", [128, 512], F32, l3)
                    bsrep = sb("bsrep", [128, 512], BF16, l3)
                    ug = [sb("ug%d" % i, [128, 512], F32, l3) for i in range(2)]

                    add("pool", lambda e: e.memset(vtok[:, :, 768:832], 0.0), writes=["vtok_z"])
                    add("pool", lambda e: e.dma_start(out=wvb[:], in_=w_in_v[:, :, 3072:3840]), writes=["wvb"], dma="wvb")
                    add("pool", lambda e: e.dma_start(out=wub[:], in_=w_in_v[:, :, 2304:3200]), writes=["wub"], dma="wub")
                    add("sp", lambda e: e.dma_start(out=lnvg[:], in_=dram["ln_v_g"].partition_broadcast(128)), writes=["lnvg"], dma="c_lnvg")
                    add("sp", lambda e: e.dma_start(out=lnvb[:], in_=dram["ln_v_b"].partition_broadcast(128)), writes=["lnvb"], dma="c_lnvb")
                    add("sp", lambda e: e.dma_start(out=wsn[:], in_=dram["w_spatial"].rearrange("g i j -> i g j")), writes=["wsn"], dma="c_wsn")
                    add("pool", lambda e: e.affine_select(out=wsm[:], in_=wsn[:], pattern=[[0, 4], [-1, 128]], compare_op=ALU.is_ge,
                                                          fill=0.0, base=0, channel_multiplier=1), reads=["wsn"], writes=["wsm"])
                    for gi in range(4):
                        add("pe", lambda e, gi=gi: e.transpose(out=bankT[:, gi * 128:(gi + 1) * 128], in_=wsm[:, gi, :], identity=ident[:]),
                            reads=["wsm", "ident"], writes=["bankT"])
                    copy_op("dve", wsT[:], bankT[:, 0:512].rearrange("p (g i) -> p g i", g=4), [], ["wsT", "bankT"])
                    add("pool", lambda e: e.memset(bs2[:], 0.0), writes=["bs2"])
                    bsrc = dram["b_spatial"].rearrange("(o g) i -> o (g i)", o=1)
                    add("sp", lambda e: e.dma_start(out=bs2[0:1, :], in_=bsrc), reads=["bs2"], writes=["bs2a"], dma="c_bs2a")
                    add("sp", lambda e: e.dma_start(out=bs2[1:2, :], in_=bsrc), reads=["bs2"], writes=["bs2b"], dma="c_bs2b")
                    add("dve", lambda e: e.tensor_copy(out=bshi[:], in_=bs2[:]), reads=["bs2", "bs2a", "bs2b"], writes=["bshi"])
                    add("dve", lambda e: e.tensor_copy(out=bshf[:], in_=bshi[:]), reads=["bshi"], writes=["bshf"])
                    add("dve", lambda e: e.tensor_tensor(out=bs2[:], in0=bs2[:], in1=bshf[:], op=ALU.subtract),
                        reads=["bshf", "bs2a", "bs2b"], writes=["bs2", "bs2a", "bs2b"])
                    add("dve", lambda e: e.tensor_scalar(out=bshf[:], in0=bshf[:], scalar1=identf[:, 0:1], scalar2=None, op0=ALU.mult),
                        reads=["bshf", "identf"], writes=["bshf"])
                    add("dve", lambda e: e.scalar_tensor_tensor(out=bsrep[:], in0=bs2[:], scalar=identf[:, 1:2], in1=bshf[:], op0=ALU.mult, op1=ALU.add),
                        reads=["bshf", "bs2", "identf"], writes=["bsrep"])

                    rotV = Rot([0, 1, 2, 3, 4, 5])

                    def vt_s1(t):
                        b0, b1 = rotV.next(), rotV.next()
                        for (b, c0, cn) in ((b0, 0, 512), (b1, 512, 256)):
                            for k in range(8):
                                add("pe", lambda e, b=b, k=k, t=t, c0=c0, cn=cn: e.matmul(
                                    banks[b][:, 0:cn], lhsT=xT[:, k, t * 128:(t + 1) * 128], rhs=wvb[:, k, c0:c0 + cn],
                                    start=(k == 0), stop=(k == 7)), reads=["wvb", "xT:%d" % t], writes=["bank%d" % b])
                        i3 = t % 3
                        add("act", lambda e, i3=i3, b0=b0: e.activation(out=gtmp3[i3][:, 0:512], in_=banks[b0][:], func=AF.Gelu_apprx_tanh),
                            writes=["gtmp%d" % i3, "bank%d" % b0])
                        add("act", lambda e, i3=i3, b1=b1: e.activation(out=gtmp3[i3][:, 512:768], in_=banks[b1][:, 0:256], func=AF.Gelu_apprx_tanh),
                            reads=["gtmp%d" % i3], writes=["gtmp%d" % i3, "bank%d" % b1])
                        for ci in range(2):
                            add("dve", lambda e, i3=i3, ci=ci: e.bn_stats(out=stats3[i3][:, ci, :], in_=gtmp3[i3][:, ci * 384:(ci + 1) * 384]),
                                reads=["gtmp%d" % i3], writes=["stats3_%d:%d" % (i3, ci)])
                        add("dve", lambda e, i3=i3: e.bn_aggr(out=mv3[i3][:, 0:2], in_=stats3[i3][:].rearrange("p a b -> p (a b)")),
                            reads=["stats3_%d:0" % i3, "stats3_%d:1" % i3], writes=["mv3_%d" % i3])
                        add("dve", lambda e, i3=i3: e.tensor_scalar(out=mv3[i3][:, 2:3], in0=mv3[i3][:, 1:2], scalar1=EPS, scalar2=None, op0=ALU.add),
                            reads=["mv3_%d" % i3], writes=["mv3_%d" % i3])
                        add("pool", lambda e, i3=i3: e.tensor_tensor(out=mv3[i3][:, 2:3], in0=mv3[i3][:, 2:3], in1=mhalf[:, 0:1], op=ALU.pow),
                            reads=["mv3_%d" % i3, "mhalf"], writes=["mv3_%d" % i3])
                        add("dve", lambda e, i3=i3: e.scalar_tensor_tensor(out=mv3[i3][:, 3:4], in0=mv3[i3][:, 0:1], scalar=-1.0, in1=mv3[i3][:, 2:3], op0=ALU.mult, op1=ALU.mult),
                            reads=["mv3_%d" % i3], writes=["mv3_%d" % i3])

                    def vt_s2(t):
                        i3 = t % 3
                        i2 = t % 2
                        add("act", lambda e, i2=i2, i3=i3: e.activation(out=ntmp[i2][:], in_=gtmp3[i3][:], func=AF.Identity, scale=mv3[i3][:, 2:3], bias=mv3[i3][:, 3:4]),
                            reads=["gtmp%d" % i3, "mv3_%d" % i3], writes=["ntmp%d" % i2])
                        add("dve", lambda e, i2=i2: e.tensor_tensor(out=ntmp[i2][:], in0=ntmp[i2][:], in1=lnvg[:], op=ALU.mult),
                            reads=["lnvg"], writes=["ntmp%d" % i2])
                        add("pool", lambda e, i2=i2, t=t: e.tensor_tensor(out=vtok[:, t, 0:768], in0=ntmp[i2][:], in1=lnvb[:], op=ALU.add),
                            reads=["ntmp%d" % i2, "lnvb"], writes=["vtok:%d" % t])
                    vt_s1(0)
                    vt_s1(1)
                    for t in range(NT):
                        if t + 2 < NT:
                            vt_s1(t + 2)
                        vt_s2(t)
                    S_.stop_at("vtok")
                    xdTp2 = sb("xdTp2", [128, 8, 128], BF16, l3)
                    xTH = sb("xTH", [128, 8, 128], BF16, l3)
                    xHb = sb("xHb", [128, D], BF16, l3)
                    dv = sb("dv", [128, 832], F32, l3)
                    vtokH = sb("vtokH", [128, 832], BF16, l3)
                    du = sb("du", [128, 8, 8], F32, l3)
                    dvT = sb("dvT", [128, 8, 8], F32, l3)
                    dmx = sb("dmx", [128, 8, 8], F32, l3)
                    ws00 = sb("ws00", [128, 4], F32, l3)
                    bs00 = sb("bs00", [128, 4], F32, l3)
                    add("pool", lambda e: e.memset(xdTp2[:], 0.0), writes=["xdTp2"])
                    add("pool", lambda e: e.tensor_copy(out=xdTp2[:, :, 0:8], in_=xT[:, :, T:TW]), reads=["xdTp2"], writes=["xdTp2"])
                    add("pool", lambda e: e.memset(dv[:, 768:832], 0.0), writes=["dv_z"])
                    add("pool", lambda e: e.memset(vtokH[:, 768:832], 0.0), writes=["vtokH_z"])
                    add("pool", lambda e: e.memset(dmx[:], 0.0), writes=["dmx"])
                    add("pool", lambda e: e.dma_start(out=xHb[:], in_=xh[T - 128:T, :]), writes=["xHb"], dma="xHb")
                    wsv = dram["w_spatial"]
                    bsv = dram["b_spatial"]
                    add("sp", lambda e: e.dma_start(out=ws00[:], in_=bass.AP(wsv.tensor, 0, [[0, 128], [128 * 128, 4]]), allow_slow_non_contiguous=True),
                        writes=["ws00"], dma="c_ws00")
                    add("sp", lambda e: e.dma_start(out=bs00[:], in_=bass.AP(bsv.tensor, 0, [[0, 128], [128, 4]]), allow_slow_non_contiguous=True),
                        writes=["bs00"], dma="c_bs00")

                    def vb_ln(lhs, lkeys, out_ap, okeys, i2):
                        b0, b1 = rotV.next(), rotV.next()
                        for (b, c0, cn) in ((b0, 0, 512), (b1, 512, 256)):
                            for k in range(8):
                                add("pe", lambda e, b=b, k=k, c0=c0, cn=cn: e.matmul(
                                    banks[b][:, 0:cn], lhsT=lhs[:, k, :], rhs=wvb[:, k, c0:c0 + cn],
                                    start=(k == 0), stop=(k == 7)), reads=["wvb"] + lkeys, writes=["bank%d" % b])
                        add("act", lambda e: e.activation(out=gtmp[i2][:, 0:512], in_=banks[b0][:], func=AF.Gelu_apprx_tanh),
                            writes=["gtmp%d" % i2, "bank%d" % b0])
                        add("act", lambda e: e.activation(out=gtmp[i2][:, 512:768], in_=banks[b1][:, 0:256], func=AF.Gelu_apprx_tanh),
                            reads=["gtmp%d" % i2], writes=["gtmp%d" % i2, "bank%d" % b1])
                        for ci in range(2):
                            add("dve", lambda e, ci=ci: e.bn_stats(out=stats[i2][:, ci, :], in_=gtmp[i2][:, ci * 384:(ci + 1) * 384]),
                                reads=["gtmp%d" % i2], writes=["stats%d:%d" % (i2, ci)])
                        add("dve", lambda e: e.bn_aggr(out=mv_[i2][:, 0:2], in_=stats[i2][:].rearrange("p a b -> p (a b)")),
                            reads=["stats%d:0" % i2, "stats%d:1" % i2], writes=["mv%d" % i2])
                        add("dve", lambda e: e.tensor_scalar(out=mv_[i2][:, 2:3], in0=mv_[i2][:, 1:2], scalar1=EPS, scalar2=None, op0=ALU.add),
                            reads=["mv%d" % i2], writes=["mv%d" % i2])
                        add("pool", lambda e: e.tensor_tensor(out=mv_[i2][:, 2:3], in0=mv_[i2][:, 2:3], in1=mhalf[:, 0:1], op=ALU.pow),
                            reads=["mv%d" % i2, "mhalf"], writes=["mv%d" % i2])
                        add("dve", lambda e: e.scalar_tensor_tensor(out=mv_[i2][:, 3:4], in0=mv_[i2][:, 0:1], scalar=-1.0, in1=mv_[i2][:, 2:3], op0=ALU.mult, op1=ALU.mult),
                            reads=["mv%d" % i2], writes=["mv%d" % i2])
                        add("act", lambda e: e.activation(out=ntmp[i2][:], in_=gtmp[i2][:], func=AF.Identity, scale=mv_[i2][:, 2:3], bias=mv_[i2][:, 3:4]),
                            reads=["gtmp%d" % i2, "mv%d" % i2], writes=["ntmp%d" % i2])
                        add("dve", lambda e: e.tensor_tensor(out=gtmp[i2][:], in0=ntmp[i2][:], in1=lnvg[:], op=ALU.mult),
                            reads=["ntmp%d" % i2, "lnvg"], writes=["gtmp%d" % i2])
                        add("pool", lambda e: e.tensor_tensor(out=out_ap, in0=gtmp[i2][:], in1=lnvb[:], op=ALU.add),
                            reads=["gtmp%d" % i2, "lnvb"], writes=okeys)
                    def gdec_p1():
                        for k in range(8):
                            add("pe", lambda e, k=k: e.transpose(out=bankT[:, k * 128:(k + 1) * 128], in_=xHb[:, k * 128:(k + 1) * 128], identity=ident[:]),
                                reads=["xHb", "ident"], writes=["bankT"])
                        copy_op("dve", xTH[:], bankT[:].rearrange("p (k c) -> p k c", k=8), [], ["xTH", "bankT"])
                    def gdec_p2():
                        vb_ln(xdTp2, ["xdTp2"], dv[:, 0:768], ["dv"], 0)
                    def gdec_p3():
                        vb_ln(xTH, ["xTH"], vtokH[:, 0:768], ["vtokH"], 1)
                        add("sp", lambda e: e.dma_start(out=dram["gmlp_v_s"], in_=dv[0:4, 0:768]), reads=["dv"], writes=["o_gv"], dma="o_gv")
                    def gdec_p4():
                        for cu in range(8):
                            for k in range(8):
                                add("pe", lambda e, cu=cu, k=k: e.matmul(
                                    banks[4][:, cu * 8:(cu + 1) * 8], lhsT=wub[:, k, UST[cu]:UST[cu] + 128], rhs=xT[:, k, T:TW],
                                    start=(k == 0), stop=(k == 7)), reads=["wub", "xTd"], writes=["bank4"])
                        add("act", lambda e: e.activation(out=du[:].rearrange("p a n -> p (a n)"), in_=banks[4][:, 0:64], func=AF.Gelu_apprx_tanh),
                            writes=["du", "bank4"])
                    def gdec_p5():
                        for hb in range(2):
                            for q4 in range(4):
                                cu = hb * 4 + q4
                                add("pe", lambda e, hb=hb, q4=q4, cu=cu: e.transpose(
                                    out=banks[5][:, q4 * 128:(q4 + 1) * 128], in_=dv[:, UST[cu]:UST[cu] + 128], identity=identf[:]),
                                    reads=["dv", "dv_z", "identf"], writes=["bank5"])
                            add("dve", lambda e, hb=hb: e.tensor_copy(out=dvT[:, hb * 4:(hb + 1) * 4, :], in_=banks[5][:].rearrange("p (a t) -> p a t", a=4)[:, :, 0:8]),
                                writes=["dvT:%d" % hb, "bank5"])
                    def gdec_p6():
                        for cu in range(8):
                            add("dve", lambda e, cu=cu: e.tensor_scalar(out=dmx[:, cu, 0:4], in0=dvT[:, cu, 0:4], scalar1=ws00[:, cu // 2:cu // 2 + 1],
                                                                       scalar2=bs00[:, cu // 2:cu // 2 + 1], op0=ALU.mult, op1=ALU.add),
                                reads=["dvT:0", "dvT:1", "ws00", "bs00", "dmx"], writes=["dmx"])
                        for cu in range(8):
                            gi = cu // 2
                            add("pe", lambda e, cu=cu, gi=gi: e.matmul(banks[6][:, cu * 2:cu * 2 + 2], lhsT=onesb[:], rhs=bsrep[:, gi * 128 + 126:gi * 128 + 128], start=True, stop=False),
                                reads=["onesb", "bsrep"], writes=["bank6"])
                            add("pe", lambda e, cu=cu, gi=gi: e.matmul(banks[6][:, cu * 2:cu * 2 + 2], lhsT=vtokH[:, UST[cu]:UST[cu] + 128], rhs=wsT[:, gi, 126:128], start=False, stop=True),
                                reads=["vtokH", "vtokH_z", "wsT"], writes=["bank6"])
                        add("dve", lambda e: e.tensor_copy(out=dmx[:, :, 4:6], in_=banks[6][:, 0:16].rearrange("p (a n) -> p a n", a=8)), reads=["dmx"], writes=["dmx", "bank6"])
                        add("dve", lambda e: e.tensor_tensor(out=o_bT[:, :, T:TW], in0=du[:], in1=dmx[:], op=ALU.mult), reads=["du", "dmx"], writes=["o_bT:d"])
                        if dbg_d is not None:
                            add("dve", lambda e: e.tensor_copy(out=du[:], in_=o_bT[:, :, T:TW]), reads=["o_bT:d"], writes=["du"])
                            add("sp", lambda e: e.dma_start(out=dbg_d[:, 0:64], in_=du[:].rearrange("p a n -> p (a n)")), reads=["du"], writes=["dbg_d"], dma="dbgd")
                    gdec_sched = {2: gdec_p1, 5: gdec_p2, 9: gdec_p3, 13: gdec_p4, 17: gdec_p5, 22: gdec_p6}
                    rotU = Rot([0, 1])
                    rotM = Rot([2, 3])
                    ui = 0
                    for cu in range(8):
                        st0 = UST[cu]
                        gi = cu // 2
                        for tt in range(4):
                            bu, bm = rotU.next(), rotM.next()
                            for k in range(8):
                                add("pe", lambda e, bu=bu, k=k, tt=tt, st0=st0: e.matmul(
                                    banks[bu][:], lhsT=wub[:, k, st0:st0 + 128], rhs=xT[:, k, tt * 512:(tt + 1) * 512],
                                    start=(k == 0), stop=(k == 7)), reads=["wub"] + ["xT:%d" % t for t in range(tt * 4, tt * 4 + 4)], writes=["bank%d" % bu])
                            for n4 in range(4):
                                n = tt * 4 + n4
                                add("pe", lambda e, bm=bm, n4=n4, gi=gi: e.matmul(
                                    banks[bm][:, n4 * 128:(n4 + 1) * 128], lhsT=onesb[:], rhs=bsrep[:, gi * 128:(gi + 1) * 128], start=True, stop=False),
                                    reads=["onesb", "bsrep"], writes=["bank%d" % bm])
                                add("pe", lambda e, bm=bm, n4=n4, gi=gi, n=n, st0=st0: e.matmul(
                                    banks[bm][:, n4 * 128:(n4 + 1) * 128], lhsT=vtok[:, n, st0:st0 + 128], rhs=wsT[:, gi, :], start=False, stop=True),
                                    reads=["vtok:%d" % n, "vtok_z", "wsT"], writes=["bank%d" % bm])
                            u2 = ui % 2
                            ui += 1
                            add("act", lambda e, u2=u2, bu=bu: e.activation(out=ug[u2][:], in_=banks[bu][:], func=AF.Gelu_apprx_tanh),
                                writes=["ug%d" % u2, "bank%d" % bu])
                            add("dve", lambda e, u2=u2, bm=bm, cu=cu, tt=tt: e.tensor_tensor(
                                out=o_bT[:, cu, tt * 512:(tt + 1) * 512], in0=ug[u2][:], in1=banks[bm][:], op=ALU.mult),
                                reads=["ug%d" % u2], writes=["o_bT:%d:%d" % (cu, tt), "bank%d" % bm])
                            if (ui - 1) in gdec_sched:
                                gdec_sched[ui - 1]()
                    S_.flush(barrier=True)
                S_.stop_at("gmlp")
                o_mT = sb("o_mT", [128, 4, TW], BF16, l2b)
                with ExitStack() as l3:
                    memb = sb("memb", [128, 2, D], BF16, l3)
                    memT = sb("memT", [128, 8, 256], BF16, l3)
                    wmem = sb("wmem", [128, 8, D], BF16, l3)
                    wqm = sb("wqm", [128, 8, 512], BF16, l3)
                    w_mem_v = dram["w_mem_kv"].rearrange("(k p) c -> p k c", p=128)
                    for hf in range(2):
                        add("pool", lambda e, hf=hf: e.dma_start(out=wmem[:, :, hf * 512:(hf + 1) * 512], in_=w_mem_v[:, :, hf * 512:(hf + 1) * 512]),
                            writes=["wmem:%d" % hf], dma="wmem%d" % hf)
                    add("pool", lambda e: e.dma_start(out=wqm[:], in_=w_in_v[:, :, 3840:4352]), writes=["wqm"], dma="wqm")
                    mkT = sb("mkT", [128, 4, 256], BF16, l3)
                    mvv = sb("mvv", [128, 2, 512], BF16, l3)
                    mst = [sb("mst%d" % i, [128, 512], F32, l3) for i in range(2)]
                    qmT = sb("qmT", [128, 4, T], BF16, l3)
                    PTm = [sb("PTm%d" % i, [128, 2, 512], BF16, l3) for i in range(2)]
                    rec = [sb("rec%d" % i, [128, 512], F32, l3) for i in range(2)]
                    add("pool", lambda e: e.dma_start(out=memb[:], in_=dram["mem"].rearrange("(t p) c -> p t c", p=128)), writes=["memb"], dma="memb")
                    for mt in range(2):
                        for k in range(8):
                            add("pe", lambda e, mt=mt, k=k: e.transpose(out=bankT[:, k * 128:(k + 1) * 128], in_=memb[:, mt, k * 128:(k + 1) * 128], identity=ident[:]),
                                reads=["memb", "ident"], writes=["bankT"])
                        copy_op("dve", memT[:, :, mt * 128:(mt + 1) * 128], bankT[:].rearrange("p (k c) -> p k c", k=8), [], ["memT:%d" % mt, "bankT"])
                    rotA = Rot([0, 1, 2, 3])
                    mi = 0
                    for mt in range(2):
                        for hf in range(2):
                            b = rotA.next()
                            for k in range(8):
                                add("pe", lambda e, b=b, k=k, mt=mt, hf=hf: e.matmul(
                                    banks[b][:], lhsT=memT[:, k, mt * 128:(mt + 1) * 128], rhs=wmem[:, k, hf * 512:(hf + 1) * 512],
                                    start=(k == 0), stop=(k == 7)), reads=["memT:%d" % mt, "wmem:%d" % hf], writes=["bank%d" % b])
                            m2 = mi % 2
                            mi += 1
                            add("act", lambda e, m2=m2, b=b: e.activation(out=mst[m2][:], in_=banks[b][:], func=AF.Identity),
                                writes=["mst%d" % m2, "bank%d" % b])
                            add("sp", lambda e, m2=m2, mt=mt, hf=hf: e.dma_start(
                                out=dram["new_mem_kv"][mt * 128:(mt + 1) * 128, hf * 512:(hf + 1) * 512], in_=mst[m2][:]),
                                reads=["mst%d" % m2], writes=["o_mst%d" % m2], dma="mst%d" % m2)
                            if hf == 1:
                                add("dve", lambda e, m2=m2, mt=mt: e.tensor_copy(out=mvv[:, mt, :], in_=mst[m2][:]),
                                    reads=["mst%d" % m2], writes=["mvv:%d" % mt])
                    for h in range(4):
                        b = rotA.next()
                        for k in range(8):
                            add("pe", lambda e, b=b, k=k, h=h: e.matmul(
                                banks[b][:, 0:256], lhsT=wmem[:, k, h * 128:(h + 1) * 128], rhs=memT[:, k, :],
                                start=(k == 0), stop=(k == 7)), reads=["memT:0", "memT:1", "wmem:0"], writes=["bank%d" % b])
                        copy_op(evac_eng(), mkT[:, h, :], banks[b][:, 0:256], [], ["mkT:%d" % h, "bank%d" % b])
                    for h in range(4):
                        for tt in range(4):
                            b = rotA.next()
                            for k in range(8):
                                add("pe", lambda e, b=b, k=k, h=h, tt=tt: e.matmul(
                                    banks[b][:], lhsT=wqm[:, k, h * 128:(h + 1) * 128], rhs=xT[:, k, tt * 512:(tt + 1) * 512],
                                    start=(k == 0), stop=(k == 7)), reads=["wqm"] + ["xT:%d" % t for t in range(tt * 4, tt * 4 + 4)], writes=["bank%d" % b])
                            copy_op(evac_eng(), qmT[:, h, tt * 512:(tt + 1) * 512], banks[b][:], [], ["qmT:%d:%d" % (h, tt), "bank%d" % b])
                    its = [(h, tt) for h in range(4) for tt in range(4)]
                    sbanks = [(0, 1), (2, 3)]
                    obanks = [4, 5]

                    def emit_S(i):
                        h, tt = its[i]
                        p2 = i % 2
                        for mt in range(2):
                            b = sbanks[p2][mt]
                            add("pe", lambda e, b=b, h=h, tt=tt, mt=mt: e.matmul(
                                banks[b][:], lhsT=mkT[:, h, mt * 128:(mt + 1) * 128], rhs=qmT[:, h, tt * 512:(tt + 1) * 512], start=True, stop=True),
                                reads=["mkT:%d" % h, "qmT:%d:%d" % (h, tt)], writes=["bank%d" % b])
                            add("act", lambda e, b=b, p2=p2, mt=mt: e.activation(out=PTm[p2][:, mt, :], in_=banks[b][:], func=AF.Exp, scale=float(128 ** -0.5)),
                                writes=["PTm%d:%d" % (p2, mt), "bank%d" % b])

                    def emit_O(i):
                        h, tt = its[i]
                        p2 = i % 2
                        ob = obanks[p2]
                        for mt in range(2):
                            add("pe", lambda e, h=h, mt=mt, p2=p2: e.matmul(
                                banks[6][:], lhsT=onesb[:], rhs=PTm[p2][:, mt, :], start=(mt == 0), stop=(mt == 1)),
                                reads=["onesb", "PTm%d:%d" % (p2, mt)], writes=["bank6"])
                        for mt in range(2):
                            add("pe", lambda e, ob=ob, h=h, mt=mt, p2=p2: e.matmul(
                                banks[ob][:], lhsT=mvv[:, mt, h * 128:(h + 1) * 128], rhs=PTm[p2][:, mt, :], start=(mt == 0), stop=(mt == 1)),
                                reads=["mvv:%d" % mt, "PTm%d:%d" % (p2, mt)], writes=["bank%d" % ob])
                        add("act", lambda e, p2=p2: e.activation(out=rec[p2][:], in_=banks[6][:], func=AF.Ln), writes=["rec%d" % p2, "bank6"])
                        add("act", lambda e, p2=p2: e.activation(out=rec[p2][:], in_=rec[p2][:], func=AF.Exp, scale=-1.0), reads=["rec%d" % p2], writes=["rec%d" % p2])
                        add("dve", lambda e, p2=p2, ob=ob, h=h, tt=tt: e.tensor_tensor(
                            out=o_mT[:, h, tt * 512:(tt + 1) * 512], in0=rec[p2][:], in1=banks[ob][:], op=ALU.mult),
                            reads=["rec%d" % p2], writes=["o_mT:%d:%d" % (h, tt), "bank%d" % ob])
                    dqm = sb("dqm", [128, 4, 8], BF16, l3)
                    cmk = sb("cmk", [128, 2, 4, D], BF16, l3)
                    cmkT = sb("cmkT", [128, 32, 128], BF16, l3)
                    dPm = sb("dPm", [128, 64], BF16, l3)
                    dden = sb("dden", [128, 4, 8], F32, l3)
                    cmkTk = ["cmkT:%d" % i8 for i8 in range(0, 32, 8)]
                    for mt_ in range(2):
                        add("pool", lambda e, mt_=mt_: e.dma_start(out=cmk[:, mt_], in_=dram["cmem"][:, mt_ * 128:(mt_ + 1) * 128, :].rearrange("n p x -> p n x")),
                            writes=["cmk:%d" % mt_], dma="cmk%d" % mt_)
                    add("pool", lambda e: e.memset(o_mT[:, :, T:TW], 0.0), writes=["o_mT:dz"])
                    def mdec_p1():
                        for h in range(4):
                            for k in range(8):
                                add("pe", lambda e, h=h, k=k: e.matmul(banks[4][:, h * 8:(h + 1) * 8], lhsT=wqm[:, k, h * 128:(h + 1) * 128], rhs=xT[:, k, T:TW],
                                                                      start=(k == 0), stop=(k == 7)), reads=["wqm", "xTd"], writes=["bank4"])
                        copy_op("dve", dqm[:], banks[4][:, 0:32].rearrange("p (h n) -> p h n", h=4), [], ["dqm", "bank4"])
                    def mdec_p2(i8):
                        for ii in range(8):
                            idx = i8 + ii
                            n_, h_, mt_ = idx // 8, (idx // 2) % 4, idx % 2
                            add("pe", lambda e, ii=ii, n_=n_, h_=h_, mt_=mt_: e.transpose(
                                out=bankT[:, ii * 128:(ii + 1) * 128], in_=cmk[:, mt_, n_, h_ * 128:(h_ + 1) * 128], identity=ident[:]),
                                reads=["cmk:0", "cmk:1", "ident"], writes=["bankT"])
                        copy_op(evac_eng(), cmkT[:, i8:i8 + 8, :], bankT[:].rearrange("p (k c) -> p k c", k=8), [], ["cmkT:%d" % i8, "bankT"])
                    def mdec_p3():
                        for n_ in range(6):
                            for h_ in range(4):
                                for mt_ in range(2):
                                    col = (n_ * 4 + h_) * 2 + mt_
                                    lhs = cmkT[:, col, :] if n_ < 4 else mkT[:, h_, mt_ * 128:(mt_ + 1) * 128]
                                    add("pe", lambda e, col=col, lhs=lhs, h_=h_, n_=n_: e.matmul(
                                        banks[5][:, col:col + 1], lhsT=lhs, rhs=dqm[:, h_, n_:n_ + 1], start=True, stop=True),
                                        reads=cmkTk + ["dqm"] + ["mkT:%d" % h for h in range(4)], writes=["bank5"])
                        add("act", lambda e: e.activation(out=dPm[:, 0:48], in_=banks[5][:, 0:48], func=AF.Exp, scale=float(128 ** -0.5)), writes=["dPm", "bank5"])
                    def mdec_p4():
                        for n_ in range(6):
                            for h_ in range(4):
                                for (base, which) in ((0, "o"), (64, "d")):
                                    for mt_ in range(2):
                                        scol = (n_ * 4 + h_) * 2 + mt_
                                        if which == "d":
                                            lhs = onesb[:]
                                        elif n_ < 4:
                                            lhs = cmk[:, mt_, n_, 512 + h_ * 128:512 + (h_ + 1) * 128]
                                        else:
                                            lhs = mvv[:, mt_, h_ * 128:(h_ + 1) * 128]
                                        add("pe", lambda e, base=base, h_=h_, n_=n_, mt_=mt_, scol=scol, lhs=lhs: e.matmul(
                                            banks[6][:, base + h_ * 8 + n_:base + h_ * 8 + n_ + 1], lhsT=lhs, rhs=dPm[:, scol:scol + 1],
                                            start=(mt_ == 0), stop=(mt_ == 1)), reads=["dPm", "cmk:0", "cmk:1", "onesb", "mvv:0", "mvv:1"], writes=["bank6"])
                        add("dve", lambda e: e.reciprocal(out=dden[:, :, 0:6], in_=banks[6][:, 64:96].rearrange("p (h n) -> p h n", h=4)[:, :, 0:6]), writes=["dden", "bank6"])
                        add("dve", lambda e: e.tensor_tensor(out=o_mT[:, :, T:T + 6], in0=dden[:, :, 0:6], in1=banks[6][:, 0:32].rearrange("p (h n) -> p h n", h=4)[:, :, 0:6], op=ALU.mult),
                            reads=["dden", "o_mT:dz"], writes=["o_mT:d", "bank6"])
                        if dbg_d is not None:
                            add("dve", lambda e: e.tensor_copy(out=dden[:], in_=o_mT[:, :, T:TW]), reads=["o_mT:d"], writes=["dden"])
                            add("sp", lambda e: e.dma_start(out=dbg_d[:, 0:32], in_=dden[:].rearrange("p a n -> p (a n)")), reads=["dden"], writes=["dbg_d"], dma="dbgd")

                    mdec_sched = {1: mdec_p1, 3: (lambda: mdec_p2(0)), 5: (lambda: mdec_p2(8)), 7: (lambda: mdec_p2(16)), 9: (lambda: mdec_p2(24)), 11: mdec_p3, 13: mdec_p4}
                    emit_S(0)
                    for i in range(len(its)):
                        if i + 1 < len(its):
                            emit_S(i + 1)
                        emit_O(i)
                        if i in mdec_sched:
                            mdec_sched[i]()
                    S_.flush(barrier=True)
                S_.stop_at("mem")
                mixedT = sb("mixedT", [128, 8, TW], BF16, l2b)
                w_out_v = dram["w_out"].rearrange("(k p) c -> p k c", p=128)

                def prefetch_wout():
                    for hf in range(2):
                        add("pool", lambda e, hf=hf: e.dma_start(out=wout[:, :, hf * 512:(hf + 1) * 512], in_=w_out_v[:, :, hf * 512:(hf + 1) * 512]),
                            writes=["wout:%d" % hf], dma="wout%d" % hf)
                    add("sp", lambda e: e.dma_start(out=ln1g[:], in_=dram["ln1_g"].partition_broadcast(128)), writes=["ln1g"], dma="c_ln1g")
                    add("sp", lambda e: e.dma_start(out=ln1b[:], in_=dram["ln1_b"].partition_broadcast(128)), writes=["ln1b"], dma="c_ln1b")
                with ExitStack() as l3:
                    wg = [sb("wg%d" % i, [128, 8, 3, 128], BF16, l3) for i in range(2)]
                    wba = sb("wba", [128, 2, D], BF16, l3)
                    wbb = sb("wbb", [128, 8, D], BF16, l3)
                    wbm = sb("wbm", [128, 4, D], BF16, l3)
                    bgn = sb("bgn", [24, 128], F32, l3)
                    bg = sb("bg", [128, 24], F32, l3)
                    gt = [[sb("gt%d_%d" % (i, j), [128, 512], F32, l3) for j in range(3)] for i in range(2)]
                    tm = [[sb("tm%d_%d" % (i, j), [128, 512], F32, l3) for j in range(3)] for i in range(2)]
                    add("pool", lambda e: e.dma_start(out=wba[:], in_=dram["w_branch_a"].rearrange("(k p) c -> p k c", p=128)), writes=["wba"], dma="wba")
                    add("pool", lambda e: e.dma_start(out=wbm[:], in_=dram["w_branch_m"].rearrange("(k p) c -> p k c", p=128)), writes=["wbm"], dma="wbm")
                    add("pool", lambda e: e.memset(wbb[:], 0.0), writes=["wbb_z"])
                    for cu in range(8):
                        rows = 128 if cu % 2 == 0 else 64
                        add("pool", lambda e, cu=cu, rows=rows: e.dma_start(out=wbb[0:rows, cu, :], in_=dram["w_branch_b"][UST[cu]:UST[cu] + rows, :]),
                            reads=["wbb_z"], writes=["wbb:%d" % cu], dma="wbb%d" % cu)
                    add("sp", lambda e: e.dma_start(out=bgn[:], in_=dram["b_gate"].rearrange("b (f p) -> (b f) p", p=128)), writes=["bgn"], dma="c_bgn")
                    add("pe", lambda e: e.transpose(out=banks[6][:, 0:24], in_=bgn[:], identity=identf[0:24, 0:24]), reads=["bgn", "identf"], writes=["bank6"])
                    copy_op("dve", bg[:], banks[6][:, 0:24], [], ["bg", "bank6"])
                    wbkeys = ["wba", "wbm", "wbb_z"] + ["wbb:%d" % cu for cu in range(8)]
                    obk = ["o_bT:%d:%d" % (cu, tt) for cu in range(8) for tt in range(4)]
                    omk = ["o_mT:%d:%d" % (h, tt) for h in range(4) for tt in range(4)]
                    oak = ["o_aT:%d:%d" % (c, tt) for c in range(2) for tt in range(4)]

                    def load_wg(f):
                        s2 = f % 2
                        for bi in range(3):
                            c0 = 4352 + bi * 1024 + f * 128
                            add("pool", lambda e, s2=s2, bi=bi, c0=c0: e.dma_start(out=wg[s2][:, :, bi, :], in_=w_in_v[:, :, c0:c0 + 128]),
                                writes=["wg%d:%d" % (s2, bi)], dma="wg%d_%d" % (s2, bi))
                    load_wg(0)
                    it = 0
                    for f in range(8):
                        if f + 1 < 8:
                            load_wg(f + 1)
                        s2 = f % 2
                        for tt in range(5):
                            i2 = it % 2
                            it += 1
                            tsl = slice(tt * 512, (tt + 1) * 512) if tt < 4 else slice(T, TW)
                            nn = 512 if tt < 4 else 8
                            xk = ["xT:%d" % t for t in range(tt * 4, tt * 4 + 4)] if tt < 4 else ["xTd"]
                            for bi in range(3):
                                for k in range(8):
                                    add("pe", lambda e, bi=bi, k=k, s2=s2, tsl=tsl, nn=nn: e.matmul(
                                        banks[bi][:, 0:nn], lhsT=wg[s2][:, k, bi, :], rhs=xT[:, k, tsl], start=(k == 0), stop=(k == 7)),
                                        reads=["wg%d:%d" % (s2, bi)] + xk, writes=["bank%d" % bi])
                                add("act", lambda e, bi=bi, i2=i2, f=f, nn=nn: e.activation(
                                    out=gt[i2][bi][:, 0:nn], in_=banks[bi][:, 0:nn], func=AF.Sigmoid, bias=bg[:, bi * 8 + f:bi * 8 + f + 1]),
                                    reads=["bg"], writes=["gt%d_%d" % (i2, bi), "bank%d" % bi])
                            fs = slice(f * 128, (f + 1) * 128)
                            for (bi, wt, nk, src, keys) in ((0, wba, 2, o_aT, oak), (1, wbb, 8, o_bT, obk), (2, wbm, 4, o_mT, omk)):
                                for k in range(nk):
                                    add("pe", lambda e, bi=bi, k=k, wt=wt, src=src, nk=nk, fs=fs, tsl=tsl, nn=nn: e.matmul(
                                        banks[3 + bi][:, 0:nn], lhsT=wt[:, k, fs], rhs=src[:, k, tsl], start=(k == 0), stop=(k == nk - 1)),
                                        reads=wbkeys + keys, writes=["bank%d" % (3 + bi)])
                                add("dve", lambda e, bi=bi, i2=i2, nn=nn: e.tensor_tensor(out=tm[i2][bi][:, 0:nn], in0=gt[i2][bi][:, 0:nn], in1=banks[3 + bi][:, 0:nn], op=ALU.mult),
                                    reads=["gt%d_%d" % (i2, bi)], writes=["tm%d_%d" % (i2, bi), "bank%d" % (3 + bi)])
                            add("pool", lambda e, i2=i2, nn=nn: e.tensor_tensor(out=tm[i2][0][:, 0:nn], in0=tm[i2][0][:, 0:nn], in1=tm[i2][1][:, 0:nn], op=ALU.add),
                                reads=["tm%d_1" % i2], writes=["tm%d_0" % i2])
                            add("pool", lambda e, i2=i2, f=f, tsl=tsl, nn=nn: e.tensor_tensor(out=mixedT[:, f, tsl], in0=tm[i2][0][:, 0:nn], in1=tm[i2][2][:, 0:nn], op=ALU.add),
                                reads=["tm%d_0" % i2, "tm%d_2" % i2], writes=["mixedT:%d:%d" % (f, tt)])
                    S_.flush(barrier=True)
                S_.stop_at("merge")
                with ExitStack() as l3:
                    wout = sb("wout", [128, 8, D], BF16, l3)
                    ln1g = sb("ln1g", [128, D], F32, l3)
                    ln1b = sb("ln1b", [128, D], F32, l3)
                    prefetch_wout()
                    xres = [sb("xres%d" % i, [128, D], F32, l3) for i in range(3)]
                    zt = [sb("zt%d" % i, [128, D], F32, l3) for i in range(3)]
                    zn = [sb("zn%d" % i, [128, D], F32, l3) for i in range(3)]
                    x1f = [sb("x1f%d" % i, [128, D], F32, l3) for i in range(3)]
                    x1b = [sb("x1b%d" % i, [128, D], BF16, l3) for i in range(3)]
                    stats = [sb("stats1_%d" % i, [128, 2, 6], F32, l3) for i in range(3)]
                    mv_ = [sb("mv1_%d" % i, [128, 4], F32, l3) for i in range(3)]
                    rotZ = Rot([0, 1, 2, 3, 4, 5])
                    mixk = ["mixedT:%d:%d" % (f, tt) for f in range(8) for tt in range(5)]
                    mxdp = sb("mxdp", [128, 8, 128], BF16, l3)
                    add("pool", lambda e: e.memset(mxdp[:], 0.0), writes=["mxdp"])
                    add("pool", lambda e: e.tensor_copy(out=mxdp[:, :, 0:8], in_=mixedT[:, :, T:TW]), reads=["mxdp"] + ["mixedT:%d:4" % f for f in range(8)], writes=["mxdp"])
                    bzs = {}

                    def ln1_mm(t):
                        bz = [rotZ.next(), rotZ.next()]
                        bzs[t] = bz
                        i2 = t % 3
                        if t < NT:
                            add("sp", lambda e, i2=i2, t=t: e.dma_start(out=xres[i2][:], in_=xh[T + t * 128:T + (t + 1) * 128, :]), writes=["xres%d" % i2], dma="xres%d" % i2)
                        else:
                            add("pool", lambda e, i2=i2: e.memset(xres[i2][:], 0.0), writes=["xres%d" % i2])
                            add("sp", lambda e, i2=i2: e.dma_start(out=xres[i2][0:8, :], in_=dram["xd"]), reads=["xres%d" % i2], writes=["xresd"], dma="xresd")
                        for hf in range(2):
                            for k in range(8):
                                lhs = mixedT[:, k, t * 128:(t + 1) * 128] if t < NT else mxdp[:, k, :]
                                add("pe", lambda e, hf=hf, k=k, lhs=lhs, bz=bz: e.matmul(
                                    banks[bz[hf]][:], lhsT=lhs, rhs=wout[:, k, hf * 512:(hf + 1) * 512],
                                    start=(k == 0), stop=(k == 7)), reads=["mixedT:%d:%d" % (k, t // 4), "wout:%d" % hf, "mxdp"], writes=["bank%d" % bz[hf]])

                    def ln1_A1(t):
                        i2 = t % 3
                        bz = bzs[t]
                        for hf in range(2):
                            add("dve", lambda e, hf=hf, i2=i2, bz=bz: e.scalar_tensor_tensor(
                                out=zt[i2][:, hf * 512:(hf + 1) * 512], in0=xres[i2][:, hf * 512:(hf + 1) * 512], scalar=ALPHA, in1=banks[bz[hf]][:],
                                op0=ALU.mult, op1=ALU.add), reads=["xres%d" % i2, "xresd"], writes=["zt%d:%d" % (i2, hf), "bank%d" % bz[hf]])
                            add("dve", lambda e, hf=hf, i2=i2: e.bn_stats(out=stats[i2][:, hf, :], in_=zt[i2][:, hf * 512:(hf + 1) * 512]),
                                reads=["zt%d:%d" % (i2, hf)], writes=["st1_%d:%d" % (i2, hf)])
                        add("dve", lambda e, i2=i2: e.bn_aggr(out=mv_[i2][:, 0:2], in_=stats[i2][:].rearrange("p a b -> p (a b)")),
                            reads=["st1_%d:0" % i2, "st1_%d:1" % i2], writes=["mv1_%d" % i2])
                        add("dve", lambda e, i2=i2: e.tensor_scalar(out=mv_[i2][:, 2:3], in0=mv_[i2][:, 1:2], scalar1=EPS, scalar2=None, op0=ALU.add),
                            reads=["mv1_%d" % i2], writes=["mv1_%d" % i2])

                    def ln1_A2a(t):
                        i2 = t % 3
                        add("act", lambda e, i2=i2: e.activation(out=mv_[i2][:, 2:3], in_=mv_[i2][:, 2:3], func=AF.Sqrt),
                            reads=["mv1_%d" % i2], writes=["mv1_%d" % i2])

                    def ln1_A2b(t):
                        i2 = t % 3
                        add("dve", lambda e, i2=i2: e.reciprocal(out=mv_[i2][:, 2:3], in_=mv_[i2][:, 2:3]),
                            reads=["mv1_%d" % i2], writes=["mv1_%d" % i2])
                        add("dve", lambda e, i2=i2: e.scalar_tensor_tensor(out=mv_[i2][:, 3:4], in0=mv_[i2][:, 0:1], scalar=-1.0, in1=mv_[i2][:, 2:3], op0=ALU.mult, op1=ALU.mult),
                            reads=["mv1_%d" % i2], writes=["mv1_%d" % i2])

                    def ln1_B1(t):
                        i2 = t % 3
                        add("act", lambda e, i2=i2: e.activation(out=zn[i2][:], in_=zt[i2][:], func=AF.Identity, scale=mv_[i2][:, 2:3], bias=mv_[i2][:, 3:4]),
                            reads=["zt%d:0" % i2, "zt%d:1" % i2, "mv1_%d" % i2], writes=["zn%d" % i2])

                    def ln1_B2a(t):
                        i2 = t % 3
                        add("dve", lambda e, i2=i2: e.tensor_tensor(out=zn[i2][:], in0=zn[i2][:], in1=ln1g[:], op=ALU.mult),
                            reads=["ln1g"], writes=["zn%d" % i2])

                    def ln1_B2b(t):
                        i2 = t % 3
                        for hf in range(2):
                            add("pool", lambda e, i2=i2, hf=hf: e.tensor_tensor(out=x1f[i2][:, hf * 512:(hf + 1) * 512], in0=zn[i2][:, hf * 512:(hf + 1) * 512],
                                                                                in1=ln1b[:, hf * 512:(hf + 1) * 512], op=ALU.add),
                                reads=["zn%d" % i2, "ln1b"], writes=["x1f%d" % i2])
                        add("sp", lambda e, i2=i2, t=t: e.dma_start(out=x1s[t * 128:(t + 1) * 128, :], in_=x1f[i2][:]),
                            reads=["x1f%d" % i2], writes=["x1s:%d" % t], dma="x1f%d" % i2)
                        add("act", lambda e, i2=i2: e.activation(out=x1b[i2][:], in_=x1f[i2][:], func=AF.Identity), reads=["x1f%d" % i2], writes=["x1b%d" % i2])

                    def ln1_stage(tA, tB):
                        if tB is not None:
                            ln1_B1(tB)
                        if tA is not None:
                            ln1_A1(tA)
                            ln1_A2a(tA)
                        if tB is not None:
                            ln1_B2a(tB)
                        if tA is not None:
                            ln1_A2b(tA)
                        if tB is not None:
                            ln1_B2b(tB)

                    def ln1_tr(t):
                        i2 = t % 3
                        bt, btk = (bankT[:], "bankT") if t % 2 == 0 else (bankT2, "bank6")
                        for k in range(8):
                            add("pe", lambda e, i2=i2, k=k, bt=bt: e.transpose(out=bt[:, k * 128:(k + 1) * 128], in_=x1b[i2][:, k * 128:(k + 1) * 128], identity=ident[:]),
                                reads=["x1b%d" % i2, "ident"], writes=[btk])
                        if t < NT:
                            copy_op("dve", xT[:, :, t * 128:(t + 1) * 128], bt.rearrange("p (k c) -> p k c", k=8), mixk, ["x1T:%d" % t, btk])
                        else:
                            copy_op("dve", xT[:, :, T:TW], bt.rearrange("p (k c) -> p k c", k=8)[:, :, 0:8], mixk + ["mxdp"], ["x1T:d", btk])

                    ln1_mm(0)
                    ln1_mm(1)
                    ln1_mm(2)
                    ln1_stage(0, None)
                    ln1_stage(1, 0)
                    for t in range(NT + 1):
                        if t + 3 <= NT:
                            ln1_mm(t + 3)
                        ln1_stage(t + 2 if t + 2 <= NT else None, t + 1 if t + 1 <= NT else None)
                        ln1_tr(t)
                    if dbg_x1 is not None:
                        for t in range(NT):
                            add("sp", lambda e, t=t: e.dma_start(out=dbg_x1[t * 128:(t + 1) * 128, :], in_=x1s[t * 128:(t + 1) * 128, :]),
                                reads=["x1s:%d" % t], writes=["dbgx1:%d" % t], dma="dbgx1")
                    S_.flush(barrier=True)
                S_.stop_at("ln1")
        S_.stop_at("pre_ffn")


        with ExitStack() as l4:
            hT = sb("hT", [128, NJ, TW], BF16, l4)
            cwn = sb("cwn", [88, 128], F32, l4)
            cw = sb("cw", [128, 88], F32, l4)
            alast = sb("alast", [128, 2, NJ], F32, l4)
            dav = sb("dav", [128, NJ, 2, 8], F32, l4)
            bflag = sb("bflag_sb", [128, 1], F32, l4)
            cstA = sb("cstA", [128, 128], F32, l4)
            cstB = sb("cstB", [128, 128], F32, l4)
            dst_ = sb("dst_", [128, NJ, 8], F32, l4)
            tcv = sb("tcv", [128, NJ, 4], F32, l4)
            tgd = sb("tgd", [128, NJ, 4], F32, l4)
            dtmp = sb("dtmp", [128, 4, NJ], F32, l4)
            dat88 = sb("dat88", [128, 128], F32, l4)
            add("sp", lambda e: e.dma_start(out=bflag[:], in_=dram["bflag"]), writes=["bflag"], dma="c_bflag")

            def halo_fn(j, AB, a2):
                add("pool", lambda e, AB=AB, j=j: e.tensor_scalar(out=AB[:, 0:2], in0=dav[:, j, 0, 4:6], scalar1=bflag[:, 0:1], scalar2=None, op0=ALU.mult),
                    reads=["dav:%d" % j, "bflag"], writes=["abuf%d:h" % a2])
            add("sp", lambda e: e.dma_start(out=cwn[0:66, :], in_=dram["conv_w"].rearrange("k (j p) -> (k j) p", p=128)), writes=["cwn:0"], dma="c_cw0")
            add("sp", lambda e: e.dma_start(out=cwn[66:88, :], in_=dram["conv_b"].rearrange("(j p) -> j p", p=128)), writes=["cwn:1"], dma="c_cw1")
            add("pe", lambda e: e.transpose(out=banks[6][:, 0:88], in_=cwn[:], identity=identf[0:88, 0:88]), reads=["cwn:0", "cwn:1", "identf"], writes=["bank6"])
            copy_op("dve", cw[:], banks[6][:, 0:88], [], ["cw", "bank6"])
            w_up_v = dram["w_up"].rearrange("(k p) c -> p k c", p=128)
            w_dn_v = dram["w_down"].rearrange("(j p) c -> p j c", p=128)
            wdnA = sb("wdnA", [128, 12, D], BF16, l4)
            with ExitStack() as l5:
                wup = [sb("wup%d" % i, [128, 8, 2, 256], BF16, l5) for i in range(2)]
                abuf = [sb("abuf%d" % i, [128, 2 + T], F32, l5) for i in range(2)]
                t0 = [sb("t0_%d" % i, [128, 512], F32, l5) for i in range(2)]
                t1 = [sb("t1_%d" % i, [128, 512], F32, l5) for i in range(2)]
                gl = [sb("gl%d" % i, [128, 512], F32, l5) for i in range(2)]

                def load_wup(jp):
                    s2 = jp % 2
                    for hv in range(2):
                        c0 = hv * DFF + jp * 256
                        add("pool", lambda e, s2=s2, hv=hv, c0=c0: e.dma_start(out=wup[s2][:, :, hv, :], in_=w_up_v[:, :, c0:c0 + 256]),
                            writes=["wup%d:%d" % (s2, hv)], dma="wup%d_%d" % (s2, hv))
                load_wup(0)
                for (j0, j1) in ((0, 6), (6, 12)):
                    add("pool", lambda e, j0=j0, j1=j1: e.dma_start(out=wdnA[:, j0:j1, :], in_=w_dn_v[:, j0:j1, :]), writes=["wdnA:%d" % j0], dma="wdnA%d" % j0)
                rotA = Rot([0, 1, 2])
                rotB = Rot([3, 4, 5])
                it = 0
                for jp in range(NJ // 2):
                    if jp + 1 < NJ // 2:
                        load_wup(jp + 1)
                    s2 = jp % 2
                    for jj in range(2):
                        j = 2 * jp + jj
                        a2 = j % 2
                        AB = abuf[a2]
                        for hv in range(2):
                            for k in range(8):
                                add("pe", lambda e, hv=hv, k=k, s2=s2, jj=jj: e.matmul(
                                    banks[6][:, hv * 8:(hv + 1) * 8], lhsT=wup[s2][:, k, hv, jj * 128:(jj + 1) * 128], rhs=xT[:, k, T:TW],
                                    start=(k == 0), stop=(k == 7)), reads=["wup%d:%d" % (s2, hv), "x1T:d"], writes=["bank6"])
                        add("dve", lambda e, j=j: e.tensor_copy(out=dav[:, j], in_=banks[6][:, 0:16].rearrange("p (a n) -> p a n", a=2)),
                            writes=["dav:%d" % j, "bank6"])
                        halo_fn(j, AB, a2)
                        for tt in range(4):
                            i2 = it % 2
                            it += 1
                            ba, bv = rotA.next(), rotB.next()
                            xk = ["x1T:%d" % t for t in range(tt * 4, tt * 4 + 4)]
                            for (b, hv) in ((ba, 0), (bv, 1)):
                                for k in range(8):
                                    add("pe", lambda e, b=b, hv=hv, k=k, s2=s2, jj=jj, tt=tt: e.matmul(
                                        banks[b][:], lhsT=wup[s2][:, k, hv, jj * 128:(jj + 1) * 128], rhs=xT[:, k, tt * 512:(tt + 1) * 512],
                                        start=(k == 0), stop=(k == 7)), reads=["wup%d:%d" % (s2, hv)] + xk, writes=["bank%d" % b])
                            c0 = 2 + tt * 512
                            add("act", lambda e, AB=AB, ba=ba, c0=c0: e.activation(out=AB[:, c0:c0 + 512], in_=banks[ba][:], func=AF.Identity),
                                writes=["abuf%d:%d" % (a2, tt), "bank%d" % ba])
                            add("act", lambda e, i2=i2, ba=ba, j=j: e.activation(out=t0[i2][:], in_=banks[ba][:], func=AF.Identity,
                                                                               scale=cw[:, 44 + j:45 + j], bias=cw[:, 66 + j:67 + j]),
                                reads=["cw"], writes=["t0_%d" % i2, "bank%d" % ba])
                            prevk = ["abuf%d:%d" % (a2, tt - 1)] if tt > 0 else ["abuf%d:h" % a2]
                            add("dve", lambda e, i2=i2, AB=AB, c0=c0, j=j: e.scalar_tensor_tensor(
                                out=t1[i2][:], in0=AB[:, c0 - 1:c0 + 511], scalar=cw[:, 22 + j:23 + j], in1=t0[i2][:], op0=ALU.mult, op1=ALU.add),
                                reads=["cw", "t0_%d" % i2, "abuf%d:%d" % (a2, tt)] + prevk, writes=["t1_%d" % i2])
                            add("dve", lambda e, i2=i2, AB=AB, c0=c0, j=j: e.scalar_tensor_tensor(
                                out=t1[i2][:], in0=AB[:, c0 - 2:c0 + 510], scalar=cw[:, j:j + 1], in1=t1[i2][:], op0=ALU.mult, op1=ALU.add),
                                reads=["cw", "abuf%d:%d" % (a2, tt)] + prevk, writes=["t1_%d" % i2])
                            add("act", lambda e, i2=i2: e.activation(out=gl[i2][:], in_=t1[i2][:], func=AF.Gelu_apprx_tanh),
                                reads=["t1_%d" % i2], writes=["gl%d" % i2])
                            add("dve", lambda e, i2=i2, bv=bv, j=j, tt=tt: e.tensor_tensor(
                                out=hT[:, j, tt * 512:(tt + 1) * 512], in0=gl[i2][:], in1=banks[bv][:], op=ALU.mult),
                                reads=["gl%d" % i2], writes=["hT:%d:%d" % (j, tt), "bank%d" % bv])
                        add("pool", lambda e, AB=AB, j=j: e.tensor_copy(out=alast[:, :, j], in_=AB[:, T:T + 2]),
                            reads=["abuf%d:3" % a2], writes=["alast:%d" % j])
                add("pe", lambda e: e.transpose(out=banks[6][0:44, 0:128], in_=alast[:].rearrange("p t j -> p (t j)"), identity=identf[:]),
                    reads=["alast:%d" % j for j in range(NJ)] + ["identf"], writes=["bank6"])
                add("act", lambda e: e.activation(out=t0[0][0:44, 0:128], in_=banks[6][0:44, 0:128], func=AF.Identity), writes=["t0_0", "bank6"])
                add("sp", lambda e: e.dma_start(out=dram["conv_p"].rearrange("t (j p) -> (t j) p", p=128), in_=t0[0][0:44, 0:128]),
                    reads=["t0_0"], writes=["o_convp"], dma="convp")
                S_.flush(barrier=True)
            S_.stop_at("ffn_up")
            with ExitStack() as l5:
                wdnB = sb("wdnB", [128, NJ - 12, D], BF16, l5)
                ln2g = sb("ln2g", [128, D], F32, l5)
                ln2b = sb("ln2b", [128, D], F32, l5)
                xres = [sb("x1r%d" % i, [128, D], F32, l5) for i in range(1)]
                zt = sb("ztB", [128, D], F32, l5)
                zn = zt
                yst = [sb("yst%d" % i, [128, D], F32, l5) for i in range(2)]
                stats = [sb("stats2_%d" % i, [128, 2, 6], F32, l5) for i in range(2)]
                mv_ = [sb("mv2_%d" % i, [128, 4], F32, l5) for i in range(2)]
                for (j0, j1) in ((12, 17), (17, NJ)):
                    add("pool", lambda e, j0=j0, j1=j1: e.dma_start(out=wdnB[:, j0 - 12:j1 - 12, :], in_=w_dn_v[:, j0:j1, :]), writes=["wdn:%d" % j0], dma="wdn%d" % j0)
                wdk = ["wdn:12", "wdn:17"]
                cstv = dram["cstate"].rearrange("n s (j p) -> (n s j) p", p=128)
                add("sp", lambda e: e.dma_start(out=cstA[:], in_=cstv[0:128, :]), writes=["cstA"], dma="c_cstA")
                add("sp", lambda e: e.dma_start(out=cstB[0:48, :], in_=cstv[128:176, :]), writes=["cstB"], dma="c_cstB")
                add("sp", lambda e: e.dma_start(out=dram["conv_s"][:, 0, :], in_=dram["cstate"][:, 1, :]), writes=["o_convs0"], dma="o_convs0")
                add("pe", lambda e: e.transpose(out=banks[5][:, 0:128], in_=cstA[:], identity=identf[:]), reads=["cstA", "identf"], writes=["bank5"])
                add("pe", lambda e: e.transpose(out=banks[5][:, 128:176], in_=cstB[0:48, :], identity=identf[0:48, 0:48]), reads=["cstB", "identf"], writes=["bank5"])
                add("dve", lambda e: e.tensor_copy(out=dst_[:], in_=banks[5][:, 0:176].rearrange("p (a j) -> p j a", a=8)), writes=["dst_", "bank5"])
                davk = ["dav:%d" % j for j in range(NJ)]
                for j in range(NJ):
                    add("dve", lambda e, j=j: e.tensor_scalar(out=tcv[:, j, :], in0=dst_[:, j, 0:8:2], scalar1=cw[:, j:j + 1], scalar2=cw[:, 66 + j:67 + j],
                                                             op0=ALU.mult, op1=ALU.add), reads=["dst_", "cw"], writes=["tcv:%d" % j])
                    add("dve", lambda e, j=j: e.scalar_tensor_tensor(out=tcv[:, j, :], in0=dst_[:, j, 1:8:2], scalar=cw[:, 22 + j:23 + j], in1=tcv[:, j, :],
                                                                    op0=ALU.mult, op1=ALU.add), reads=["dst_", "cw", "tcv:%d" % j], writes=["tcv:%d" % j])
                    add("dve", lambda e, j=j: e.scalar_tensor_tensor(out=tcv[:, j, :], in0=dav[:, j, 0, 0:4], scalar=cw[:, 44 + j:45 + j], in1=tcv[:, j, :],
                                                                    op0=ALU.mult, op1=ALU.add), reads=["cw", "tcv:%d" % j], writes=["tcv:%d" % j])
                add("act", lambda e: e.activation(out=tgd[:].rearrange("p j n -> p (j n)"), in_=tcv[:].rearrange("p j n -> p (j n)"), func=AF.Gelu_apprx_tanh),
                    reads=["tcv:%d" % j for j in range(NJ)], writes=["tgd"])
                add("pool", lambda e: e.memset(hT[:, :, T:TW], 0.0), writes=["hT:dz"])
                add("dve", lambda e: e.tensor_tensor(out=hT[:, :, T:T + 4], in0=tgd[:], in1=dav[:, :, 1, 0:4], op=ALU.mult),
                    reads=["tgd", "hT:dz"], writes=["hT:d"])
                add("dve", lambda e: e.tensor_copy(out=dtmp[:], in_=dav[:, :, 0, 0:4].rearrange("p j n -> p n j")), writes=["dtmp"])
                add("pe", lambda e: e.transpose(out=banks[4][0:88, 0:128], in_=dtmp[:].rearrange("p n j -> p (n j)"), identity=identf[:]),
                    reads=["dtmp", "identf"], writes=["bank4"])
                add("act", lambda e: e.activation(out=dat88[0:88, :], in_=banks[4][0:88, 0:128], func=AF.Identity), writes=["dat88", "bank4"])
                for n_ in range(4):
                    add("sp", lambda e, n_=n_: e.dma_start(out=dram["conv_s"][n_, 1, :].rearrange("(j p) -> j p", p=128), in_=dat88[n_ * NJ:(n_ + 1) * NJ, :]),
                        reads=["dat88"], writes=["o_convs1:%d" % n_], dma="o_convs1")
                add("sp", lambda e: e.dma_start(out=ln2g[:], in_=dram["ln2_g"].partition_broadcast(128)), writes=["ln2g"], dma="c_ln2g")
                add("sp", lambda e: e.dma_start(out=ln2b[:], in_=dram["ln2_b"].partition_broadcast(128)), writes=["ln2b"], dma="c_ln2b")
                rotZ = Rot([0, 1, 2, 3, 4, 5])
                hdp = sb("hdp", [128, NJ, 128], BF16, l5)
                add("pool", lambda e: e.memset(hdp[:], 0.0), writes=["hdp"])
                add("pool", lambda e: e.tensor_copy(out=hdp[:, :, 0:8], in_=hT[:, :, T:TW]), reads=["hdp", "hT:d", "hT:dz"], writes=["hdp"])
                def ln2_load(t):
                    add("sp", lambda e, t=t: e.dma_start(out=xres[0][:], in_=x1s[t * 128:(t + 1) * 128, :]), writes=["x1r0"], dma="x1r0")
                ln2_load(0)
                for t in range(NT + 1):
                    i2 = t % 2
                    bz = [rotZ.next(), rotZ.next()]
                    for hf in range(2):
                        for j in range(NJ):
                            lhs = hT[:, j, t * 128:(t + 1) * 128] if t < NT else hdp[:, j, :]
                            add("pe", lambda e, hf=hf, j=j, lhs=lhs, bz=bz: e.matmul(
                                banks[bz[hf]][:], lhsT=lhs, rhs=(wdnA[:, j, hf * 512:(hf + 1) * 512] if j < 12 else wdnB[:, j - 12, hf * 512:(hf + 1) * 512]),
                                start=(j == 0), stop=(j == NJ - 1)), reads=wdk + ["hT:%d:%d" % (j, t // 4), "hdp"], writes=["bank%d" % bz[hf]])
                        add("dve", lambda e, hf=hf, i2=i2, bz=bz: e.scalar_tensor_tensor(
                            out=zt[:, hf * 512:(hf + 1) * 512], in0=xres[0][:, hf * 512:(hf + 1) * 512], scalar=ALPHA, in1=banks[bz[hf]][:],
                            op0=ALU.mult, op1=ALU.add), reads=["x1r0"], writes=["zt2:%d" % hf, "bank%d" % bz[hf]])
                        add("dve", lambda e, hf=hf, i2=i2: e.bn_stats(out=stats[i2][:, hf, :], in_=zt[:, hf * 512:(hf + 1) * 512]),
                            reads=["zt2:%d" % hf], writes=["st2_%d:%d" % (i2, hf)])
                    if t + 1 <= NT:
                        ln2_load(t + 1)
                    add("dve", lambda e, i2=i2: e.bn_aggr(out=mv_[i2][:, 0:2], in_=stats[i2][:].rearrange("p a b -> p (a b)")),
                        reads=["st2_%d:0" % i2, "st2_%d:1" % i2], writes=["mv2_%d" % i2])
                    add("dve", lambda e, i2=i2: e.tensor_scalar(out=mv_[i2][:, 2:3], in0=mv_[i2][:, 1:2], scalar1=EPS, scalar2=None, op0=ALU.add),
                        reads=["mv2_%d" % i2], writes=["mv2_%d" % i2])
                    add("pool", lambda e, i2=i2: e.tensor_tensor(out=mv_[i2][:, 2:3], in0=mv_[i2][:, 2:3], in1=mhalf[:, 0:1], op=ALU.pow),
                        reads=["mv2_%d" % i2, "mhalf"], writes=["mv2_%d" % i2])
                    add("dve", lambda e, i2=i2: e.scalar_tensor_tensor(out=mv_[i2][:, 3:4], in0=mv_[i2][:, 0:1], scalar=-1.0, in1=mv_[i2][:, 2:3], op0=ALU.mult, op1=ALU.mult),
                        reads=["mv2_%d" % i2], writes=["mv2_%d" % i2])
                    add("act", lambda e, i2=i2: e.activation(out=zn[:], in_=zt[:], func=AF.Identity, scale=mv_[i2][:, 2:3], bias=mv_[i2][:, 3:4]),
                        reads=["mv2_%d" % i2], writes=["zn2", "zt2:0", "zt2:1"])
                    add("dve", lambda e: e.tensor_tensor(out=zn[:], in0=zn[:], in1=ln2g[:], op=ALU.mult), reads=["ln2g"], writes=["zn2"])
                    add("pool", lambda e, i2=i2: e.tensor_tensor(out=yst[i2][:], in0=zn[:], in1=ln2b[:], op=ALU.add),
                        reads=["ln2b"], writes=["yst%d" % i2, "zn2", "zt2:0", "zt2:1"])
                    if t < NT:
                        add("sp", lambda e, i2=i2, t=t: e.dma_start(out=dram["y_p"][t * 128:(t + 1) * 128, :], in_=yst[i2][:]),
                            reads=["yst%d" % i2], writes=["o_y:%d" % t], dma="yst%d" % i2)
                    else:
                        add("sp", lambda e, i2=i2: e.dma_start(out=dram["y_s"], in_=yst[i2][0:4, :]),
                            reads=["yst%d" % i2], writes=["o_ys"], dma="yst%d" % i2)
                S_.flush(barrier=True)
        S_.flush(barrier=True)
    print("instructions emitted:", S_.nins)
    return nc


_CACHE = {}


def kernel(**inputs):
    x = np.ascontiguousarray(inputs["x_prompt"][0])
    if "nc" not in _CACHE:
        _CACHE["nc"] = build_program()
    nc = _CACHE["nc"]
    in_maps = []
    xs = np.ascontiguousarray(inputs["x_sample"][:, 0, :])
    caches = [np.ascontiguousarray(inputs[k][0]).reshape(32, -1, 512) for k in ("cache_win128_kv", "cache_win512_kv", "cache_win2048_kv")]
    cmem = np.ascontiguousarray(inputs["cache_mem_kv"][0]).reshape(32, 256, 1024)
    cstate = np.ascontiguousarray(inputs["state_ffn_conv"][0])
    for c in range(NCORES):
        xh = np.zeros((2 * T, D), np.float32)
        if c > 0:
            xh[:T] = x[(c - 1) * T:c * T]
        xh[T:] = x[c * T:(c + 1) * T]
        hb = np.full((128, 1), NEG if c == 0 else 0.0, np.float32)
        xd = np.zeros((8, D), np.float32)
        xd[0:4] = xs[4 * c:4 * c + 4]
        xbk = np.zeros((2, 3, 128, D), np.float32)
        bkmask = np.zeros((128, 6), np.float32)
        if c > 0:
            for nb in range(2):
                p = c * T - 2 + nb
                xd[4 + nb] = x[p]
                for g, (win, dil) in enumerate(GROUPS):
                    pos = p - dil * np.arange(1, 129)
                    ok = pos >= 0
                    xbk[nb, g, ok] = x[pos[ok]]
                    bkmask[~ok, nb * 3 + g] = NEG
        m = {"xh": xh, "hbias": hb, "mem": np.ascontiguousarray(inputs["mem_prompt"][0]),
             "xd": xd, "xbk": xbk, "bkmask": bkmask, "bflag": np.full((128, 1), 0.0 if c == 0 else 1.0, np.float32),
             "cwin0": caches[0][4 * c:4 * c + 4], "cwin1": caches[1][4 * c:4 * c + 4], "cwin2": caches[2][4 * c:4 * c + 4],
             "cmem": cmem[4 * c:4 * c + 4], "cstate": cstate[4 * c:4 * c + 4]}
        for nm in ("w_in", "w_mem_kv", "ln_v_g", "ln_v_b", "w_spatial", "b_spatial", "w_branch_a", "w_branch_b", "w_branch_m",
                   "b_gate", "w_out", "ln1_g", "ln1_b", "w_up", "conv_w", "conv_b", "w_down", "ln2_g", "ln2_b"):
            m[nm] = np.ascontiguousarray(inputs[nm][0])
        in_maps.append(m)
    sel = os.environ.get("MK_CORES")
    if sel:
        ids = [int(v) for v in sel.split(",")]
        res = run_bass_kernel_spmd(nc, [in_maps[i] for i in ids], core_ids=list(range(len(ids))))
        return res
    res = run_bass_kernel_spmd(nc, in_maps, core_ids=list(range(NCORES)))
    R = res.results
    f32 = np.float32
    y_prompt = np.concatenate([R[c]["y_p"] for c in range(NCORES)], axis=0).reshape(1, S, D).astype(f32)
    y_sample = np.concatenate([R[c]["y_s"] for c in range(NCORES)], axis=0).reshape(32, 1, D).astype(f32)
    win_p = [np.asarray(R[NCORES - 1]["win%d_p" % g]).reshape(1, 1, GROUPS[g][0], 2, 4, 64).astype(f32) for g in range(3)]
    mem_p = np.asarray(R[0]["new_mem_kv"]).reshape(1, 1, 256, 2, 4, 128).astype(f32)
    conv_p = np.asarray(R[NCORES - 1]["conv_p"]).reshape(1, 1, 2, DFF).astype(f32)
    win_s = [np.concatenate([R[c]["win%d_s" % g] for c in range(NCORES)], axis=0).reshape(1, 32, 1, 2, 4, 64).astype(f32) for g in range(3)]
    gv_s = np.concatenate([R[c]["gmlp_v_s"] for c in range(NCORES)], axis=0).reshape(1, 32, 1, 768).astype(f32)
    conv_s = np.concatenate([R[c]["conv_s"] for c in range(NCORES)], axis=0).reshape(1, 32, 2, DFF).astype(f32)
    return (y_prompt, y_sample, win_p[0], win_p[1], win_p[2], mem_p, conv_p, win_s[0], win_s[1], win_s[2], gv_s, conv_s)
```

```python
import os
import numpy as np
from contextlib import ExitStack
import concourse.bass as bass
import concourse.mybir as mybir
from concourse.bass_utils import run_bass_kernel_spmd

F32 = mybir.dt.float32
BF16 = mybir.dt.bfloat16
AF = mybir.ActivationFunctionType
ALU = mybir.AluOpType
AX = mybir.AxisListType

NCORES = 8
D = 1024
S = 16384
T = S // NCORES
NT = T // 128
TW = T + 8
INW = 7424
DFF = 2816
NJ = DFF // 128
ALPHA = 2.0 ** 0.25
EPS = 1e-5
NEG = -30000.0
GROUPS = ((128, 1), (512, 4), (2048, 16))
DBG = os.environ.get("MK_DEBUG", "")


class Op:
    __slots__ = ("eng", "fn", "reads", "writes", "dma", "deps", "signal", "sigval", "waits", "waitall")

    def __init__(self, eng, fn, reads, writes, dma, waitall):
        self.eng = eng
        self.fn = fn
        self.reads = tuple(reads)
        self.writes = tuple(writes)
        self.dma = dma
        self.deps = ()
        self.signal = False
        self.sigval = None
        self.waits = ()
        self.waitall = waitall


class Sched:
    def __init__(self, nc, stack):
        self.nc = nc
        self.stack = stack
        self.engines = {"pe": nc.tensor, "act": nc.scalar, "dve": nc.vector, "pool": nc.gpsimd, "sp": nc.sync}
        self.esem = {e: stack.enter_context(nc.semaphore("s_" + e)) for e in self.engines}
        self.ecnt = {e: 0 for e in self.engines}
        self.dsem = {}
        self.dcnt = {}
        self.waited = {e: {} for e in self.engines}
        self.ops = []
        self.nins = 0

    enabled = True

    def add(self, eng, fn, reads=(), writes=(), dma=None, waitall=False):
        if self.enabled:
            self.ops.append(Op(eng, fn, reads, writes, dma, waitall))

    def stop_at(self, name):
        if os.environ.get("MK_STOP") == name:
            self.enabled = False

    def _sem(self, key):
        if key[0] == "e":
            return self.esem[key[1]]
        return self.dsem[key[1]]

    def flush(self, barrier=True):
        ops = self.ops
        self.ops = []
        last_w = {}
        readers = {}
        for i, op in enumerate(ops):
            deps = set()
            for k in op.reads:
                if k in last_w:
                    deps.add(last_w[k])
            for k in op.writes:
                if k in last_w:
                    deps.add(last_w[k])
                deps.update(readers.get(k, ()))
            deps.discard(i)
            for k in op.reads:
                readers.setdefault(k, []).append(i)
            for k in op.writes:
                last_w[k] = i
                readers[k] = []
            op.deps = deps

        def skip(pj, op):
            return pj.dma is None and op.dma is None and pj.eng == "pe" and op.eng == "pe"

        for op in ops:
            for j in op.deps:
                pj = ops[j]
                if pj.dma is None and not skip(pj, op):
                    pj.signal = True
        if barrier:
            lastc = {}
            for i, op in enumerate(ops):
                if op.dma is None and op.fn is not None:
                    lastc[op.eng] = i
            for i in lastc.values():
                ops[i].signal = True
        dfinal = dict(self.dcnt)
        for op in ops:
            if op.dma is not None:
                dfinal[op.dma] = dfinal.get(op.dma, 0) + 16
        for op in ops:
            if op.dma is not None:
                if op.dma not in self.dsem:
                    self.dsem[op.dma] = self.stack.enter_context(self.nc.semaphore("d_" + op.dma))
                    self.dcnt[op.dma] = 0
                self.dcnt[op.dma] += 16
                op.sigval = (("d", op.dma), dfinal[op.dma] if op.waitall else self.dcnt[op.dma])
            elif op.signal:
                self.ecnt[op.eng] += 1
                op.sigval = (("e", op.eng), self.ecnt[op.eng])
        for op in ops:
            need = {}
            for j in op.deps:
                pj = ops[j]
                if skip(pj, op):
                    continue
                k, v = pj.sigval
                if need.get(k, 0) < v:
                    need[k] = v
            eng = self.engines[op.eng]
            wd = self.waited[op.eng]
            for k, v in need.items():
                if wd.get(k, 0) < v:
                    eng.wait_ge(self._sem(k), v)
                    wd[k] = v
                    self.nins += 1
            if op.fn is None:
                continue
            ins = op.fn(eng)
            self.nins += 1
            if op.dma is not None:
                ins.then_inc(self.dsem[op.dma], 16)
            elif op.signal:
                ins.then_inc(self.esem[op.eng], 1)
        if barrier:
            for e, eng in self.engines.items():
                wd = self.waited[e]
                for e2 in self.engines:
                    if e2 != e and wd.get(("e", e2), 0) < self.ecnt[e2]:
                        eng.wait_ge(self.esem[e2], self.ecnt[e2])
                        wd[("e", e2)] = self.ecnt[e2]
                for dk, dv in self.dcnt.items():
                    if wd.get(("d", dk), 0) < dv:
                        eng.wait_ge(self.dsem[dk], dv)
                        wd[("d", dk)] = dv


def build_program(stage=99):
    nc = bass.Bass("TRN2", target_bir_lowering=False)
    dram = {}

    def din(name, shape):
        dram[name] = nc.dram_tensor(name, list(shape), F32, kind="ExternalInput").ap()
        return dram[name]

    def dout(name, shape):
        dram[name] = nc.dram_tensor(name, list(shape), F32, kind="ExternalOutput").ap()
        return dram[name]

    xh = din("xh", [2 * T, D])
    hbias_d = din("hbias", [128, 1])
    w_in = din("w_in", [D, INW])
    out_win = [dout("win%d_p" % g, [GROUPS[g][0], 512]) for g in range(3)]
    din("mem", [256, D])
    din("w_mem_kv", [D, D])
    din("ln_v_g", [768])
    din("ln_v_b", [768])
    din("w_spatial", [4, 128, 128])
    din("b_spatial", [4, 128])
    din("w_branch_a", [256, D])
    din("w_branch_b", [768, D])
    din("w_branch_m", [512, D])
    din("b_gate", [3, D])
    din("w_out", [D, D])
    din("ln1_g", [D])
    din("ln1_b", [D])
    dout("new_mem_kv", [256, D])
    din("w_up", [D, 2 * DFF])
    din("conv_w", [3, DFF])
    din("conv_b", [DFF])
    din("w_down", [DFF, D])
    din("ln2_g", [D])
    din("ln2_b", [D])
    din("xd", [8, D])
    din("xbk", [2, 3, 128, D])
    din("bkmask", [128, 6])
    din("bflag", [128, 1])
    din("cwin0", [4, 128, 512])
    din("cwin1", [4, 512, 512])
    din("cwin2", [4, 2048, 512])
    din("cmem", [4, 256, D])
    din("cstate", [4, 2, DFF])
    dout("y_s", [4, D])
    for g_ in range(3):
        dout("win%d_s" % g_, [4, 512])
    dout("gmlp_v_s", [4, 768])
    dout("conv_s", [4, 2, DFF])
    dbg_d = dout("dbg_d", [128, 64]) if "dd" in DBG else None
    dout("y_p", [T, D])
    dout("conv_p", [2, DFF])
    x1s = nc.dram_tensor("x1s", [T + 128, D], F32, kind="Internal").ap()
    dbg_x1 = dout("dbg_x1", [T, D]) if "x1" in DBG else None
    dbg_oa = dout("dbg_oa", [256, T]) if "oa" in DBG else None

    with ExitStack() as st:
        S_ = Sched(nc, st)
        add = S_.add

        def sb(name, shape, dt, stack=st):
            return stack.enter_context(nc.sbuf_tensor(name, list(shape), dt))

        banks = [st.enter_context(nc.psum_tensor("bank%d" % i, [128, 512], F32)) for i in range(7)]
        bankT = st.enter_context(nc.psum_tensor("bankT", [128, 1024], BF16))
        bankT2 = banks[6][:].bitcast(BF16)
        identf = sb("identf", [128, 128], F32)
        ident = sb("ident", [128, 128], BF16)
        zerosb = sb("zerosb", [128, 512], BF16)
        maskP = sb("maskP", [128, 512], BF16)
        maskC = sb("maskC", [128, 512], BF16)
        maskPH = sb("maskPH", [128, 512], BF16)
        onespad = sb("onespad", [128, 2, 128], BF16)
        hbias = sb("hbias_sb", [128, 1], F32)
        mhalf = sb("mhalf", [128, 1], F32)
        xT = sb("xT", [128, 8, TW], BF16)

        add("pool", lambda e: e.memset(identf[:], 1.0), writes=["identf"])
        add("pool", lambda e: e.affine_select(out=identf[:], in_=identf[:], pattern=[[-1, 128]], compare_op=ALU.is_equal,
                                              fill=0.0, base=0, channel_multiplier=1), reads=["identf"], writes=["identf"])
        add("dve", lambda e: e.tensor_copy(out=ident[:], in_=identf[:]), reads=["identf"], writes=["ident"])
        add("pool", lambda e: e.memset(zerosb[:], 0.0), writes=["zerosb"])
        onesm = sb("onesm", [128, 512], BF16)
        add("pool", lambda e: e.memset(onesm[:], 1.0), writes=["onesm"])
        add("pool", lambda e: e.affine_select(out=maskP[:], in_=onesm[:], pattern=[[0, 4], [-1, 128]], compare_op=ALU.is_ge,
                                              fill=0.0, base=0, channel_multiplier=1), reads=["onesm"], writes=["maskP"])
        add("pool", lambda e: e.affine_select(out=maskC[:], in_=onesm[:], pattern=[[0, 4], [1, 128]], compare_op=ALU.is_ge,
                                              fill=0.0, base=0, channel_multiplier=-1), reads=["onesm"], writes=["maskC"])
        add("sp", lambda e: e.dma_start(out=hbias[:], in_=hbias_d), writes=["hbias"], dma="c_hb")
        with ExitStack() as lc:
            maskPf = sb("maskPf", [128, 512], F32, lc)
            add("dve", lambda e: e.tensor_copy(out=maskPf[:], in_=maskP[:]), reads=["maskP"], writes=["maskPf"])
            add("dve", lambda e: e.tensor_scalar(out=hbias[:, 0:1], in0=hbias[:, 0:1], scalar1=1.0 / 30000.0, scalar2=1.0, op0=ALU.mult, op1=ALU.add),
                reads=["hbias"], writes=["hbias"])
            add("dve", lambda e: e.tensor_scalar(out=maskPH[:], in0=maskPf[:], scalar1=hbias[:, 0:1], scalar2=None, op0=ALU.mult),
                reads=["maskPf", "hbias"], writes=["maskPH"])
            S_.flush(barrier=True)
        add("pool", lambda e: e.memset(mhalf[:], -0.5), writes=["mhalf"])
        add("pool", lambda e: e.memset(onespad[:], 0.0), writes=["onespad"])
        add("pool", lambda e: e.memset(onespad[:, 0, 0:64], 1.0), reads=["onespad"], writes=["onespad"])
        add("pool", lambda e: e.memset(onespad[:, 1, 64:128], 1.0), reads=["onespad"], writes=["onespad"])

        S_.stop_at("consts")
        w_in_v = w_in.rearrange("(k p) c -> p k c", p=128)
        xh_v = xh.rearrange("(t p) c -> p t c", p=128)

        class Rot:
            def __init__(self, ids):
                self.ids = ids
                self.i = 0

            def next(self):
                b = self.ids[self.i % len(self.ids)]
                self.i += 1
                return b

        evac_tog = [0]

        def evac_eng():
            evac_tog[0] ^= 1
            return "act" if evac_tog[0] else "dve"

        def copy_op(eng, out, in_, reads, writes):
            if eng == "act":
                add("act", lambda e: e.activation(out=out, in_=in_, func=AF.Identity), reads=reads, writes=writes)
            else:
                add(eng, lambda e: e.tensor_copy(out=out, in_=in_), reads=reads, writes=writes)

        with ExitStack() as l1:
            o_aT = sb("o_aT", [128, 2, TW], BF16, l1)
            with ExitStack() as l2:
                xb = [sb("xb%d" % i, [128, 2, D], BF16, l2) for i in range(2)]
                xTh = sb("xTh", [128, 8, 512], BF16, l2)
                wqkv = [sb("wqkv%d" % i, [128, 8, 768], BF16, l2) for i in range(2)]
                qpad = sb("qpad", [128, 2, 2, T], BF16, l2)
                kT = sb("kT", [128, 2, 2 * T], BF16, l2)
                vpad = sb("vpad", [128, 32, 4, 128], BF16, l2)
                acc = sb("acc", [128, 4, T], F32, l2)
                PT = [sb("PT%d" % i, [128, 2, 512], BF16, l2) for i in range(2)]
                kvst = [sb("kvst0", [128, 512], F32, l2), None]

                kvst[1] = PT[1][:].rearrange("p a b -> p (a b)").bitcast(F32)
                xdb = PT[0][:].rearrange("p a b -> p (a b)")
                xdTp = sb("xdTp", [128, 8, 128], BF16, l2)
                dfm = sb("dfm", [128, 3, 6, 8], F32, l2)
                dqpad = sb("dqpad", [128, 3, 2, 2, 8], BF16, l2)
                ck = sb("ck", [128, 6, 512], BF16, l2)
                ckT = sb("ckT", [128, 12, 128], BF16, l2)
                cvpad = sb("cvpad", [128, 6, 4, 128], BF16, l2)
                dP = sb("dP", [128, 32], BF16, l2)
                dacc = sb("dacc", [128, 2, 2, 8], F32, l2)
                dprod = sb("dprod", [128, 2, 8], F32, l2)
                dpo = sb("dpo", [128, 2, 8], F32, l2)
                dov = sb("dov", [128, 2, 8], F32, l2)
                headsel = sb("headsel", [128, 128], F32, l2)
                bkm = sb("bkm", [128, 6], F32, l2)
                def kvst_ap(i):
                    return kvst[0][:] if i == 0 else kvst[1]
                xb_i = [0]

                def load_xT(rows_ap_fn, ntiles, dst, dst_key):
                    for t0 in range(0, ntiles, 2):
                        slot = xb_i[0] % 2
                        xb_i[0] += 1
                        src = rows_ap_fn(t0, 2)
                        add("pool", lambda e, slot=slot, src=src: e.dma_start(out=xb[slot][:], in_=src),
                            writes=["xb%d" % slot], dma="xb%d" % slot)
                        if zero_jobs:
                            zj = zero_jobs.pop(0)
                            add("pool", lambda e, zj=zj: e.memset(zj[0], 0.0), writes=[zj[1]])
                        for tt in range(2):
                            t = t0 + tt
                            bt, btk = (bankT[:], "bankT") if tt == 0 else (bankT2, "bank6")
                            for k in range(8):
                                add("pe", lambda e, slot=slot, tt=tt, k=k, bt=bt: e.transpose(
                                    out=bt[:, k * 128:(k + 1) * 128], in_=xb[slot][:, tt, k * 128:(k + 1) * 128], identity=ident[:]),
                                    reads=["xb%d" % slot, "ident"], writes=[btk])
                            copy_op(evac_eng(), dst[:, :, t * 128:(t + 1) * 128], bt.rearrange("p (k c) -> p k c", k=8),
                                    [], ["%s:%d" % (dst_key, t), btk])

                zero_jobs = [(qpad[:, c], "qpad_z%d" % c) for c in range(2)] + [(vpad[:, 8 * i:8 * (i + 1)], "vpad_z%d" % i) for i in range(4)]
                def load_wqkv(g):
                    ws_ = g % 2
                    for part, c0 in enumerate((g * 256, 768 + g * 256, 1536 + g * 256)):
                        add("pool", lambda e, part=part, c0=c0, ws_=ws_: e.dma_start(
                            out=wqkv[ws_][:, :, part * 256:(part + 1) * 256], in_=w_in_v[:, :, c0:c0 + 256]),
                            writes=["wqkv%d:%d" % (ws_, part)], dma="wqkv%d_%d" % (ws_, part))
                load_wqkv(2)
                load_xT(lambda t0, n: xh_v[:, 16 + t0:16 + t0 + n, :], 16, xT, "xT")


                add("dve", lambda e: e.memset(xdb, 0.0), writes=["xdb"])
                add("pool", lambda e: e.dma_start(out=xdb[0:8, :], in_=dram["xd"]), reads=["xdb"], writes=["xdb2"], dma="xdb")
                add("pool", lambda e: e.memset(dqpad[:], 0.0), writes=["dqpad_z"])
                add("dve", lambda e: e.memset(cvpad[:], 0.0), writes=["cvpad_z"])
                add("pool", lambda e: e.memset(headsel[:], 0.0), writes=["headsel"])
                add("pool", lambda e: e.memset(headsel[0:64, 0:64], 1.0), reads=["headsel"], writes=["headsel"])
                add("pool", lambda e: e.memset(headsel[64:128, 64:128], 1.0), reads=["headsel"], writes=["headsel"])
                add("sp", lambda e: e.dma_start(out=bkm[:], in_=dram["bkmask"]), writes=["bkm"], dma="c_bkm")
                for k in range(8):
                    add("pe", lambda e, k=k: e.transpose(out=bankT[:, k * 128:(k + 1) * 128], in_=xdb[:, k * 128:(k + 1) * 128], identity=ident[:]),
                        reads=["xdb", "xdb2", "ident"], writes=["bankT"])
                copy_op("dve", xdTp[:], bankT[:].rearrange("p (k c) -> p k c", k=8), [], ["xdTp", "bankT"])
                add("pool", lambda e: e.tensor_copy(out=xT[:, :, T:TW], in_=xdTp[:, :, 0:8]), reads=["xdTp"], writes=["xTd"])

                S_.stop_at("xT")
                rotP = Rot([0, 1, 2, 3])
                wq_i = [0]
                first_group = [True]
                kv_i = [0]
                pt_i = [0]

                for g in (2, 1, 0):
                    win, dil = GROUPS[g]
                    ws = g % 2
                    W = wqkv[ws]
                    if g > 0:
                        load_wqkv(g - 1)
                    wkeys = ["wqkv%d:%d" % (ws, p) for p in range(3)]
                    S_.stop_at("wload%d" % g)
                    nblk_halo = 1
                    if g == 0:
                        halo_tok = 128
                    elif g == 1:
                        halo_tok = 512
                    else:
                        halo_tok = 2048
                    nb = T // win
                    kstride = (nb + 1) * 128

                    def kcol(r, n):
                        return r * kstride + (n + 1) * 128

                    def qcol(r, n):
                        return r * nb * 128 + n * 128

                    def vblk(r, n):
                        return r * (nb + 1) + (n + 1)

                    nh_tiles = halo_tok // 128
                    hbase = T - halo_tok

                    def halo_rows(t0, n, dil=dil, hbase=hbase):
                        if dil == 1:
                            return xh_v[:, (hbase // 128) + t0:(hbase // 128) + t0 + n, :]
                        v = xh[hbase:hbase + 128 * dil, :].rearrange("(i r) c -> i r c", r=dil)
                        return v[:, t0:t0 + n, :]
                    S_.stop_at("halo%d" % g)
                    def perm_dst(buf2d, tt, width_per_r, col_of):
                        if g == 0:
                            return buf2d[:, col_of(0, 0) + tt * 512:col_of(0, 0) + (tt + 1) * 512], None
                        if g == 1:
                            v = buf2d.rearrange("p (r m) -> p r m", r=4)
                            off = col_of(0, tt)
                            return v[:, :, off:off + 128], 4
                        v = buf2d.rearrange("p (r m) -> p r m", r=16)
                        off = col_of(0, 0) + 32 * tt
                        return v[:, :, off:off + 32], 16

                    for c in range(2):
                        for what in ("k", "q"):
                            for tt in range(4):
                                b = rotP.next()
                                wc0 = (256 if what == "k" else 0) + c * 128
                                for k in range(8):
                                    add("pe", lambda e, b=b, k=k, tt=tt, wc0=wc0, W=W: e.matmul(
                                        banks[b][:], lhsT=W[:, k, wc0:wc0 + 128], rhs=xT[:, k, tt * 512:(tt + 1) * 512],
                                        start=(k == 0), stop=(k == 7)),
                                        reads=wkeys + ["xT:%d" % t for t in range(tt * 4, tt * 4 + 4)], writes=["bank%d" % b])
                                if what == "k":
                                    dstv, rr = perm_dst(kT[:, c, 0:dil * kstride], tt, kstride, kcol)
                                    srcv = banks[b][:] if rr is None else banks[b][:].rearrange("p (i r) -> p r i", r=rr)
                                    copy_op(evac_eng(), dstv, srcv, ["bank%d" % b], ["kT:%d:own%d" % (c, tt)])
                                else:
                                    ve = evac_eng()
                                    for hp in range(2):
                                        ps = slice(hp * 64, (hp + 1) * 64)
                                        dstv, rr = perm_dst(qpad[:, c, hp, :], tt, nb * 128, qcol)
                                        dstv = dstv[ps]
                                        srcv = banks[b][ps, :] if rr is None else banks[b][ps, :].rearrange("p (i r) -> p r i", r=rr)
                                        copy_op(ve, dstv, srcv, ["bank%d" % b, "qpad_z0", "qpad_z1"], ["qpad:%d:%d:%d" % (c, hp, tt)])
                    kown = ["kT:%d:own%d" % (c, tt) for c in range(2) for tt in range(4)]
                    qown = ["qpad:%d:%d:%d" % (c, hp, tt) for c in range(2) for hp in range(2) for tt in range(4)]

                    S_.stop_at("kq%d" % g)
                    for r in range(dil):
                        for n in range(nb):
                            b = rotP.next()
                            base = n * win + r
                            for k in range(8):
                                add("pe", lambda e, b=b, k=k, base=base, W=W, dil=dil: e.matmul(
                                    banks[b][:, 0:256], lhsT=xT[:, k, base:base + 127 * dil + 1:dil], rhs=W[:, k, 512:768],
                                    start=(k == 0), stop=(k == 7)),
                                    reads=wkeys + ["xT:%d" % t for t in range(n * win // 128, (n + 1) * win // 128)], writes=["bank%d" % b])
                            blk = vblk(r, n)
                            ve = evac_eng()
                            for hp in range(2):
                                dstv = vpad[:, blk, hp::2, hp * 64:(hp + 1) * 64]
                                srcv = banks[b][:, 0:256].rearrange("p (c q x) -> p c q x", c=2, q=2)[:, :, hp, :]
                                copy_op(ve, dstv, srcv, ["bank%d" % b, "vpad_z0", "vpad_z1", "vpad_z2", "vpad_z3"], ["vpad:%d:%d" % (blk, hp)])

                    S_.stop_at("v%d" % g)
                    for t in range(NT - win // 128, NT):
                        b = rotP.next()
                        for k in range(8):
                            add("pe", lambda e, b=b, k=k, t=t, W=W: e.matmul(
                                banks[b][:, 0:256], lhsT=xT[:, k, t * 128:(t + 1) * 128], rhs=W[:, k, 256:512],
                                start=(k == 0), stop=(k == 7)), reads=wkeys + ["xT:%d" % t], writes=["bank%d" % b])
                        for k in range(8):
                            add("pe", lambda e, b=b, k=k, t=t, W=W: e.matmul(
                                banks[b][:, 256:512], lhsT=xT[:, k, t * 128:(t + 1) * 128], rhs=W[:, k, 512:768],
                                start=(k == 0), stop=(k == 7)), reads=wkeys + ["xT:%d" % t], writes=["bank%d" % b])
                        ks = kv_i[0] % 2
                        kv_i[0] += 1
                        copy_op(evac_eng(), kvst_ap(ks), banks[b][:], ["bank%d" % b], ["kvst%d" % ks])
                        row0 = (t - (NT - win // 128)) * 128
                        add("sp", lambda e, ks=ks, g=g, row0=row0: e.dma_start(out=out_win[g][row0:row0 + 128, :], in_=kvst_ap(ks)),
                            reads=["kvst%d" % ks], writes=["out_kvst%d" % ks], dma="kvst%d" % ks)

                    for hc0 in range(0, nh_tiles, 4):
                        hn = min(4, nh_tiles - hc0)
                        if hn >= 2:
                            load_xT(lambda t0, n, hc0=hc0: halo_rows(hc0 + t0, n), hn, xTh, "xTh")
                        else:
                            slot = xb_i[0] % 2
                            xb_i[0] += 1
                            src = halo_rows(0, 1)
                            add("pool", lambda e, slot=slot, src=src: e.dma_start(out=xb[slot][:, 0:1, :], in_=src),
                                writes=["xb%d" % slot], dma="xb%d" % slot)
                            for k in range(8):
                                add("pe", lambda e, slot=slot, k=k: e.transpose(
                                    out=bankT[:, k * 128:(k + 1) * 128], in_=xb[slot][:, 0, k * 128:(k + 1) * 128], identity=ident[:]),
                                    reads=["xb%d" % slot, "ident"], writes=["bankT"])
                            copy_op(evac_eng(), xTh[:, :, 0:128], bankT[:].rearrange("p (k c) -> p k c", k=8), ["bankT"], ["xTh:0"])
                        S_.stop_at("hload%d" % g)
                        for c in range(2):
                            for h0 in range(0, hn * 128, 512):
                                n = min(512, hn * 128 - h0)
                                b = rotP.next()
                                for k in range(8):
                                    add("pe", lambda e, b=b, c=c, k=k, h0=h0, n=n, W=W: e.matmul(
                                        banks[b][:, 0:n], lhsT=W[:, k, 256 + c * 128:256 + (c + 1) * 128], rhs=xTh[:, k, h0:h0 + n],
                                        start=(k == 0), stop=(k == 7)),
                                        reads=wkeys + ["xTh:%d" % t for t in range(h0 // 128, (h0 + n) // 128)], writes=["bank%d" % b])
                                nr = n // 128
                                r0 = hc0 + h0 // 128
                                dstv = kT[:, c, 0:dil * kstride].rearrange("p (r m) -> p r m", m=kstride)[:, r0:r0 + nr, 0:128]
                                copy_op(evac_eng(), dstv, banks[b][:, 0:n].rearrange("p (r i) -> p r i", i=128),
                                        ["bank%d" % b], ["kT:%d:%d" % (c, r) for r in range(r0, r0 + nr)])
                        S_.stop_at("hk%d" % g)
                        for rl in range(hn):
                            r = hc0 + rl
                            b = rotP.next()
                            for k in range(8):
                                add("pe", lambda e, b=b, k=k, rl=rl, W=W: e.matmul(
                                    banks[b][:, 0:256], lhsT=xTh[:, k, rl * 128:(rl + 1) * 128], rhs=W[:, k, 512:768],
                                    start=(k == 0), stop=(k == 7)), reads=wkeys + ["xTh:%d" % rl], writes=["bank%d" % b])
                            blk = vblk(r, -1)
                            ve = evac_eng()
                            for hp in range(2):
                                dstv = vpad[:, blk, hp::2, hp * 64:(hp + 1) * 64]
                                srcv = banks[b][:, 0:256].rearrange("p (c q x) -> p c q x", c=2, q=2)[:, :, hp, :]
                                copy_op(ve, dstv, srcv, ["bank%d" % b, "vpad_z0", "vpad_z1", "vpad_z2", "vpad_z3"], ["vpad:%d:%d" % (blk, hp)])
                                S_.stop_at("hv%d_%d_%d" % (g, rl, hp))
                        S_.stop_at("hchunk%d" % g)

                    ckk = ["ck:s", "ck:4", "ck:5"]
                    dpk = ["dP:s", "dP:4", "dP:5"]
                    def dec_p1():
                        for part in range(3):
                            for c in range(2):
                                pc = part * 2 + c
                                for k in range(8):
                                    add("pe", lambda e, pc=pc, part=part, c=c, k=k, W=W: e.matmul(
                                        banks[6][:, pc * 8:(pc + 1) * 8], lhsT=W[:, k, part * 256 + c * 128:part * 256 + (c + 1) * 128], rhs=xT[:, k, T:TW],
                                        start=(k == 0), stop=(k == 7)), reads=wkeys + ["xTd"], writes=["bank6"])
                        add("dve", lambda e, g=g: e.tensor_copy(out=dfm[:, g, :, :], in_=banks[6][:, 0:48].rearrange("p (a n) -> p a n", a=6)),
                            writes=["dfm:%d" % g, "bank6"])
                        for c in range(2):
                            for hp in range(2):
                                ps = slice(hp * 64, (hp + 1) * 64)
                                add("pool", lambda e, g=g, c=c, hp=hp, ps=ps: e.tensor_copy(out=dqpad[ps, g, c, hp, :], in_=dfm[ps, g, c, :]),
                                    reads=["dfm:%d" % g, "dqpad_z"], writes=["dqpad:%d" % g])
                    def dec_p2():
                        b = rotP.next()
                        for (part, c0) in ((1, 0), (2, 256)):
                            for k in range(8):
                                add("pe", lambda e, b=b, k=k, part=part, c0=c0, W=W: e.matmul(
                                    banks[b][:, c0:c0 + 256], lhsT=xdTp[:, k, :], rhs=W[:, k, part * 256:(part + 1) * 256],
                                    start=(k == 0), stop=(k == 7)), reads=wkeys + ["xdTp"], writes=["bank%d" % b])
                        ks = 0
                        copy_op(evac_eng(), kvst_ap(ks), banks[b][:], [], ["kvst%d" % ks, "bank%d" % b])
                        add("sp", lambda e, ks=ks, g=g: e.dma_start(out=dram["win%d_s" % g], in_=kvst_ap(ks)[0:4, :]),
                            reads=["kvst%d" % ks], writes=["out_kvst%d" % ks], dma="kvst%d" % ks)
                    def dec_ck():
                        csrc = dram["cwin%d" % g][:, 0:win - dil + 1:dil, :].rearrange("b r x -> r b x")
                        add("pool", lambda e, csrc=csrc: e.dma_start(out=ck[:, 0:4, :], in_=csrc), writes=["ck:s"], dma="ck_s")
                    bslots = []

                    def dec_p3a():
                        for nb_ in range(2):
                            slot = xb_i[0] % 2
                            xb_i[0] += 1
                            bslots.append(slot)
                            add("pool", lambda e, slot=slot, nb_=nb_, g=g: e.dma_start(out=xb[slot][:, 0, :], in_=dram["xbk"][nb_, g]),
                                writes=["xb%d" % slot], dma="xb%d" % slot)

                    def dec_p3():
                        for nb_ in range(2):
                            slot = bslots[nb_]
                            for k in range(8):
                                add("pe", lambda e, slot=slot, k=k: e.transpose(
                                    out=bankT[:, k * 128:(k + 1) * 128], in_=xb[slot][:, 0, k * 128:(k + 1) * 128], identity=ident[:]),
                                    reads=["xb%d" % slot, "ident"], writes=["bankT"])
                            copy_op(evac_eng(), xTh[:, :, 0:128], bankT[:].rearrange("p (k c) -> p k c", k=8), [], ["xTh:0", "bankT"])
                            b = rotP.next()
                            for (part, c0) in ((1, 0), (2, 256)):
                                for k in range(8):
                                    add("pe", lambda e, b=b, k=k, part=part, c0=c0, W=W: e.matmul(
                                        banks[b][:, c0:c0 + 256], lhsT=xTh[:, k, 0:128], rhs=W[:, k, part * 256:(part + 1) * 256],
                                        start=(k == 0), stop=(k == 7)), reads=wkeys + ["xTh:0"], writes=["bank%d" % b])
                            copy_op(evac_eng(), ck[:, 4 + nb_, :], banks[b][:], [], ["ck:%d" % (4 + nb_), "bank%d" % b])
                    def dec_p4():
                        for i8 in range(0, 12, 8):
                            cnt = min(8, 12 - i8)
                            for ii in range(cnt):
                                idx = i8 + ii
                                n_, c_ = idx // 2, idx % 2
                                add("pe", lambda e, ii=ii, n_=n_, c_=c_: e.transpose(
                                    out=bankT[:, ii * 128:(ii + 1) * 128], in_=ck[:, n_, c_ * 128:(c_ + 1) * 128], identity=ident[:]),
                                    reads=ckk + ["ident"], writes=["bankT"])
                            copy_op("dve", ckT[:, i8:i8 + cnt, :], bankT[:, 0:cnt * 128].rearrange("p (k c) -> p k c", k=cnt), [], ["ckT:%d" % i8, "bankT"])
                        for hp in range(2):
                            add("pool", lambda e, hp=hp: e.tensor_copy(
                                out=cvpad[:, :, hp::2, hp * 64:(hp + 1) * 64],
                                in_=ck[:, :, 256:512].rearrange("p n (c q x) -> p n c q x", c=2, q=2)[:, :, :, hp, :]),
                                reads=ckk + ["cvpad_z"], writes=["cvpad:%d" % hp])
                    def dec_p5():
                        for n_ in range(6):
                            for c_ in range(2):
                                for hp in range(2):
                                    col = 64 + n_ * 4 + c_ * 2 + hp
                                    add("pe", lambda e, n_=n_, c_=c_, hp=hp, col=col, g=g: e.matmul(
                                        banks[6][:, col:col + 1], lhsT=ckT[:, n_ * 2 + c_, :], rhs=dqpad[:, g, c_, hp, n_:n_ + 1], start=True, stop=True),
                                        reads=["ckT:0", "ckT:8", "dqpad:%d" % g, "dqpad_z"], writes=["bank6"])
                        add("act", lambda e: e.activation(out=dP[:, 0:16], in_=banks[6][:, 64:80], func=AF.Exp, scale=0.125), writes=["dP:s", "bank6"])
                        for n_ in (4, 5):
                            add("act", lambda e, n_=n_, g=g: e.activation(out=dP[:, n_ * 4:(n_ + 1) * 4], in_=banks[6][:, 64 + n_ * 4:64 + (n_ + 1) * 4], func=AF.Exp,
                                                                           scale=0.125, bias=bkm[:, (n_ - 4) * 3 + g:(n_ - 4) * 3 + g + 1]),
                                reads=["bkm"], writes=["dP:%d" % n_, "bank6"])
                    def dec_p6():
                        for n_ in range(6):
                            for c_ in range(2):
                                for (base, which) in ((128, "o"), (160, "d")):
                                    for hp in range(2):
                                        col = n_ * 4 + c_ * 2 + hp
                                        lhs = cvpad[:, n_, 2 * c_ + hp, :] if which == "o" else onespad[:, hp, :]
                                        add("pe", lambda e, n_=n_, c_=c_, hp=hp, col=col, base=base, lhs=lhs: e.matmul(
                                            banks[6][:, base + c_ * 8 + n_:base + c_ * 8 + n_ + 1], lhsT=lhs, rhs=dP[:, col:col + 1],
                                            start=(hp == 0), stop=(hp == 1)), reads=dpk + ["cvpad:0", "cvpad:1", "cvpad_z", "onespad"], writes=["bank6"])
                    def dec_p7():
                        add("dve", lambda e, g=g: e.tensor_tensor(out=dprod[:], in0=dfm[:, g, 0:2, :], in1=dfm[:, g, 2:4, :], op=ALU.mult),
                            reads=["dfm:%d" % g], writes=["dprod"])
                        add("pe", lambda e: e.matmul(banks[5][:, 0:16], lhsT=headsel[:], rhs=dprod[:].rearrange("p c n -> p (c n)"), start=True, stop=True),
                            reads=["headsel", "dprod"], writes=["bank5"])
                        add("act", lambda e: e.activation(out=dpo[:].rearrange("p c n -> p (c n)"), in_=banks[5][:, 0:16], func=AF.Exp, scale=0.125),
                            writes=["dpo", "bank5"])
                        add("dve", lambda e, g=g: e.tensor_tensor(out=dov[:], in0=dpo[:], in1=dfm[:, g, 4:6, :], op=ALU.mult),
                            reads=["dpo", "dfm:%d" % g], writes=["dov"])
                        add("dve", lambda e: e.tensor_tensor(out=dov[:, :, 0:6], in0=dov[:, :, 0:6], in1=banks[6][:, 128:144].rearrange("p (c n) -> p c n", c=2)[:, :, 0:6], op=ALU.add),
                            reads=["dov"], writes=["dov", "bank6"])
                        add("dve", lambda e: e.tensor_tensor(out=dpo[:, :, 0:6], in0=dpo[:, :, 0:6], in1=banks[6][:, 160:176].rearrange("p (c n) -> p c n", c=2)[:, :, 0:6], op=ALU.add),
                            reads=["dpo"], writes=["dpo", "bank6"])
                        if first_group[0]:
                            add("dve", lambda e: e.tensor_copy(out=dacc[:, 0], in_=dov[:]), reads=["dov"], writes=["dacc"])
                            add("dve", lambda e: e.tensor_copy(out=dacc[:, 1], in_=dpo[:]), reads=["dpo"], writes=["dacc"])
                        else:
                            add("dve", lambda e: e.tensor_tensor(out=dacc[:, 0], in0=dacc[:, 0], in1=dov[:], op=ALU.add), reads=["dov", "dacc"], writes=["dacc"])
                            add("dve", lambda e: e.tensor_tensor(out=dacc[:, 1], in0=dacc[:, 1], in1=dpo[:], op=ALU.add), reads=["dpo", "dacc"], writes=["dacc"])
                    dec_ck()
                    dec_sched = {1: dec_p1, 2: (lambda: (dec_p2(), dec_p3a())), 4: dec_p3, 7: dec_p4, 9: dec_p5, 11: dec_p6, 13: dec_p7}
                    S_.stop_at("kvout%d" % g)
                    rotS = Rot([0, 1, 2, 3])
                    rotO = Rot([4, 5])
                    tiles = [(r, n) for r in range(dil) for n in range(nb)]
                    tinfo = {}

                    def emit_S(ti):
                        r, n = tiles[ti]
                        sbk = [rotS.next(), rotS.next()]
                        ob = rotO.next()
                        pslot = pt_i[0] % 2
                        pt_i[0] += 1
                        tinfo[ti] = (ob, pslot)
                        halo_prev = (n == 0)
                        for bi, (kn, mask) in enumerate(((n - 1, maskPH if halo_prev else maskP), (n, maskC))):
                            b = sbk[bi]
                            mk = "maskPH" if (bi == 0 and halo_prev) else ("maskP" if bi == 0 else "maskC")
                            kc0 = kcol(r, kn)
                            kkeys = kown + (["kT:%d:%d" % (c, r) for c in range(2)] if kn < 0 else [])
                            for hp in range(2):
                                for c in range(2):
                                    j = hp * 2 + c
                                    add("pe", lambda e, b=b, c=c, hp=hp, j=j, kc0=kc0, q0=qcol(r, n): e.matmul(
                                        banks[b][:, j * 128:(j + 1) * 128], lhsT=kT[:, c, kc0:kc0 + 128], rhs=qpad[:, c, hp, q0:q0 + 128],
                                        start=True, stop=True), reads=kkeys + qown + ["qpad_z0", "qpad_z1"], writes=["bank%d" % b])
                            add("act", lambda e, b=b, pslot=pslot, bi=bi: e.activation(
                                out=PT[pslot][:, bi, :], in_=banks[b][:], func=AF.Exp, scale=0.125),
                                reads=["bank%d" % b] + (["out_kvst1", "kvst1"] if pslot == 1 else []), writes=["PT%d:%d" % (pslot, bi)])
                            add("dve", lambda e, pslot=pslot, bi=bi, mask=mask: e.tensor_tensor(out=PT[pslot][:, bi, :], in0=PT[pslot][:, bi, :], in1=mask[:], op=ALU.mult),
                                reads=[mk], writes=["PT%d:%d" % (pslot, bi)])

                    def emit_PV(ti):
                        r, n = tiles[ti]
                        ob, pslot = tinfo[ti]
                        pkeys = ["PT%d:0" % pslot, "PT%d:1" % pslot]
                        for c in range(2):
                            i = 0
                            for hp in range(2):
                                h = 2 * c + hp
                                j = hp * 2 + c
                                for bi, kn in enumerate((n - 1, n)):
                                    blk = vblk(r, kn)
                                    add("pe", lambda e, ob=ob, c=c, h=h, j=j, bi=bi, blk=blk, pslot=pslot, i=i: e.matmul(
                                        banks[ob][:, c * 128:(c + 1) * 128], lhsT=vpad[:, blk, h, :], rhs=PT[pslot][:, bi, j * 128:(j + 1) * 128],
                                        start=(i == 0), stop=(i == 3)),
                                        reads=pkeys + ["vpad_z0", "vpad_z1", "vpad_z2", "vpad_z3", "vpad:%d:%d" % (blk, hp)], writes=["bank%d" % ob])
                                    i += 1
                        i = 0
                        for hp in range(2):
                            for bi in range(2):
                                add("pe", lambda e, ob=ob, hp=hp, bi=bi, pslot=pslot, i=i: e.matmul(
                                    banks[ob][:, 256:512], lhsT=onespad[:, hp, :], rhs=PT[pslot][:, bi, hp * 256:(hp + 1) * 256],
                                    start=(i == 0), stop=(i == 3)), reads=pkeys + ["onespad"], writes=["bank%d" % ob])
                                i += 1
                        p0 = n * win + r
                        accv = acc[:, :, p0:p0 + 127 * dil + 1:dil]
                        srcv = banks[ob][:].rearrange("p (a q) -> p a q", a=4)
                        tkeys = ["acc:%d" % t for t in range(n * win // 128, (n + 1) * win // 128)]
                        if first_group[0]:
                            add("dve", lambda e, accv=accv, srcv=srcv: e.tensor_copy(out=accv, in_=srcv),
                                reads=["bank%d" % ob], writes=tkeys)
                        else:
                            add("dve", lambda e, accv=accv, srcv=srcv: e.tensor_tensor(out=accv, in0=accv, in1=srcv, op=ALU.add),
                                reads=["bank%d" % ob] + tkeys, writes=tkeys)
                        if g == 0:
                            sl = slice(n * 128, (n + 1) * 128)
                            add("dve", lambda e, sl=sl: e.reciprocal(out=acc[:, 2:4, sl], in_=acc[:, 2:4, sl]), reads=tkeys, writes=tkeys)
                            add("pool", lambda e, sl=sl: e.tensor_tensor(out=o_aT[:, :, sl], in0=acc[:, 0:2, sl], in1=acc[:, 2:4, sl], op=ALU.mult),
                                reads=tkeys, writes=["o_aT:%d" % n])
                    emit_S(0)
                    for ti in range(len(tiles)):
                        if ti + 1 < len(tiles):
                            emit_S(ti + 1)
                        emit_PV(ti)
                        if ti in dec_sched:
                            dec_sched[ti]()
                    first_group[0] = False
                    S_.stop_at("attn%d" % g)
                    S_.flush(barrier=True)

                add("dve", lambda e: e.reciprocal(out=dacc[:, 1], in_=dacc[:, 1]), reads=["dacc"], writes=["dacc"])
                add("dve", lambda e: e.tensor_tensor(out=o_aT[:, :, T:TW], in0=dacc[:, 0], in1=dacc[:, 1], op=ALU.mult), reads=["dacc"], writes=["o_aT:d"])
                if dbg_d is not None:
                    add("dve", lambda e: e.tensor_copy(out=dprod[:], in_=o_aT[:, :, T:TW]), reads=["o_aT:d"], writes=["dprod"])
                    add("sp", lambda e: e.dma_start(out=dbg_d[:, 0:16], in_=dprod[:].rearrange("p c n -> p (c n)")), reads=["dprod"], writes=["dbg_d"], dma="dbgd")
                if dbg_oa is not None:
                    add("dve", lambda e: e.tensor_copy(out=acc[:, 0:2, :], in_=o_aT[:, :, 0:T]),
                        reads=["o_aT:%d" % n for n in range(16)],
                        writes=["acc:%d" % t for t in range(16)])
                    add("sp", lambda e: e.dma_start(out=dbg_oa.rearrange("(c p) t -> p c t", p=128), in_=acc[:, 0:2, :]),
                        reads=["acc:%d" % t for t in range(16)], writes=["dbg_oa"], dma="dbg")
                S_.flush(barrier=True)
            S_.stop_at("gmlp_start")
            with ExitStack() as l2b:
                o_bT = sb("o_bT", [128, 8, TW], BF16, l2b)
                onesb = sb("onesb", [128, 128], BF16, l2b)
                add("pool", lambda e: e.memset(onesb[:], 1.0), writes=["onesb"])
                UST = [0, 128, 192, 320, 384, 512, 576, 704]
                with ExitStack() as l3:
                    vtok = sb("vtok", [128, NT, 832], BF16, l3)
                    wvb = sb("wvb", [128, 8, 768], BF16, l3)
                    wub = sb("wub", [128, 8, 896], BF16, l3)
                    gtmp = [sb("gtmp%d" % i, [128, 768], F32, l3) for i in range(2)]
                    gtmp3 = gtmp + [sb("gtmp2", [128, 768], F32, l3)]
                    stats3 = [sb("stats3_%d" % i, [128, 2, 6], F32, l3) for i in range(3)]
                    mv3 = [sb("mv3_%d" % i, [128, 4], F32, l3) for i in range(3)]
                    ntmp = [sb("ntmp%d" % i, [128, 768], F32, l3) for i in range(2)]
                    lnvg = sb("lnvg", [128, 768], F32, l3)
                    lnvb = sb("lnvb", [128, 768], F32, l3)
                    stats = [sb("stats%d" % i, [128, 2, 6], F32, l3) for i in range(2)]
                    mv_ = [sb("mv%d" % i, [128, 4], F32, l3) for i in range(2)]
                    wsn = sb("wsn", [128, 4, 128], F32, l3)
                    wsm = sb("wsm", [128, 4, 128], BF16, l3)
                    wsT = sb("wsT", [128, 4, 128], BF16, l3)
                    bs2 = sb("bs2", [128, 512], F32, l3)
                    bshi = sb("bshi", [128, 512], BF16, l3)
                    bshf = sb("bshf", [128, 512], F32, l3)
                    bsrep = sb("bsrep", [128, 512], BF16, l3)
                    ug = [sb("ug%d" % i, [128, 512], F32, l3) for i in range(2)]

                    add("pool", lambda e: e.memset(vtok[:, :, 768:832], 0.0), writes=["vtok_z"])
                    add("pool", lambda e: e.dma_start(out=wvb[:], in_=w_in_v[:, :, 3072:3840]), writes=["wvb"], dma="wvb")
                    add("pool", lambda e: e.dma_start(out=wub[:], in_=w_in_v[:, :, 2304:3200]), writes=["wub"], dma="wub")
                    add("sp", lambda e: e.dma_start(out=lnvg[:], in_=dram["ln_v_g"].partition_broadcast(128)), writes=["lnvg"], dma="c_lnvg")
                    add("sp", lambda e: e.dma_start(out=lnvb[:], in_=dram["ln_v_b"].partition_broadcast(128)), writes=["lnvb"], dma="c_lnvb")
                    add("sp", lambda e: e.dma_start(out=wsn[:], in_=dram["w_spatial"].rearrange("g i j -> i g j")), writes=["wsn"], dma="c_wsn")
                    add("pool", lambda e: e.affine_select(out=wsm[:], in_=wsn[:], pattern=[[0, 4], [-1, 128]], compare_op=ALU.is_ge,
                                                          fill=0.0, base=0, channel_multiplier=1), reads=["wsn"], writes=["wsm"])
                    for gi in range(4):
                        add("pe", lambda e, gi=gi: e.transpose(out=bankT[:, gi * 128:(gi + 1) * 128], in_=wsm[:, gi, :], identity=ident[:]),
                            reads=["wsm", "ident"], writes=["bankT"])
                    copy_op("dve", wsT[:], bankT[:, 0:512].rearrange("p (g i) -> p g i", g=4), [], ["wsT", "bankT"])
                    add("pool", lambda e: e.memset(bs2[:], 0.0), writes=["bs2"])
                    bsrc = dram["b_spatial"].rearrange("(o g) i -> o (g i)", o=1)
                    add("sp", lambda e: e.dma_start(out=bs2[0:1, :], in_=bsrc), reads=["bs2"], writes=["bs2a"], dma="c_bs2a")
                    add("sp", lambda e: e.dma_start(out=bs2[1:2, :], in_=bsrc), reads=["bs2"], writes=["bs2b"], dma="c_bs2b")
                    add("dve", lambda e: e.tensor_copy(out=bshi[:], in_=bs2[:]), reads=["bs2", "bs2a", "bs2b"], writes=["bshi"])
                    add("dve", lambda e: e.tensor_copy(out=bshf[:], in_=bshi[:]), reads=["bshi"], writes=["bshf"])
                    add("dve", lambda e: e.tensor_tensor(out=bs2[:], in0=bs2[:], in1=bshf[:], op=ALU.subtract),
                        reads=["bshf", "bs2a", "bs2b"], writes=["bs2", "bs2a", "bs2b"])
                    add("dve", lambda e: e.tensor_scalar(out=bshf[:], in0=bshf[:], scalar1=identf[:, 0:1], scalar2=None, op0=ALU.mult),
                        reads=["bshf", "identf"], writes=["bshf"])
                    add("dve", lambda e: e.scalar_tensor_tensor(out=bsrep[:], in0=bs2[:], scalar=identf[:, 1:2], in1=bshf[:], op0=ALU.mult, op1=ALU.add),
                        reads=["bshf", "bs2", "identf"], writes=["bsrep"])

                    rotV = Rot([0, 1, 2, 3, 4, 5])

                    def vt_s1(t):
                        b0, b1 = rotV.next(), rotV.next()
                        for (b, c0, cn) in ((b0, 0, 512), (b1, 512, 256)):
                            for k in range(8):
                                add("pe", lambda e, b=b, k=k, t=t, c0=c0, cn=cn: e.matmul(
                                    banks[b][:, 0:cn], lhsT=xT[:, k, t * 128:(t + 1) * 128], rhs=wvb[:, k, c0:c0 + cn],
                                    start=(k == 0), stop=(k == 7)), reads=["wvb", "xT:%d" % t], writes=["bank%d" % b])
                        i3 = t % 3
                        add("act", lambda e, i3=i3, b0=b0: e.activation(out=gtmp3[i3][:, 0:512], in_=banks[b0][:], func=AF.Gelu_apprx_tanh),
                            writes=["gtmp%d" % i3, "bank%d" % b0])
                        add("act", lambda e, i3=i3, b1=b1: e.activation(out=gtmp3[i3][:, 512:768], in_=banks[b1][:, 0:256], func=AF.Gelu_apprx_tanh),
                            reads=["gtmp%d" % i3], writes=["gtmp%d" % i3, "bank%d" % b1])
                        for ci in range(2):
                            add("dve", lambda e, i3=i3, ci=ci: e.bn_stats(out=stats3[i3][:, ci, :], in_=gtmp3[i3][:, ci * 384:(ci + 1) * 384]),
                                reads=["gtmp%d" % i3], writes=["stats3_%d:%d" % (i3, ci)])
                        add("dve", lambda e, i3=i3: e.bn_aggr(out=mv3[i3][:, 0:2], in_=stats3[i3][:].rearrange("p a b -> p (a b)")),
                            reads=["stats3_%d:0" % i3, "stats3_%d:1" % i3], writes=["mv3_%d" % i3])
                        add("dve", lambda e, i3=i3: e.tensor_scalar(out=mv3[i3][:, 2:3], in0=mv3[i3][:, 1:2], scalar1=EPS, scalar2=None, op0=ALU.add),
                            reads=["mv3_%d" % i3], writes=["mv3_%d" % i3])
                        add("pool", lambda e, i3=i3: e.tensor_tensor(out=mv3[i3][:, 2:3], in0=mv3[i3][:, 2:3], in1=mhalf[:, 0:1], op=ALU.pow),
                            reads=["mv3_%d" % i3, "mhalf"], writes=["mv3_%d" % i3])
                        add("dve", lambda e, i3=i3: e.scalar_tensor_tensor(out=mv3[i3][:, 3:4], in0=mv3[i3][:, 0:1], scalar=-1.0, in1=mv3[i3][:, 2:3], op0=ALU.mult, op1=ALU.mult),
                            reads=["mv3_%d" % i3], writes=["mv3_%d" % i3])

                    def vt_s2(t):
                        i3 = t % 3
                        i2 = t % 2
                        add("act", lambda e, i2=i2, i3=i3: e.activation(out=ntmp[i2][:], in_=gtmp3[i3][:], func=AF.Identity, scale=mv3[i3][:, 2:3], bias=mv3[i3][:, 3:4]),
                            reads=["gtmp%d" % i3, "mv3_%d" % i3], writes=["ntmp%d" % i2])
                        add("dve", lambda e, i2=i2: e.tensor_tensor(out=ntmp[i2][:], in0=ntmp[i2][:], in1=lnvg[:], op=ALU.mult),
                            reads=["lnvg"], writes=["ntmp%d" % i2])
                        add("pool", lambda e, i2=i2, t=t: e.tensor_tensor(out=vtok[:, t, 0:768], in0=ntmp[i2][:], in1=lnvb[:], op=ALU.add),
                            reads=["ntmp%d" % i2, "lnvb"], writes=["vtok:%d" % t])
                    vt_s1(0)
                    vt_s1(1)
                    for t in range(NT):
                        if t + 2 < NT:
                            vt_s1(t + 2)
                        vt_s2(t)
                    S_.stop_at("vtok")
                    xdTp2 = sb("xdTp2", [128, 8, 128], BF16, l3)
                    xTH = sb("xTH", [128, 8, 128], BF16, l3)
                    xHb = sb("xHb", [128, D], BF16, l3)
                    dv = sb("dv", [128, 832], F32, l3)
                    vtokH = sb("vtokH", [128, 832], BF16, l3)
                    du = sb("du", [128, 8, 8], F32, l3)
                    dvT = sb("dvT", [128, 8, 8], F32, l3)
                    dmx = sb("dmx", [128, 8, 8], F32, l3)
                    ws00 = sb("ws00", [128, 4], F32, l3)
                    bs00 = sb("bs00", [128, 4], F32, l3)
                    add("pool", lambda e: e.memset(xdTp2[:], 0.0), writes=["xdTp2"])
                    add("pool", lambda e: e.tensor_copy(out=xdTp2[:, :, 0:8], in_=xT[:, :, T:TW]), reads=["xdTp2"], writes=["xdTp2"])
                    add("pool", lambda e: e.memset(dv[:, 768:832], 0.0), writes=["dv_z"])
                    add("pool", lambda e: e.memset(vtokH[:, 768:832], 0.0), writes=["vtokH_z"])
                    add("pool", lambda e: e.memset(dmx[:], 0.0), writes=["dmx"])
                    add("pool", lambda e: e.dma_start(out=xHb[:], in_=xh[T - 128:T, :]), writes=["xHb"], dma="xHb")
                    wsv = dram["w_spatial"]
                    bsv = dram["b_spatial"]
                    add("sp", lambda e: e.dma_start(out=ws00[:], in_=bass.AP(wsv.tensor, 0, [[0, 128], [128 * 128, 4]]), allow_slow_non_contiguous=True),
                        writes=["ws00"], dma="c_ws00")
                    add("sp", lambda e: e.dma_start(out=bs00[:], in_=bass.AP(bsv.tensor, 0, [[0, 128], [128, 4]]), allow_slow_non_contiguous=True),
                        writes=["bs00"], dma="c_bs00")

                    def vb_ln(lhs, lkeys, out_ap, okeys, i2):
                        b0, b1 = rotV.next(), rotV.next()
                        for (b, c0, cn) in ((b0, 0, 512), (b1, 512, 256)):
                            for k in range(8):
                                add("pe", lambda e, b=b, k=k, c0=c0, cn=cn: e.matmul(
                                    banks[b][:, 0:cn], lhsT=lhs[:, k, :], rhs=wvb[:, k, c0:c0 + cn],
                                    start=(k == 0), stop=(k == 7)), reads=["wvb"] + lkeys, writes=["bank%d" % b])
                        add("act", lambda e: e.activation(out=gtmp[i2][:, 0:512], in_=banks[b0][:], func=AF.Gelu_apprx_tanh),
                            writes=["gtmp%d" % i2, "bank%d" % b0])
                        add("act", lambda e: e.activation(out=gtmp[i2][:, 512:768], in_=banks[b1][:, 0:256], func=AF.Gelu_apprx_tanh),
                            reads=["gtmp%d" % i2], writes=["gtmp%d" % i2, "bank%d" % b1])
                        for ci in range(2):
                            add("dve", lambda e, ci=ci: e.bn_stats(out=stats[i2][:, ci, :], in_=gtmp[i2][:, ci * 384:(ci + 1) * 384]),
                                reads=["gtmp%d" % i2], writes=["stats%d:%d" % (i2, ci)])
                        add("dve", lambda e: e.bn_aggr(out=mv_[i2][:, 0:2], in_=stats[i2][:].rearrange("p a b -> p (a b)")),
                            reads=["stats%d:0" % i2, "stats%d:1" % i2], writes=["mv%d" % i2])
                        add("dve", lambda e: e.tensor_scalar(out=mv_[i2][:, 2:3], in0=mv_[i2][:, 1:2], scalar1=EPS, scalar2=None, op0=ALU.add),
                            reads=["mv%d" % i2], writes=["mv%d" % i2])
                        add("pool", lambda e: e.tensor_tensor(out=mv_[i2][:, 2:3], in0=mv_[i2][:, 2:3], in1=mhalf[:, 0:1], op=ALU.pow),
                            reads=["mv%d" % i2, "mhalf"], writes=["mv%d" % i2])
                        add("dve", lambda e: e.scalar_tensor_tensor(out=mv_[i2][:, 3:4], in0=mv_[i2][:, 0:1], scalar=-1.0, in1=mv_[i2][:, 2:3], op0=ALU.mult, op1=ALU.mult),
                            reads=["mv%d" % i2], writes=["mv%d" % i2])
                        add("act", lambda e: e.activation(out=ntmp[i2][:], in_=gtmp[i2][:], func=AF.Identity, scale=mv_[i2][:, 2:3], bias=mv_[i2][:, 3:4]),
                            reads=["gtmp%d" % i2, "mv%d" % i2], writes=["ntmp%d" % i2])
                        add("dve", lambda e: e.tensor_tensor(out=gtmp[i2][:], in0=ntmp[i2][:], in1=lnvg[:], op=ALU.mult),
                            reads=["ntmp%d" % i2, "lnvg"], writes=["gtmp%d" % i2])
                        add("pool", lambda e: e.tensor_tensor(out=out_ap, in0=gtmp[i2][:], in1=lnvb[:], op=ALU.add),
                            reads=["gtmp%d" % i2, "lnvb"], writes=okeys)
                    def gdec_p1():
                        for k in range(8):
                            add("pe", lambda e, k=k: e.transpose(out=bankT[:, k * 128:(k + 1) * 128], in_=xHb[:, k * 128:(k + 1) * 128], identity=ident[:]),
                                reads=["xHb", "ident"], writes=["bankT"])
                        copy_op("dve", xTH[:], bankT[:].rearrange("p (k c) -> p k c", k=8), [], ["xTH", "bankT"])
                    def gdec_p2():
                        vb_ln(xdTp2, ["xdTp2"], dv[:, 0:768], ["dv"], 0)
                    def gdec_p3():
                        vb_ln(xTH, ["xTH"], vtokH[:, 0:768], ["vtokH"], 1)
                        add("sp", lambda e: e.dma_start(out=dram["gmlp_v_s"], in_=dv[0:4, 0:768]), reads=["dv"], writes=["o_gv"], dma="o_gv")
                    def gdec_p4():
                        for cu in range(8):
                            for k in range(8):
                                add("pe", lambda e, cu=cu, k=k: e.matmul(
                                    banks[4][:, cu * 8:(cu + 1) * 8], lhsT=wub[:, k, UST[cu]:UST[cu] + 128], rhs=xT[:, k, T:TW],
                                    start=(k == 0), stop=(k == 7)), reads=["wub", "xTd"], writes=["bank4"])
                        add("act", lambda e: e.activation(out=du[:].rearrange("p a n -> p (a n)"), in_=banks[4][:, 0:64], func=AF.Gelu_apprx_tanh),
                            writes=["du", "bank4"])
                    def gdec_p5():
                        for hb in range(2):
                            for q4 in range(4):
                                cu = hb * 4 + q4
                                add("pe", lambda e, hb=hb, q4=q4, cu=cu: e.transpose(
                                    out=banks[5][:, q4 * 128:(q4 + 1) * 128], in_=dv[:, UST[cu]:UST[cu] + 128], identity=identf[:]),
                                    reads=["dv", "dv_z", "identf"], writes=["bank5"])
                            add("dve", lambda e, hb=hb: e.tensor_copy(out=dvT[:, hb * 4:(hb + 1) * 4, :], in_=banks[5][:].rearrange("p (a t) -> p a t", a=4)[:, :, 0:8]),
                                writes=["dvT:%d" % hb, "bank5"])
                    def gdec_p6():
                        for cu in range(8):
                            add("dve", lambda e, cu=cu: e.tensor_scalar(out=dmx[:, cu, 0:4], in0=dvT[:, cu, 0:4], scalar1=ws00[:, cu // 2:cu // 2 + 1],
                                                                       scalar2=bs00[:, cu // 2:cu // 2 + 1], op0=ALU.mult, op1=ALU.add),
                                reads=["dvT:0", "dvT:1", "ws00", "bs00", "dmx"], writes=["dmx"])
                        for cu in range(8):
                            gi = cu // 2
                            add("pe", lambda e, cu=cu, gi=gi: e.matmul(banks[6][:, cu * 2:cu * 2 + 2], lhsT=onesb[:], rhs=bsrep[:, gi * 128 + 126:gi * 128 + 128], start=True, stop=False),
                                reads=["onesb", "bsrep"], writes=["bank6"])
                            add("pe", lambda e, cu=cu, gi=gi: e.matmul(banks[6][:, cu * 2:cu * 2 + 2], lhsT=vtokH[:, UST[cu]:UST[cu] + 128], rhs=wsT[:, gi, 126:128], start=False, stop=True),
                                reads=["vtokH", "vtokH_z", "wsT"], writes=["bank6"])
                        add("dve", lambda e: e.tensor_copy(out=dmx[:, :, 4:6], in_=banks[6][:, 0:16].rearrange("p (a n) -> p a n", a=8)), reads=["dmx"], writes=["dmx", "bank6"])
                        add("dve", lambda e: e.tensor_tensor(out=o_bT[:, :, T:TW], in0=du[:], in1=dmx[:], op=ALU.mult), reads=["du", "dmx"], writes=["o_bT:d"])
                        if dbg_d is not None:
                            add("dve", lambda e: e.tensor_copy(out=du[:], in_=o_bT[:, :, T:TW]), reads=["o_bT:d"], writes=["du"])
                            add("sp", lambda e: e.dma_start(out=dbg_d[:, 0:64], in_=du[:].rearrange("p a n -> p (a n)")), reads=["du"], writes=["dbg_d"], dma="dbgd")
                    gdec_sched = {2: gdec_p1, 5: gdec_p2, 9: gdec_p3, 13: gdec_p4, 17: gdec_p5, 22: gdec_p6}
                    rotU = Rot([0, 1])
                    rotM = Rot([2, 3])
                    ui = 0
                    for cu in range(8):
                        st0 = UST[cu]
                        gi = cu // 2
                        for tt in range(4):
                            bu, bm = rotU.next(), rotM.next()
                            for k in range(8):
                                add("pe", lambda e, bu=bu, k=k, tt=tt, st0=st0: e.matmul(
                                    banks[bu][:], lhsT=wub[:, k, st0:st0 + 128], rhs=xT[:, k, tt * 512:(tt + 1) * 512],
                                    start=(k == 0), stop=(k == 7)), reads=["wub"] + ["xT:%d" % t for t in range(tt * 4, tt * 4 + 4)], writes=["bank%d" % bu])
                            for n4 in range(4):
                                n = tt * 4 + n4
                                add("pe", lambda e, bm=bm, n4=n4, gi=gi: e.matmul(
                                    banks[bm][:, n4 * 128:(n4 + 1) * 128], lhsT=onesb[:], rhs=bsrep[:, gi * 128:(gi + 1) * 128], start=True, stop=False),
                                    reads=["onesb", "bsrep"], writes=["bank%d" % bm])
                                add("pe", lambda e, bm=bm, n4=n4, gi=gi, n=n, st0=st0: e.matmul(
                                    banks[bm][:, n4 * 128:(n4 + 1) * 128], lhsT=vtok[:, n, st0:st0 + 128], rhs=wsT[:, gi, :], start=False, stop=True),
                                    reads=["vtok:%d" % n, "vtok_z", "wsT"], writes=["bank%d" % bm])
                            u2 = ui % 2
                            ui += 1
                            add("act", lambda e, u2=u2, bu=bu: e.activation(out=ug[u2][:], in_=banks[bu][:], func=AF.Gelu_apprx_tanh),
                                writes=["ug%d" % u2, "bank%d" % bu])
                            add("dve", lambda e, u2=u2, bm=bm, cu=cu, tt=tt: e.tensor_tensor(
                                out=o_bT[:, cu, tt * 512:(tt + 1) * 512], in0=ug[u2][:], in1=banks[bm][:], op=ALU.mult),
                                reads=["ug%d" % u2], writes=["o_bT:%d:%d" % (cu, tt), "bank%d" % bm])
                            if (ui - 1) in gdec_sched:
                                gdec_sched[ui - 1]()
                    S_.flush(barrier=True)
                S_.stop_at("gmlp")
                o_mT = sb("o_mT", [128, 4, TW], BF16, l2b)
                with ExitStack() as l3:
                    memb = sb("memb", [128, 2, D], BF16, l3)
                    memT = sb("memT", [128, 8, 256], BF16, l3)
                    wmem = sb("wmem", [128, 8, D], BF16, l3)
                    wqm = sb("wqm", [128, 8, 512], BF16, l3)
                    w_mem_v = dram["w_mem_kv"].rearrange("(k p) c -> p k c", p=128)
                    for hf in range(2):
                        add("pool", lambda e, hf=hf: e.dma_start(out=wmem[:, :, hf * 512:(hf + 1) * 512], in_=w_mem_v[:, :, hf * 512:(hf + 1) * 512]),
                            writes=["wmem:%d" % hf], dma="wmem%d" % hf)
                    add("pool", lambda e: e.dma_start(out=wqm[:], in_=w_in_v[:, :, 3840:4352]), writes=["wqm"], dma="wqm")
                    mkT = sb("mkT", [128, 4, 256], BF16, l3)
                    mvv = sb("mvv", [128, 2, 512], BF16, l3)
                    mst = [sb("mst%d" % i, [128, 512], F32, l3) for i in range(2)]
                    qmT = sb("qmT", [128, 4, T], BF16, l3)
                    PTm = [sb("PTm%d" % i, [128, 2, 512], BF16, l3) for i in range(2)]
                    rec = [sb("rec%d" % i, [128, 512], F32, l3) for i in range(2)]
                    add("pool", lambda e: e.dma_start(out=memb[:], in_=dram["mem"].rearrange("(t p) c -> p t c", p=128)), writes=["memb"], dma="memb")
                    for mt in range(2):
                        for k in range(8):
                            add("pe", lambda e, mt=mt, k=k: e.transpose(out=bankT[:, k * 128:(k + 1) * 128], in_=memb[:, mt, k * 128:(k + 1) * 128], identity=ident[:]),
                                reads=["memb", "ident"], writes=["bankT"])
                        copy_op("dve", memT[:, :, mt * 128:(mt + 1) * 128], bankT[:].rearrange("p (k c) -> p k c", k=8), [], ["memT:%d" % mt, "bankT"])
                    rotA = Rot([0, 1, 2, 3])
                    mi = 0
                    for mt in range(2):
                        for hf in range(2):
                            b = rotA.next()
                            for k in range(8):
                                add("pe", lambda e, b=b, k=k, mt=mt, hf=hf: e.matmul(
                                    banks[b][:], lhsT=memT[:, k, mt * 128:(mt + 1) * 128], rhs=wmem[:, k, hf * 512:(hf + 1) * 512],
                                    start=(k == 0), stop=(k == 7)), reads=["memT:%d" % mt, "wmem:%d" % hf], writes=["bank%d" % b])
                            m2 = mi % 2
                            mi += 1
                            add("act", lambda e, m2=m2, b=b: e.activation(out=mst[m2][:], in_=banks[b][:], func=AF.Identity),
                                writes=["mst%d" % m2, "bank%d" % b])
                            add("sp", lambda e, m2=m2, mt=mt, hf=hf: e.dma_start(
                                out=dram["new_mem_kv"][mt * 128:(mt + 1) * 128, hf * 512:(hf + 1) * 512], in_=mst[m2][:]),
                                reads=["mst%d" % m2], writes=["o_mst%d" % m2], dma="mst%d" % m2)
                            if hf == 1:
                                add("dve", lambda e, m2=m2, mt=mt: e.tensor_copy(out=mvv[:, mt, :], in_=mst[m2][:]),
                                    reads=["mst%d" % m2], writes=["mvv:%d" % mt])
                    for h in range(4):
                        b = rotA.next()
                        for k in range(8):
                            add("pe", lambda e, b=b, k=k, h=h: e.matmul(
                                banks[b][:, 0:256], lhsT=wmem[:, k, h * 128:(h + 1) * 128], rhs=memT[:, k, :],
                                start=(k == 0), stop=(k == 7)), reads=["memT:0", "memT:1", "wmem:0"], writes=["bank%d" % b])
                        copy_op(evac_eng(), mkT[:, h, :], banks[b][:, 0:256], [], ["mkT:%d" % h, "bank%d" % b])
                    for h in range(4):
                        for tt in range(4):
                            b = rotA.next()
                            for k in range(8):
                                add("pe", lambda e, b=b, k=k, h=h, tt=tt: e.matmul(
                                    banks[b][:], lhsT=wqm[:, k, h * 128:(h + 1) * 128], rhs=xT[:, k, tt * 512:(tt + 1) * 512],
                                    start=(k == 0), stop=(k == 7)), reads=["wqm"] + ["xT:%d" % t for t in range(tt * 4, tt * 4 + 4)], writes=["bank%d" % b])
                            copy_op(evac_eng(), qmT[:, h, tt * 512:(tt + 1) * 512], banks[b][:], [], ["qmT:%d:%d" % (h, tt), "bank%d" % b])
                    its = [(h, tt) for h in range(4) for tt in range(4)]
                    sbanks = [(0, 1), (2, 3)]
                    obanks = [4, 5]

                    def emit_S(i):
                        h, tt = its[i]
                        p2 = i % 2
                        for mt in range(2):
                            b = sbanks[p2][mt]
                            add("pe", lambda e, b=b, h=h, tt=tt, mt=mt: e.matmul(
                                banks[b][:], lhsT=mkT[:, h, mt * 128:(mt + 1) * 128], rhs=qmT[:, h, tt * 512:(tt + 1) * 512], start=True, stop=True),
                                reads=["mkT:%d" % h, "qmT:%d:%d" % (h, tt)], writes=["bank%d" % b])
                            add("act", lambda e, b=b, p2=p2, mt=mt: e.activation(out=PTm[p2][:, mt, :], in_=banks[b][:], func=AF.Exp, scale=float(128 ** -0.5)),
                                writes=["PTm%d:%d" % (p2, mt), "bank%d" % b])

                    def emit_O(i):
                        h, tt = its[i]
                        p2 = i % 2
                        ob = obanks[p2]
                        for mt in range(2):
                            add("pe", lambda e, h=h, mt=mt, p2=p2: e.matmul(
                                banks[6][:], lhsT=onesb[:], rhs=PTm[p2][:, mt, :], start=(mt == 0), stop=(mt == 1)),
                                reads=["onesb", "PTm%d:%d" % (p2, mt)], writes=["bank6"])
                        for mt in range(2):
                            add("pe", lambda e, ob=ob, h=h, mt=mt, p2=p2: e.matmul(
                                banks[ob][:], lhsT=mvv[:, mt, h * 128:(h + 1) * 128], rhs=PTm[p2][:, mt, :], start=(mt == 0), stop=(mt == 1)),
                                reads=["mvv:%d" % mt, "PTm%d:%d" % (p2, mt)], writes=["bank%d" % ob])
                        add("act", lambda e, p2=p2: e.activation(out=rec[p2][:], in_=banks[6][:], func=AF.Ln), writes=["rec%d" % p2, "bank6"])
                        add("act", lambda e, p2=p2: e.activation(out=rec[p2][:], in_=rec[p2][:], func=AF.Exp, scale=-1.0), reads=["rec%d" % p2], writes=["rec%d" % p2])
                        add("dve", lambda e, p2=p2, ob=ob, h=h, tt=tt: e.tensor_tensor(
                            out=o_mT[:, h, tt * 512:(tt + 1) * 512], in0=rec[p2][:], in1=banks[ob][:], op=ALU.mult),
                            reads=["rec%d" % p2], writes=["o_mT:%d:%d" % (h, tt), "bank%d" % ob])
                    dqm = sb("dqm", [128, 4, 8], BF16, l3)
                    cmk = sb("cmk", [128, 2, 4, D], BF16, l3)
                    cmkT = sb("cmkT", [128, 32, 128], BF16, l3)
                    dPm = sb("dPm", [128, 64], BF16, l3)
                    dden = sb("dden", [128, 4, 8], F32, l3)
                    cmkTk = ["cmkT:%d" % i8 for i8 in range(0, 32, 8)]
                    for mt_ in range(2):
                        add("pool", lambda e, mt_=mt_: e.dma_start(out=cmk[:, mt_], in_=dram["cmem"][:, mt_ * 128:(mt_ + 1) * 128, :].rearrange("n p x -> p n x")),
                            writes=["cmk:%d" % mt_], dma="cmk%d" % mt_)
                    add("pool", lambda e: e.memset(o_mT[:, :, T:TW], 0.0), writes=["o_mT:dz"])
                    def mdec_p1():
                        for h in range(4):
                            for k in range(8):
                                add("pe", lambda e, h=h, k=k: e.matmul(banks[4][:, h * 8:(h + 1) * 8], lhsT=wqm[:, k, h * 128:(h + 1) * 128], rhs=xT[:, k, T:TW],
                                                                      start=(k == 0), stop=(k == 7)), reads=["wqm", "xTd"], writes=["bank4"])
                        copy_op("dve", dqm[:], banks[4][:, 0:32].rearrange("p (h n) -> p h n", h=4), [], ["dqm", "bank4"])
                    def mdec_p2(i8):
                        for ii in range(8):
                            idx = i8 + ii
                            n_, h_, mt_ = idx // 8, (idx // 2) % 4, idx % 2
                            add("pe", lambda e, ii=ii, n_=n_, h_=h_, mt_=mt_: e.transpose(
                                out=bankT[:, ii * 128:(ii + 1) * 128], in_=cmk[:, mt_, n_, h_ * 128:(h_ + 1) * 128], identity=ident[:]),
                                reads=["cmk:0", "cmk:1", "ident"], writes=["bankT"])
                        copy_op(evac_eng(), cmkT[:, i8:i8 + 8, :], bankT[:].rearrange("p (k c) -> p k c", k=8), [], ["cmkT:%d" % i8, "bankT"])
                    def mdec_p3():
                        for n_ in range(6):
                            for h_ in range(4):
                                for mt_ in range(2):
                                    col = (n_ * 4 + h_) * 2 + mt_
                                    lhs = cmkT[:, col, :] if n_ < 4 else mkT[:, h_, mt_ * 128:(mt_ + 1) * 128]
                                    add("pe", lambda e, col=col, lhs=lhs, h_=h_, n_=n_: e.matmul(
                                        banks[5][:, col:col + 1], lhsT=lhs, rhs=dqm[:, h_, n_:n_ + 1], start=True, stop=True),
                                        reads=cmkTk + ["dqm"] + ["mkT:%d" % h for h in range(4)], writes=["bank5"])
                        add("act", lambda e: e.activation(out=dPm[:, 0:48], in_=banks[5][:, 0:48], func=AF.Exp, scale=float(128 ** -0.5)), writes=["dPm", "bank5"])
                    def mdec_p4():
                        for n_ in range(6):
                            for h_ in range(4):
                                for (base, which) in ((0, "o"), (64, "d")):
                                    for mt_ in range(2):
                                        scol = (n_ * 4 + h_) * 2 + mt_
                                        if which == "d":
                                            lhs = onesb[:]
                                        elif n_ < 4:
                                            lhs = cmk[:, mt_, n_, 512 + h_ * 128:512 + (h_ + 1) * 128]
                                        else:
                                            lhs = mvv[:, mt_, h_ * 128:(h_ + 1) * 128]
                                        add("pe", lambda e, base=base, h_=h_, n_=n_, mt_=mt_, scol=scol, lhs=lhs: e.matmul(
                                            banks[6][:, base + h_ * 8 + n_:base + h_ * 8 + n_ + 1], lhsT=lhs, rhs=dPm[:, scol:scol + 1],
                                            start=(mt_ == 0), stop=(mt_ == 1)), reads=["dPm", "cmk:0", "cmk:1", "onesb", "mvv:0", "mvv:1"], writes=["bank6"])
                        add("dve", lambda e: e.reciprocal(out=dden[:, :, 0:6], in_=banks[6][:, 64:96].rearrange("p (h n) -> p h n", h=4)[:, :, 0:6]), writes=["dden", "bank6"])
                        add("dve", lambda e: e.tensor_tensor(out=o_mT[:, :, T:T + 6], in0=dden[:, :, 0:6], in1=banks[6][:, 0:32].rearrange("p (h n) -> p h n", h=4)[:, :, 0:6], op=ALU.mult),
                            reads=["dden", "o_mT:dz"], writes=["o_mT:d", "bank6"])
                        if dbg_d is not None:
                            add("dve", lambda e: e.tensor_copy(out=dden[:], in_=o_mT[:, :, T:TW]), reads=["o_mT:d"], writes=["dden"])
                            add("sp", lambda e: e.dma_start(out=dbg_d[:, 0:32], in_=dden[:].rearrange("p a n -> p (a n)")), reads=["dden"], writes=["dbg_d"], dma="dbgd")

                    mdec_sched = {1: mdec_p1, 3: (lambda: mdec_p2(0)), 5: (lambda: mdec_p2(8)), 7: (lambda: mdec_p2(16)), 9: (lambda: mdec_p2(24)), 11: mdec_p3, 13: mdec_p4}
                    emit_S(0)
                    for i in range(len(its)):
                        if i + 1 < len(its):
                            emit_S(i + 1)
                        emit_O(i)
                        if i in mdec_sched:
                            mdec_sched[i]()
                    S_.flush(barrier=True)
                S_.stop_at("mem")
                mixedT = sb("mixedT", [128, 8, TW], BF16, l2b)
                w_out_v = dram["w_out"].rearrange("(k p) c -> p k c", p=128)

                def prefetch_wout():
                    for hf in range(2):
                        add("pool", lambda e, hf=hf: e.dma_start(out=wout[:, :, hf * 512:(hf + 1) * 512], in_=w_out_v[:, :, hf * 512:(hf + 1) * 512]),
                            writes=["wout:%d" % hf], dma="wout%d" % hf)
                    add("sp", lambda e: e.dma_start(out=ln1g[:], in_=dram["ln1_g"].partition_broadcast(128)), writes=["ln1g"], dma="c_ln1g")
                    add("sp", lambda e: e.dma_start(out=ln1b[:], in_=dram["ln1_b"].partition_broadcast(128)), writes=["ln1b"], dma="c_ln1b")
                with ExitStack() as l3:
                    wg = [sb("wg%d" % i, [128, 8, 3, 128], BF16, l3) for i in range(2)]
                    wba = sb("wba", [128, 2, D], BF16, l3)
                    wbb = sb("wbb", [128, 8, D], BF16, l3)
                    wbm = sb("wbm", [128, 4, D], BF16, l3)
                    bgn = sb("bgn", [24, 128], F32, l3)
                    bg = sb("bg", [128, 24], F32, l3)
                    gt = [[sb("gt%d_%d" % (i, j), [128, 512], F32, l3) for j in range(3)] for i in range(2)]
                    tm = [[sb("tm%d_%d" % (i, j), [128, 512], F32, l3) for j in range(3)] for i in range(2)]
                    add("pool", lambda e: e.dma_start(out=wba[:], in_=dram["w_branch_a"].rearrange("(k p) c -> p k c", p=128)), writes=["wba"], dma="wba")
                    add("pool", lambda e: e.dma_start(out=wbm[:], in_=dram["w_branch_m"].rearrange("(k p) c -> p k c", p=128)), writes=["wbm"], dma="wbm")
                    add("pool", lambda e: e.memset(wbb[:], 0.0), writes=["wbb_z"])
                    for cu in range(8):
                        rows = 128 if cu % 2 == 0 else 64
                        add("pool", lambda e, cu=cu, rows=rows: e.dma_start(out=wbb[0:rows, cu, :], in_=dram["w_branch_b"][UST[cu]:UST[cu] + rows, :]),
                            reads=["wbb_z"], writes=["wbb:%d" % cu], dma="wbb%d" % cu)
                    add("sp", lambda e: e.dma_start(out=bgn[:], in_=dram["b_gate"].rearrange("b (f p) -> (b f) p", p=128)), writes=["bgn"], dma="c_bgn")
                    add("pe", lambda e: e.transpose(out=banks[6][:, 0:24], in_=bgn[:], identity=identf[0:24, 0:24]), reads=["bgn", "identf"], writes=["bank6"])
                    copy_op("dve", bg[:], banks[6][:, 0:24], [], ["bg", "bank6"])
                    wbkeys = ["wba", "wbm", "wbb_z"] + ["wbb:%d" % cu for cu in range(8)]
                    obk = ["o_bT:%d:%d" % (cu, tt) for cu in range(8) for tt in range(4)]
                    omk = ["o_mT:%d:%d" % (h, tt) for h in range(4) for tt in range(4)]
                    oak = ["o_aT:%d:%d" % (c, tt) for c in range(2) for tt in range(4)]

                    def load_wg(f):
                        s2 = f % 2
                        for bi in range(3):
                            c0 = 4352 + bi * 1024 + f * 128
                            add("pool", lambda e, s2=s2, bi=bi, c0=c0: e.dma_start(out=wg[s2][:, :, bi, :], in_=w_in_v[:, :, c0:c0 + 128]),
                                writes=["wg%d:%d" % (s2, bi)], dma="wg%d_%d" % (s2, bi))
                    load_wg(0)
                    it = 0
                    for f in range(8):
                        if f + 1 < 8:
                            load_wg(f + 1)
                        s2 = f % 2
                        for tt in range(5):
                            i2 = it % 2
                            it += 1
                            tsl = slice(tt * 512, (tt + 1) * 512) if tt < 4 else slice(T, TW)
                            nn = 512 if tt < 4 else 8
                            xk = ["xT:%d" % t for t in range(tt * 4, tt * 4 + 4)] if tt < 4 else ["xTd"]
                            for bi in range(3):
                                for k in range(8):
                                    add("pe", lambda e, bi=bi, k=k, s2=s2, tsl=tsl, nn=nn: e.matmul(
                                        banks[bi][:, 0:nn], lhsT=wg[s2][:, k, bi, :], rhs=xT[:, k, tsl], start=(k == 0), stop=(k == 7)),
                                        reads=["wg%d:%d" % (s2, bi)] + xk, writes=["bank%d" % bi])
                                add("act", lambda e, bi=bi, i2=i2, f=f, nn=nn: e.activation(
                                    out=gt[i2][bi][:, 0:nn], in_=banks[bi][:, 0:nn], func=AF.Sigmoid, bias=bg[:, bi * 8 + f:bi * 8 + f + 1]),
                                    reads=["bg"], writes=["gt%d_%d" % (i2, bi), "bank%d" % bi])
                            fs = slice(f * 128, (f + 1) * 128)
                            for (bi, wt, nk, src, keys) in ((0, wba, 2, o_aT, oak), (1, wbb, 8, o_bT, obk), (2, wbm, 4, o_mT, omk)):
                                for k in range(nk):
                                    add("pe", lambda e, bi=bi, k=k, wt=wt, src=src, nk=nk, fs=fs, tsl=tsl, nn=nn: e.matmul(
                                        banks[3 + bi][:, 0:nn], lhsT=wt[:, k, fs], rhs=src[:, k, tsl], start=(k == 0), stop=(k == nk - 1)),
                                        reads=wbkeys + keys, writes=["bank%d" % (3 + bi)])
                                add("dve", lambda e, bi=bi, i2=i2, nn=nn: e.tensor_tensor(out=tm[i2][bi][:, 0:nn], in0=gt[i2][bi][:, 0:nn], in1=banks[3 + bi][:, 0:nn], op=ALU.mult),
                                    reads=["gt%d_%d" % (i2, bi)], writes=["tm%d_%d" % (i2, bi), "bank%d" % (3 + bi)])
                            add("pool", lambda e, i2=i2, nn=nn: e.tensor_tensor(out=tm[i2][0][:, 0:nn], in0=tm[i2][0][:, 0:nn], in1=tm[i2][1][:, 0:nn], op=ALU.add),
                                reads=["tm%d_1" % i2], writes=["tm%d_0" % i2])
                            add("pool", lambda e, i2=i2, f=f, tsl=tsl, nn=nn: e.tensor_tensor(out=mixedT[:, f, tsl], in0=tm[i2][0][:, 0:nn], in1=tm[i2][2][:, 0:nn], op=ALU.add),
                                reads=["tm%d_0" % i2, "tm%d_2" % i2], writes=["mixedT:%d:%d" % (f, tt)])
                    S_.flush(barrier=True)
                S_.stop_at("merge")
                with ExitStack() as l3:
                    wout = sb("wout", [128, 8, D], BF16, l3)
                    ln1g = sb("ln1g", [128, D], F32, l3)
                    ln1b = sb("ln1b", [128, D], F32, l3)
                    prefetch_wout()
                    xres = [sb("xres%d" % i, [128, D], F32, l3) for i in range(3)]
                    zt = [sb("zt%d" % i, [128, D], F32, l3) for i in range(3)]
                    zn = [sb("zn%d" % i, [128, D], F32, l3) for i in range(3)]
                    x1f = [sb("x1f%d" % i, [128, D], F32, l3) for i in range(3)]
                    x1b = [sb("x1b%d" % i, [128, D], BF16, l3) for i in range(3)]
                    stats = [sb("stats1_%d" % i, [128, 2, 6], F32, l3) for i in range(3)]
                    mv_ = [sb("mv1_%d" % i, [128, 4], F32, l3) for i in range(3)]
                    rotZ = Rot([0, 1, 2, 3, 4, 5])
                    mixk = ["mixedT:%d:%d" % (f, tt) for f in range(8) for tt in range(5)]
                    mxdp = sb("mxdp", [128, 8, 128], BF16, l3)
                    add("pool", lambda e: e.memset(mxdp[:], 0.0), writes=["mxdp"])
                    add("pool", lambda e: e.tensor_copy(out=mxdp[:, :, 0:8], in_=mixedT[:, :, T:TW]), reads=["mxdp"] + ["mixedT:%d:4" % f for f in range(8)], writes=["mxdp"])
                    bzs = {}

                    def ln1_mm(t):
                        bz = [rotZ.next(), rotZ.next()]
                        bzs[t] = bz
                        i2 = t % 3
                        if t < NT:
                            add("sp", lambda e, i2=i2, t=t: e.dma_start(out=xres[i2][:], in_=xh[T + t * 128:T + (t + 1) * 128, :]), writes=["xres%d" % i2], dma="xres%d" % i2)
                        else:
                            add("pool", lambda e, i2=i2: e.memset(xres[i2][:], 0.0), writes=["xres%d" % i2])
                            add("sp", lambda e, i2=i2: e.dma_start(out=xres[i2][0:8, :], in_=dram["xd"]), reads=["xres%d" % i2], writes=["xresd"], dma="xresd")
                        for hf in range(2):
                            for k in range(8):
                                lhs = mixedT[:, k, t * 128:(t + 1) * 128] if t < NT else mxdp[:, k, :]
                                add("pe", lambda e, hf=hf, k=k, lhs=lhs, bz=bz: e.matmul(
                                    banks[bz[hf]][:], lhsT=lhs, rhs=wout[:, k, hf * 512:(hf + 1) * 512],
                                    start=(k == 0), stop=(k == 7)), reads=["mixedT:%d:%d" % (k, t // 4), "wout:%d" % hf, "mxdp"], writes=["bank%d" % bz[hf]])

                    def ln1_A1(t):
                        i2 = t % 3
                        bz = bzs[t]
                        for hf in range(2):
                            add("dve", lambda e, hf=hf, i2=i2, bz=bz: e.scalar_tensor_tensor(
                                out=zt[i2][:, hf * 512:(hf + 1) * 512], in0=xres[i2][:, hf * 512:(hf + 1) * 512], scalar=ALPHA, in1=banks[bz[hf]][:],
                                op0=ALU.mult, op1=ALU.add), reads=["xres%d" % i2, "xresd"], writes=["zt%d:%d" % (i2, hf), "bank%d" % bz[hf]])
                            add("dve", lambda e, hf=hf, i2=i2: e.bn_stats(out=stats[i2][:, hf, :], in_=zt[i2][:, hf * 512:(hf + 1) * 512]),
                                reads=["zt%d:%d" % (i2, hf)], writes=["st1_%d:%d" % (i2, hf)])
                        add("dve", lambda e, i2=i2: e.bn_aggr(out=mv_[i2][:, 0:2], in_=stats[i2][:].rearrange("p a b -> p (a b)")),
                            reads=["st1_%d:0" % i2, "st1_%d:1" % i2], writes=["mv1_%d" % i2])
                        add("dve", lambda e, i2=i2: e.tensor_scalar(out=mv_[i2][:, 2:3], in0=mv_[i2][:, 1:2], scalar1=EPS, scalar2=None, op0=ALU.add),
                            reads=["mv1_%d" % i2], writes=["mv1_%d" % i2])

                    def ln1_A2a(t):
                        i2 = t % 3
                        add("act", lambda e, i2=i2: e.activation(out=mv_[i2][:, 2:3], in_=mv_[i2][:, 2:3], func=AF.Sqrt),
                            reads=["mv1_%d" % i2], writes=["mv1_%d" % i2])

                    def ln1_A2b(t):
                        i2 = t % 3
                        add("dve", lambda e, i2=i2: e.reciprocal(out=mv_[i2][:, 2:3], in_=mv_[i2][:, 2:3]),
                            reads=["mv1_%d" % i2], writes=["mv1_%d" % i2])
                        add("dve", lambda e, i2=i2: e.scalar_tensor_tensor(out=mv_[i2][:, 3:4], in0=mv_[i2][:, 0:1], scalar=-1.0, in1=mv_[i2][:, 2:3], op0=ALU.mult, op1=ALU.mult),
                            reads=["mv1_%d" % i2], writes=["mv1_%d" % i2])

                    def ln1_B1(t):
                        i2 = t % 3
                        add("act", lambda e, i2=i2: e.activation(out=zn[i2][:], in_=zt[i2][:], func=AF.Identity, scale=mv_[i2][:, 2:3], bias=mv_[i2][:, 3:4]),
                            reads=["zt%d:0" % i2, "zt%d:1" % i2, "mv1_%d" % i2], writes=["zn%d" % i2])

                    def ln1_B2a(t):
                        i2 = t % 3
                        add("dve", lambda e, i2=i2: e.tensor_tensor(out=zn[i2][:], in0=zn[i2][:], in1=ln1g[:], op=ALU.mult),
                            reads=["ln1g"], writes=["zn%d" % i2])

                    def ln1_B2b(t):
                        i2 = t % 3
                        add("pool", lambda e, i2=i2: e.tensor_tensor(out=x1f[i2][:], in0=zn[i2][:], in1=ln1b[:], op=ALU.add),
                            reads=["zn%d" % i2, "ln1b"], writes=["x1f%d" % i2])
                        add("sp", lambda e, i2=i2, t=t: e.dma_start(out=x1s[t * 128:(t + 1) * 128, :], in_=x1f[i2][:]),
                            reads=["x1f%d" % i2], writes=["x1s:%d" % t], dma="x1f%d" % i2)
                        add("act", lambda e, i2=i2: e.activation(out=x1b[i2][:], in_=x1f[i2][:], func=AF.Identity), reads=["x1f%d" % i2], writes=["x1b%d" % i2])

                    def ln1_stage(tA, tB):
                        if tB is not None:
                            ln1_B1(tB)
                        if tA is not None:
                            ln1_A1(tA)
                            ln1_A2a(tA)
                        if tB is not None:
                            ln1_B2a(tB)
                        if tA is not None:
                            ln1_A2b(tA)
                        if tB is not None:
                            ln1_B2b(tB)

                    def ln1_tr(t):
                        i2 = t % 3
                        bt, btk = (bankT[:], "bankT") if t % 2 == 0 else (bankT2, "bank6")
                        for k in range(8):
                            add("pe", lambda e, i2=i2, k=k, bt=bt: e.transpose(out=bt[:, k * 128:(k + 1) * 128], in_=x1b[i2][:, k * 128:(k + 1) * 128], identity=ident[:]),
                                reads=["x1b%d" % i2, "ident"], writes=[btk])
                        if t < NT:
                            copy_op("dve", xT[:, :, t * 128:(t + 1) * 128], bt.rearrange("p (k c) -> p k c", k=8), mixk, ["x1T:%d" % t, btk])
                        else:
                            copy_op("dve", xT[:, :, T:TW], bt.rearrange("p (k c) -> p k c", k=8)[:, :, 0:8], mixk + ["mxdp"], ["x1T:d", btk])

                    ln1_mm(0)
                    ln1_mm(1)
                    ln1_mm(2)
                    ln1_stage(0, None)
                    ln1_stage(1, 0)
                    for t in range(NT + 1):
                        if t + 3 <= NT:
                            ln1_mm(t + 3)
                        ln1_stage(t + 2 if t + 2 <= NT else None, t + 1 if t + 1 <= NT else None)
                        ln1_tr(t)
                    if dbg_x1 is not None:
                        for t in range(NT):
                            add("sp", lambda e, t=t: e.dma_start(out=dbg_x1[t * 128:(t + 1) * 128, :], in_=x1s[t * 128:(t + 1) * 128, :]),
                                reads=["x1s:%d" % t], writes=["dbgx1:%d" % t], dma="dbgx1")
                    S_.flush(barrier=True)
                S_.stop_at("ln1")
        S_.stop_at("pre_ffn")


        with ExitStack() as l4:
            hT = sb("hT", [128, NJ, TW], BF16, l4)
            cwn = sb("cwn", [88, 128], F32, l4)
            cw = sb("cw", [128, 88], F32, l4)
            alast = sb("alast", [128, 2, NJ], F32, l4)
            dav = sb("dav", [128, NJ, 2, 8], F32, l4)
            bflag = sb("bflag_sb", [128, 1], F32, l4)
            cstA = sb("cstA", [128, 128], F32, l4)
            cstB = sb("cstB", [128, 128], F32, l4)
            dst_ = sb("dst_", [128, NJ, 8], F32, l4)
            tcv = sb("tcv", [128, NJ, 4], F32, l4)
            tgd = sb("tgd", [128, NJ, 4], F32, l4)
            dtmp = sb("dtmp", [128, 4, NJ], F32, l4)
            dat88 = sb("dat88", [128, 128], F32, l4)
            add("sp", lambda e: e.dma_start(out=bflag[:], in_=dram["bflag"]), writes=["bflag"], dma="c_bflag")

            def halo_fn(j, AB, a2):
                add("pool", lambda e, AB=AB, j=j: e.tensor_scalar(out=AB[:, 0:2], in0=dav[:, j, 0, 4:6], scalar1=bflag[:, 0:1], scalar2=None, op0=ALU.mult),
                    reads=["dav:%d" % j, "bflag"], writes=["abuf%d:h" % a2])
            add("sp", lambda e: e.dma_start(out=cwn[0:66, :], in_=dram["conv_w"].rearrange("k (j p) -> (k j) p", p=128)), writes=["cwn:0"], dma="c_cw0")
            add("sp", lambda e: e.dma_start(out=cwn[66:88, :], in_=dram["conv_b"].rearrange("(j p) -> j p", p=128)), writes=["cwn:1"], dma="c_cw1")
            add("pe", lambda e: e.transpose(out=banks[6][:, 0:88], in_=cwn[:], identity=identf[0:88, 0:88]), reads=["cwn:0", "cwn:1", "identf"], writes=["bank6"])
            copy_op("dve", cw[:], banks[6][:, 0:88], [], ["cw", "bank6"])
            w_up_v = dram["w_up"].rearrange("(k p) c -> p k c", p=128)
            w_dn_v = dram["w_down"].rearrange("(j p) c -> p j c", p=128)
            wdnA = sb("wdnA", [128, 12, D], BF16, l4)
            with ExitStack() as l5:
                wup = [sb("wup%d" % i, [128, 8, 2, 256], BF16, l5) for i in range(2)]
                abuf = [sb("abuf%d" % i, [128, 2 + T], F32, l5) for i in range(2)]
                t0 = [sb("t0_%d" % i, [128, 512], F32, l5) for i in range(2)]
                t1 = [sb("t1_%d" % i, [128, 512], F32, l5) for i in range(2)]
                gl = [sb("gl%d" % i, [128, 512], F32, l5) for i in range(2)]

                def load_wup(jp):
                    s2 = jp % 2
                    for hv in range(2):
                        c0 = hv * DFF + jp * 256
                        add("pool", lambda e, s2=s2, hv=hv, c0=c0: e.dma_start(out=wup[s2][:, :, hv, :], in_=w_up_v[:, :, c0:c0 + 256]),
                            writes=["wup%d:%d" % (s2, hv)], dma="wup%d_%d" % (s2, hv))
                load_wup(0)
                for (j0, j1) in ((0, 6), (6, 12)):
                    add("pool", lambda e, j0=j0, j1=j1: e.dma_start(out=wdnA[:, j0:j1, :], in_=w_dn_v[:, j0:j1, :]), writes=["wdnA:%d" % j0], dma="wdnA%d" % j0)
                rotA = Rot([0, 1, 2])
                rotB = Rot([3, 4, 5])
                it = 0
                for jp in range(NJ // 2):
                    if jp + 1 < NJ // 2:
                        load_wup(jp + 1)
                    s2 = jp % 2
                    for jj in range(2):
                        j = 2 * jp + jj
                        a2 = j % 2
                        AB = abuf[a2]
                        for hv in range(2):
                            for k in range(8):
                                add("pe", lambda e, hv=hv, k=k, s2=s2, jj=jj: e.matmul(
                                    banks[6][:, hv * 8:(hv + 1) * 8], lhsT=wup[s2][:, k, hv, jj * 128:(jj + 1) * 128], rhs=xT[:, k, T:TW],
                                    start=(k == 0), stop=(k == 7)), reads=["wup%d:%d" % (s2, hv), "x1T:d"], writes=["bank6"])
                        add("dve", lambda e, j=j: e.tensor_copy(out=dav[:, j], in_=banks[6][:, 0:16].rearrange("p (a n) -> p a n", a=2)),
                            writes=["dav:%d" % j, "bank6"])
                        halo_fn(j, AB, a2)
                        for tt in range(4):
                            i2 = it % 2
                            it += 1
                            ba, bv = rotA.next(), rotB.next()
                            xk = ["x1T:%d" % t for t in range(tt * 4, tt * 4 + 4)]
                            for (b, hv) in ((ba, 0), (bv, 1)):
                                for k in range(8):
                                    add("pe", lambda e, b=b, hv=hv, k=k, s2=s2, jj=jj, tt=tt: e.matmul(
                                        banks[b][:], lhsT=wup[s2][:, k, hv, jj * 128:(jj + 1) * 128], rhs=xT[:, k, tt * 512:(tt + 1) * 512],
                                        start=(k == 0), stop=(k == 7)), reads=["wup%d:%d" % (s2, hv)] + xk, writes=["bank%d" % b])
                            c0 = 2 + tt * 512
                            add("act", lambda e, AB=AB, ba=ba, c0=c0: e.activation(out=AB[:, c0:c0 + 512], in_=banks[ba][:], func=AF.Identity),
                                writes=["abuf%d:%d" % (a2, tt), "bank%d" % ba])
                            add("act", lambda e, i2=i2, ba=ba, j=j: e.activation(out=t0[i2][:], in_=banks[ba][:], func=AF.Identity,
                                                                               scale=cw[:, 44 + j:45 + j], bias=cw[:, 66 + j:67 + j]),
                                reads=["cw"], writes=["t0_%d" % i2, "bank%d" % ba])
                            prevk = ["abuf%d:%d" % (a2, tt - 1)] if tt > 0 else ["abuf%d:h" % a2]
                            add("dve", lambda e, i2=i2, AB=AB, c0=c0, j=j: e.scalar_tensor_tensor(
                                out=t1[i2][:], in0=AB[:, c0 - 1:c0 + 511], scalar=cw[:, 22 + j:23 + j], in1=t0[i2][:], op0=ALU.mult, op1=ALU.add),
                                reads=["cw", "t0_%d" % i2, "abuf%d:%d" % (a2, tt)] + prevk, writes=["t1_%d" % i2])
                            add("dve", lambda e, i2=i2, AB=AB, c0=c0, j=j: e.scalar_tensor_tensor(
                                out=t1[i2][:], in0=AB[:, c0 - 2:c0 + 510], scalar=cw[:, j:j + 1], in1=t1[i2][:], op0=ALU.mult, op1=ALU.add),
                                reads=["cw", "abuf%d:%d" % (a2, tt)] + prevk, writes=["t1_%d" % i2])
                            add("act", lambda e, i2=i2: e.activation(out=gl[i2][:], in_=t1[i2][:], func=AF.Gelu_apprx_tanh),
                                reads=["t1_%d" % i2], writes=["gl%d" % i2])
                            add("dve", lambda e, i2=i2, bv=bv, j=j, tt=tt: e.tensor_tensor(
                                out=hT[:, j, tt * 512:(tt + 1) * 512], in0=gl[i2][:], in1=banks[bv][:], op=ALU.mult),
                                reads=["gl%d" % i2], writes=["hT:%d:%d" % (j, tt), "bank%d" % bv])
                        add("pool", lambda e, AB=AB, j=j: e.tensor_copy(out=alast[:, :, j], in_=AB[:, T:T + 2]),
                            reads=["abuf%d:3" % a2], writes=["alast:%d" % j])
                add("pe", lambda e: e.transpose(out=banks[6][0:44, 0:128], in_=alast[:].rearrange("p t j -> p (t j)"), identity=identf[:]),
                    reads=["alast:%d" % j for j in range(NJ)] + ["identf"], writes=["bank6"])
                add("act", lambda e: e.activation(out=t0[0][0:44, 0:128], in_=banks[6][0:44, 0:128], func=AF.Identity), writes=["t0_0", "bank6"])
                add("sp", lambda e: e.dma_start(out=dram["conv_p"].rearrange("t (j p) -> (t j) p", p=128), in_=t0[0][0:44, 0:128]),
                    reads=["t0_0"], writes=["o_convp"], dma="convp")
                S_.flush(barrier=True)
            S_.stop_at("ffn_up")
            with ExitStack() as l5:
                wdnB = sb("wdnB", [128, NJ - 12, D], BF16, l5)
                ln2g = sb("ln2g", [128, D], F32, l5)
                ln2b = sb("ln2b", [128, D], F32, l5)
                xres = [sb("x1r%d" % i, [128, D], F32, l5) for i in range(1)]
                zt = sb("ztB", [128, D], F32, l5)
                zn = zt
                yst = [sb("yst%d" % i, [128, D], F32, l5) for i in range(2)]
                stats = [sb("stats2_%d" % i, [128, 2, 6], F32, l5) for i in range(2)]
                mv_ = [sb("mv2_%d" % i, [128, 4], F32, l5) for i in range(2)]
                for (j0, j1) in ((12, 17), (17, NJ)):
                    add("pool", lambda e, j0=j0, j1=j1: e.dma_start(out=wdnB[:, j0 - 12:j1 - 12, :], in_=w_dn_v[:, j0:j1, :]), writes=["wdn:%d" % j0], dma="wdn%d" % j0)
                wdk = ["wdn:12", "wdn:17"]
                cstv = dram["cstate"].rearrange("n s (j p) -> (n s j) p", p=128)
                add("sp", lambda e: e.dma_start(out=cstA[:], in_=cstv[0:128, :]), writes=["cstA"], dma="c_cstA")
                add("sp", lambda e: e.dma_start(out=cstB[0:48, :], in_=cstv[128:176, :]), writes=["cstB"], dma="c_cstB")
                add("sp", lambda e: e.dma_start(out=dram["conv_s"][:, 0, :], in_=dram["cstate"][:, 1, :]), writes=["o_convs0"], dma="o_convs0")
                add("pe", lambda e: e.transpose(out=banks[5][:, 0:128], in_=cstA[:], identity=identf[:]), reads=["cstA", "identf"], writes=["bank5"])
                add("pe", lambda e: e.transpose(out=banks[5][:, 128:176], in_=cstB[0:48, :], identity=identf[0:48, 0:48]), reads=["cstB", "identf"], writes=["bank5"])
                add("dve", lambda e: e.tensor_copy(out=dst_[:], in_=banks[5][:, 0:176].rearrange("p (a j) -> p j a", a=8)), writes=["dst_", "bank5"])
                davk = ["dav:%d" % j for j in range(NJ)]
                for j in range(NJ):
                    add("dve", lambda e, j=j: e.tensor_scalar(out=tcv[:, j, :], in0=dst_[:, j, 0:8:2], scalar1=cw[:, j:j + 1], scalar2=cw[:, 66 + j:67 + j],
                                                             op0=ALU.mult, op1=ALU.add), reads=["dst_", "cw"], writes=["tcv:%d" % j])
                    add("dve", lambda e, j=j: e.scalar_tensor_tensor(out=tcv[:, j, :], in0=dst_[:, j, 1:8:2], scalar=cw[:, 22 + j:23 + j], in1=tcv[:, j, :],
                                                                    op0=ALU.mult, op1=ALU.add), reads=["dst_", "cw", "tcv:%d" % j], writes=["tcv:%d" % j])
                    add("dve", lambda e, j=j: e.scalar_tensor_tensor(out=tcv[:, j, :], in0=dav[:, j, 0, 0:4], scalar=cw[:, 44 + j:45 + j], in1=tcv[:, j, :],
                                                                    op0=ALU.mult, op1=ALU.add), reads=["cw", "tcv:%d" % j], writes=["tcv:%d" % j])
                add("act", lambda e: e.activation(out=tgd[:].rearrange("p j n -> p (j n)"), in_=tcv[:].rearrange("p j n -> p (j n)"), func=AF.Gelu_apprx_tanh),
                    reads=["tcv:%d" % j for j in range(NJ)], writes=["tgd"])
                add("pool", lambda e: e.memset(hT[:, :, T:TW], 0.0), writes=["hT:dz"])
                add("dve", lambda e: e.tensor_tensor(out=hT[:, :, T:T + 4], in0=tgd[:], in1=dav[:, :, 1, 0:4], op=ALU.mult),
                    reads=["tgd", "hT:dz"], writes=["hT:d"])
                add("dve", lambda e: e.tensor_copy(out=dtmp[:], in_=dav[:, :, 0, 0:4].rearrange("p j n -> p n j")), writes=["dtmp"])
                add("pe", lambda e: e.transpose(out=banks[4][0:88, 0:128], in_=dtmp[:].rearrange("p n j -> p (n j)"), identity=identf[:]),
                    reads=["dtmp", "identf"], writes=["bank4"])
                add("act", lambda e: e.activation(out=dat88[0:88, :], in_=banks[4][0:88, 0:128], func=AF.Identity), writes=["dat88", "bank4"])
                for n_ in range(4):
                    add("sp", lambda e, n_=n_: e.dma_start(out=dram["conv_s"][n_, 1, :].rearrange("(j p) -> j p", p=128), in_=dat88[n_ * NJ:(n_ + 1) * NJ, :]),
                        reads=["dat88"], writes=["o_convs1:%d" % n_], dma="o_convs1")
                add("sp", lambda e: e.dma_start(out=ln2g[:], in_=dram["ln2_g"].partition_broadcast(128)), writes=["ln2g"], dma="c_ln2g")
                add("sp", lambda e: e.dma_start(out=ln2b[:], in_=dram["ln2_b"].partition_broadcast(128)), writes=["ln2b"], dma="c_ln2b")
                rotZ = Rot([0, 1, 2, 3, 4, 5])
                hdp = sb("hdp", [128, NJ, 128], BF16, l5)
                add("pool", lambda e: e.memset(hdp[:], 0.0), writes=["hdp"])
                add("pool", lambda e: e.tensor_copy(out=hdp[:, :, 0:8], in_=hT[:, :, T:TW]), reads=["hdp", "hT:d", "hT:dz"], writes=["hdp"])
                def ln2_load(t):
                    add("sp", lambda e, t=t: e.dma_start(out=xres[0][:], in_=x1s[t * 128:(t + 1) * 128, :]), writes=["x1r0"], dma="x1r0")
                ln2_load(0)
                for t in range(NT + 1):
                    i2 = t % 2
                    bz = [rotZ.next(), rotZ.next()]
                    for hf in range(2):
                        for j in range(NJ):
                            lhs = hT[:, j, t * 128:(t + 1) * 128] if t < NT else hdp[:, j, :]
                            add("pe", lambda e, hf=hf, j=j, lhs=lhs, bz=bz: e.matmul(
                                banks[bz[hf]][:], lhsT=lhs, rhs=(wdnA[:, j, hf * 512:(hf + 1) * 512] if j < 12 else wdnB[:, j - 12, hf * 512:(hf + 1) * 512]),
                                start=(j == 0), stop=(j == NJ - 1)), reads=wdk + ["hT:%d:%d" % (j, t // 4), "hdp"], writes=["bank%d" % bz[hf]])
                        add("dve", lambda e, hf=hf, i2=i2, bz=bz: e.scalar_tensor_tensor(
                            out=zt[:, hf * 512:(hf + 1) * 512], in0=xres[0][:, hf * 512:(hf + 1) * 512], scalar=ALPHA, in1=banks[bz[hf]][:],
                            op0=ALU.mult, op1=ALU.add), reads=["x1r0"], writes=["zt2:%d" % hf, "bank%d" % bz[hf]])
                        add("dve", lambda e, hf=hf, i2=i2: e.bn_stats(out=stats[i2][:, hf, :], in_=zt[:, hf * 512:(hf + 1) * 512]),
                            reads=["zt2:%d" % hf], writes=["st2_%d:%d" % (i2, hf)])
                    if t + 1 <= NT:
                        ln2_load(t + 1)
                    add("dve", lambda e, i2=i2: e.bn_aggr(out=mv_[i2][:, 0:2], in_=stats[i2][:].rearrange("p a b -> p (a b)")),
                        reads=["st2_%d:0" % i2, "st2_%d:1" % i2], writes=["mv2_%d" % i2])
                    add("dve", lambda e, i2=i2: e.tensor_scalar(out=mv_[i2][:, 2:3], in0=mv_[i2][:, 1:2], scalar1=EPS, scalar2=None, op0=ALU.add),
                        reads=["mv2_%d" % i2], writes=["mv2_%d" % i2])
                    add("pool", lambda e, i2=i2: e.tensor_tensor(out=mv_[i2][:, 2:3], in0=mv_[i2][:, 2:3], in1=mhalf[:, 0:1], op=ALU.pow),
                        reads=["mv2_%d" % i2, "mhalf"], writes=["mv2_%d" % i2])
                    add("dve", lambda e, i2=i2: e.scalar_tensor_tensor(out=mv_[i2][:, 3:4], in0=mv_[i2][:, 0:1], scalar=-1.0, in1=mv_[i2][:, 2:3], op0=ALU.mult, op1=ALU.mult),
                        reads=["mv2_%d" % i2], writes=["mv2_%d" % i2])
                    add("act", lambda e, i2=i2: e.activation(out=zn[:], in_=zt[:], func=AF.Identity, scale=mv_[i2][:, 2:3], bias=mv_[i2][:, 3:4]),
                        reads=["mv2_%d" % i2], writes=["zn2", "zt2:0", "zt2:1"])
                    add("dve", lambda e: e.tensor_tensor(out=zn[:], in0=zn[:], in1=ln2g[:], op=ALU.mult), reads=["ln2g"], writes=["zn2"])
                    add("pool", lambda e, i2=i2: e.tensor_tensor(out=yst[i2][:], in0=zn[:], in1=ln2b[:], op=ALU.add),
                        reads=["ln2b"], writes=["yst%d" % i2, "zn2", "zt2:0", "zt2:1"])
                    if t < NT:
                        add("sp", lambda e, i2=i2, t=t: e.dma_start(out=dram["y_p"][t * 128:(t + 1) * 128, :], in_=yst[i2][:]),
                            reads=["yst%d" % i2], writes=["o_y:%d" % t], dma="yst%d" % i2)
                    else:
                        add("sp", lambda e, i2=i2: e.dma_start(out=dram["y_s"], in_=yst[i2][0:4, :]),
                            reads=["yst%d" % i2], writes=["o_ys"], dma="yst%d" % i2)
                S_.flush(barrier=True)
        S_.flush(barrier=True)
    print("instructions emitted:", S_.nins)
    return nc


_CACHE = {}


def kernel(**inputs):
    x = np.ascontiguousarray(inputs["x_prompt"][0])
    if "nc" not in _CACHE:
        _CACHE["nc"] = build_program()
    nc = _CACHE["nc"]
    in_maps = []
    xs = np.ascontiguousarray(inputs["x_sample"][:, 0, :])
    caches = [np.ascontiguousarray(inputs[k][0]).reshape(32, -1, 512) for k in ("cache_win128_kv", "cache_win512_kv", "cache_win2048_kv")]
    cmem = np.ascontiguousarray(inputs["cache_mem_kv"][0]).reshape(32, 256, 1024)
    cstate = np.ascontiguousarray(inputs["state_ffn_conv"][0])
    for c in range(NCORES):
        xh = np.zeros((2 * T, D), np.float32)
        if c > 0:
            xh[:T] = x[(c - 1) * T:c * T]
        xh[T:] = x[c * T:(c + 1) * T]
        hb = np.full((128, 1), NEG if c == 0 else 0.0, np.float32)
        xd = np.zeros((8, D), np.float32)
        xd[0:4] = xs[4 * c:4 * c + 4]
        xbk = np.zeros((2, 3, 128, D), np.float32)
        bkmask = np.zeros((128, 6), np.float32)
        if c > 0:
            for nb in range(2):
                p = c * T - 2 + nb
                xd[4 + nb] = x[p]
                for g, (win, dil) in enumerate(GROUPS):
                    pos = p - dil * np.arange(1, 129)
                    ok = pos >= 0
                    xbk[nb, g, ok] = x[pos[ok]]
                    bkmask[~ok, nb * 3 + g] = NEG
        m = {"xh": xh, "hbias": hb, "mem": np.ascontiguousarray(inputs["mem_prompt"][0]),
             "xd": xd, "xbk": xbk, "bkmask": bkmask, "bflag": np.full((128, 1), 0.0 if c == 0 else 1.0, np.float32),
             "cwin0": caches[0][4 * c:4 * c + 4], "cwin1": caches[1][4 * c:4 * c + 4], "cwin2": caches[2][4 * c:4 * c + 4],
             "cmem": cmem[4 * c:4 * c + 4], "cstate": cstate[4 * c:4 * c + 4]}
        for nm in ("w_in", "w_mem_kv", "ln_v_g", "ln_v_b", "w_spatial", "b_spatial", "w_branch_a", "w_branch_b", "w_branch_m",
                   "b_gate", "w_out", "ln1_g", "ln1_b", "w_up", "conv_w", "conv_b", "w_down", "ln2_g", "ln2_b"):
            m[nm] = np.ascontiguousarray(inputs[nm][0])
        in_maps.append(m)
    sel = os.environ.get("MK_CORES")
    if sel:
        ids = [int(v) for v in sel.split(",")]
        res = run_bass_kernel_spmd(nc, [in_maps[i] for i in ids], core_ids=list(range(len(ids))))
        return res
    res = run_bass_kernel_spmd(nc, in_maps, core_ids=list(range(NCORES)))
    R = res.results
    f32 = np.float32
    y_prompt = np.concatenate([R[c]["y_p"] for c in range(NCORES)], axis=0).reshape(1, S, D).astype(f32)
    y_sample = np.concatenate([R[c]["y_s"] for c in range(NCORES)], axis=0).reshape(32, 1, D).astype(f32)
    win_p = [np.asarray(R[NCORES - 1]["win%d_p" % g]).reshape(1, 1, GROUPS[g][0], 2, 4, 64).astype(f32) for g in range(3)]
    mem_p = np.asarray(R[0]["new_mem_kv"]).reshape(1, 1, 256, 2, 4, 128).astype(f32)
    conv_p = np.asarray(R[NCORES - 1]["conv_p"]).reshape(1, 1, 2, DFF).astype(f32)
    win_s = [np.concatenate([R[c]["win%d_s" % g] for c in range(NCORES)], axis=0).reshape(1, 32, 1, 2, 4, 64).astype(f32) for g in range(3)]
    gv_s = np.concatenate([R[c]["gmlp_v_s"] for c in range(NCORES)], axis=0).reshape(1, 32, 1, 768).astype(f32)
    conv_s = np.concatenate([R[c]["conv_s"] for c in range(NCORES)], axis=0).reshape(1, 32, 2, DFF).astype(f32)
    return (y_prompt, y_sample, win_p[0], win_p[1], win_p[2], mem_p, conv_p, win_s[0], win_s[1], win_s[2], gv_s, conv_s)
```

```python
import os
import numpy as np
from contextlib import ExitStack
import concourse.bass as bass
import concourse.mybir as mybir
from concourse.bass_utils import run_bass_kernel_spmd

F32 = mybir.dt.float32
BF16 = mybir.dt.bfloat16
AF = mybir.ActivationFunctionType
ALU = mybir.AluOpType
AX = mybir.AxisListType

NCORES = 8
D = 1024
S = 16384
T = S // NCORES
NT = T // 128
TW = T + 8
INW = 7424
DFF = 2816
NJ = DFF // 128
ALPHA = 2.0 ** 0.25
EPS = 1e-5
NEG = -30000.0
GROUPS = ((128, 1), (512, 4), (2048, 16))
DBG = os.environ.get("MK_DEBUG", "")


class Op:
    __slots__ = ("eng", "fn", "reads", "writes", "dma", "deps", "signal", "sigval", "waits", "waitall")

    def __init__(self, eng, fn, reads, writes, dma, waitall):
        self.eng = eng
        self.fn = fn
        self.reads = tuple(reads)
        self.writes = tuple(writes)
        self.dma = dma
        self.deps = ()
        self.signal = False
        self.sigval = None
        self.waits = ()
        self.waitall = waitall


class Sched:
    def __init__(self, nc, stack):
        self.nc = nc
        self.stack = stack
        self.engines = {"pe": nc.tensor, "act": nc.scalar, "dve": nc.vector, "pool": nc.gpsimd, "sp": nc.sync}
        self.esem = {e: stack.enter_context(nc.semaphore("s_" + e)) for e in self.engines}
        self.ecnt = {e: 0 for e in self.engines}
        self.dsem = {}
        self.dcnt = {}
        self.waited = {e: {} for e in self.engines}
        self.ops = []
        self.nins = 0

    enabled = True

    def add(self, eng, fn, reads=(), writes=(), dma=None, waitall=False):
        if self.enabled:
            self.ops.append(Op(eng, fn, reads, writes, dma, waitall))

    def stop_at(self, name):
        if os.environ.get("MK_STOP") == name:
            self.enabled = False

    def _sem(self, key):
        if key[0] == "e":
            return self.esem[key[1]]
        return self.dsem[key[1]]

    def flush(self, barrier=True):
        ops = self.ops
        self.ops = []
        last_w = {}
        readers = {}
        for i, op in enumerate(ops):
            deps = set()
            for k in op.reads:
                if k in last_w:
                    deps.add(last_w[k])
            for k in op.writes:
                if k in last_w:
                    deps.add(last_w[k])
                deps.update(readers.get(k, ()))
            deps.discard(i)
            for k in op.reads:
                readers.setdefault(k, []).append(i)
            for k in op.writes:
                last_w[k] = i
                readers[k] = []
            op.deps = deps

        def skip(pj, op):
            return pj.dma is None and op.dma is None and pj.eng == "pe" and op.eng == "pe"

        for op in ops:
            for j in op.deps:
                pj = ops[j]
                if pj.dma is None and not skip(pj, op):
                    pj.signal = True
        if barrier:
            lastc = {}
            for i, op in enumerate(ops):
                if op.dma is None and op.fn is not None:
                    lastc[op.eng] = i
            for i in lastc.values():
                ops[i].signal = True
        dfinal = dict(self.dcnt)
        for op in ops:
            if op.dma is not None:
                dfinal[op.dma] = dfinal.get(op.dma, 0) + 16
        for op in ops:
            if op.dma is not None:
                if op.dma not in self.dsem:
                    self.dsem[op.dma] = self.stack.enter_context(self.nc.semaphore("d_" + op.dma))
                    self.dcnt[op.dma] = 0
                self.dcnt[op.dma] += 16
                op.sigval = (("d", op.dma), dfinal[op.dma] if op.waitall else self.dcnt[op.dma])
            elif op.signal:
                self.ecnt[op.eng] += 1
                op.sigval = (("e", op.eng), self.ecnt[op.eng])
        for op in ops:
            need = {}
            for j in op.deps:
                pj = ops[j]
                if skip(pj, op):
                    continue
                k, v = pj.sigval
                if need.get(k, 0) < v:
                    need[k] = v
            eng = self.engines[op.eng]
            wd = self.waited[op.eng]
            for k, v in need.items():
                if wd.get(k, 0) < v:
                    eng.wait_ge(self._sem(k), v)
                    wd[k] = v
                    self.nins += 1
            if op.fn is None:
                continue
            ins = op.fn(eng)
            self.nins += 1
            if op.dma is not None:
                ins.then_inc(self.dsem[op.dma], 16)
            elif op.signal:
                ins.then_inc(self.esem[op.eng], 1)
        if barrier:
            for e, eng in self.engines.items():
                wd = self.waited[e]
                for e2 in self.engines:
                    if e2 != e and wd.get(("e", e2), 0) < self.ecnt[e2]:
                        eng.wait_ge(self.esem[e2], self.ecnt[e2])
                        wd[("e", e2)] = self.ecnt[e2]
                for dk, dv in self.dcnt.items():
                    if wd.get(("d", dk), 0) < dv:
                        eng.wait_ge(self.dsem[dk], dv)
                        wd[("d", dk)] = dv


def build_program(stage=99):
    nc = bass.Bass("TRN2", target_bir_lowering=False)
    dram = {}

    def din(name, shape):
        dram[name] = nc.dram_tensor(name, list(shape), F32, kind="ExternalInput").ap()
        return dram[name]

    def dout(name, shape):
        dram[name] = nc.dram_tensor(name, list(shape), F32, kind="ExternalOutput").ap()
        return dram[name]

    xh = din("xh", [2 * T, D])
    hbias_d = din("hbias", [128, 1])
    w_in = din("w_in", [D, INW])
    out_win = [dout("win%d_p" % g, [GROUPS[g][0], 512]) for g in range(3)]
    din("mem", [256, D])
    din("w_mem_kv", [D, D])
    din("ln_v_g", [768])
    din("ln_v_b", [768])
    din("w_spatial", [4, 128, 128])
    din("b_spatial", [4, 128])
    din("w_branch_a", [256, D])
    din("w_branch_b", [768, D])
    din("w_branch_m", [512, D])
    din("b_gate", [3, D])
    din("w_out", [D, D])
    din("ln1_g", [D])
    din("ln1_b", [D])
    dout("new_mem_kv", [256, D])
    din("w_up", [D, 2 * DFF])
    din("conv_w", [3, DFF])
    din("conv_b", [DFF])
    din("w_down", [DFF, D])
    din("ln2_g", [D])
    din("ln2_b", [D])
    din("xd", [8, D])
    din("xbk", [2, 3, 128, D])
    din("bkmask", [128, 6])
    din("bflag", [128, 1])
    din("cwin0", [4, 128, 512])
    din("cwin1", [4, 512, 512])
    din("cwin2", [4, 2048, 512])
    din("cmem", [4, 256, D])
    din("cstate", [4, 2, DFF])
    dout("y_s", [4, D])
    for g_ in range(3):
        dout("win%d_s" % g_, [4, 512])
    dout("gmlp_v_s", [4, 768])
    dout("conv_s", [4, 2, DFF])
    dbg_d = dout("dbg_d", [128, 64]) if "dd" in DBG else None
    dout("y_p", [T, D])
    dout("conv_p", [2, DFF])
    x1s = nc.dram_tensor("x1s", [T + 128, D], F32, kind="Internal").ap()
    dbg_x1 = dout("dbg_x1", [T, D]) if "x1" in DBG else None
    dbg_oa = dout("dbg_oa", [256, T]) if "oa" in DBG else None

    with ExitStack() as st:
        S_ = Sched(nc, st)
        add = S_.add

        def sb(name, shape, dt, stack=st):
            return stack.enter_context(nc.sbuf_tensor(name, list(shape), dt))

        banks = [st.enter_context(nc.psum_tensor("bank%d" % i, [128, 512], F32)) for i in range(7)]
        bankT = st.enter_context(nc.psum_tensor("bankT", [128, 1024], BF16))
        bankT2 = banks[6][:].bitcast(BF16)
        identf = sb("identf", [128, 128], F32)
        ident = sb("ident", [128, 128], BF16)
        zerosb = sb("zerosb", [128, 512], BF16)
        maskP = sb("maskP", [128, 512], BF16)
        maskC = sb("maskC", [128, 512], BF16)
        maskPH = sb("maskPH", [128, 512], BF16)
        onespad = sb("onespad", [128, 2, 128], BF16)
        hbias = sb("hbias_sb", [128, 1], F32)
        mhalf = sb("mhalf", [128, 1], F32)
        xT = sb("xT", [128, 8, TW], BF16)

        add("pool", lambda e: e.memset(identf[:], 1.0), writes=["identf"])
        add("pool", lambda e: e.affine_select(out=identf[:], in_=identf[:], pattern=[[-1, 128]], compare_op=ALU.is_equal,
                                              fill=0.0, base=0, channel_multiplier=1), reads=["identf"], writes=["identf"])
        add("dve", lambda e: e.tensor_copy(out=ident[:], in_=identf[:]), reads=["identf"], writes=["ident"])
        add("pool", lambda e: e.memset(zerosb[:], 0.0), writes=["zerosb"])
        add("pool", lambda e: e.affine_select(out=maskP[:], in_=zerosb[:], pattern=[[0, 4], [-1, 128]], compare_op=ALU.is_ge,
                                              fill=NEG, base=0, channel_multiplier=1), reads=["zerosb"], writes=["maskP"])
        add("pool", lambda e: e.affine_select(out=maskC[:], in_=zerosb[:], pattern=[[0, 4], [1, 128]], compare_op=ALU.is_ge,
                                              fill=NEG, base=0, channel_multiplier=-1), reads=["zerosb"], writes=["maskC"])
        add("sp", lambda e: e.dma_start(out=hbias[:], in_=hbias_d), writes=["hbias"], dma="c_hb")
        with ExitStack() as lc:
            maskPf = sb("maskPf", [128, 512], F32, lc)
            add("dve", lambda e: e.tensor_copy(out=maskPf[:], in_=maskP[:]), reads=["maskP"], writes=["maskPf"])
            add("dve", lambda e: e.tensor_scalar(out=maskPH[:], in0=maskPf[:], scalar1=hbias[:, 0:1], scalar2=None, op0=ALU.add),
                reads=["maskPf", "hbias"], writes=["maskPH"])
            S_.flush(barrier=True)
        add("pool", lambda e: e.memset(mhalf[:], -0.5), writes=["mhalf"])
        add("pool", lambda e: e.memset(onespad[:], 0.0), writes=["onespad"])
        add("pool", lambda e: e.memset(onespad[:, 0, 0:64], 1.0), reads=["onespad"], writes=["onespad"])
        add("pool", lambda e: e.memset(onespad[:, 1, 64:128], 1.0), reads=["onespad"], writes=["onespad"])

        S_.stop_at("consts")
        w_in_v = w_in.rearrange("(k p) c -> p k c", p=128)
        xh_v = xh.rearrange("(t p) c -> p t c", p=128)

        class Rot:
            def __init__(self, ids):
                self.ids = ids
                self.i = 0

            def next(self):
                b = self.ids[self.i % len(self.ids)]
                self.i += 1
                return b

        evac_tog = [0]

        def evac_eng():
            evac_tog[0] ^= 1
            return "act" if evac_tog[0] else "dve"

        def copy_op(eng, out, in_, reads, writes):
            if eng == "act":
                add("act", lambda e: e.activation(out=out, in_=in_, func=AF.Identity), reads=reads, writes=writes)
            else:
                add(eng, lambda e: e.tensor_copy(out=out, in_=in_), reads=reads, writes=writes)

        with ExitStack() as l1:
            o_aT = sb("o_aT", [128, 2, TW], BF16, l1)
            with ExitStack() as l2:
                xb = [sb("xb%d" % i, [128, 2, D], BF16, l2) for i in range(2)]
                xTh = sb("xTh", [128, 8, 512], BF16, l2)
                wqkv = [sb("wqkv%d" % i, [128, 8, 768], BF16, l2) for i in range(2)]
                qpad = sb("qpad", [128, 2, 2, T], BF16, l2)
                kT = sb("kT", [128, 2, 2 * T], BF16, l2)
                vpad = sb("vpad", [128, 32, 4, 128], BF16, l2)
                acc = sb("acc", [128, 4, T], F32, l2)
                PT = [sb("PT%d" % i, [128, 2, 512], BF16, l2) for i in range(2)]
                kvst = [sb("kvst0", [128, 512], F32, l2), None]

                kvst[1] = PT[1][:].rearrange("p a b -> p (a b)").bitcast(F32)
                xdb = PT[0][:].rearrange("p a b -> p (a b)")
                xdTp = sb("xdTp", [128, 8, 128], BF16, l2)
                dfm = sb("dfm", [128, 3, 6, 8], F32, l2)
                dqpad = sb("dqpad", [128, 3, 2, 2, 8], BF16, l2)
                ck = sb("ck", [128, 6, 512], BF16, l2)
                ckT = sb("ckT", [128, 12, 128], BF16, l2)
                cvpad = sb("cvpad", [128, 6, 4, 128], BF16, l2)
                dP = sb("dP", [128, 32], BF16, l2)
                dacc = sb("dacc", [128, 2, 2, 8], F32, l2)
                dprod = sb("dprod", [128, 2, 8], F32, l2)
                dpo = sb("dpo", [128, 2, 8], F32, l2)
                dov = sb("dov", [128, 2, 8], F32, l2)
                headsel = sb("headsel", [128, 128], F32, l2)
                bkm = sb("bkm", [128, 6], F32, l2)
                def kvst_ap(i):
                    return kvst[0][:] if i == 0 else kvst[1]
                xb_i = [0]

                def load_xT(rows_ap_fn, ntiles, dst, dst_key):
                    for t0 in range(0, ntiles, 2):
                        slot = xb_i[0] % 2
                        xb_i[0] += 1
                        src = rows_ap_fn(t0, 2)
                        add("pool", lambda e, slot=slot, src=src: e.dma_start(out=xb[slot][:], in_=src),
                            writes=["xb%d" % slot], dma="xb%d" % slot)
                        if zero_jobs:
                            zj = zero_jobs.pop(0)
                            add("pool", lambda e, zj=zj: e.memset(zj[0], 0.0), writes=[zj[1]])
                        for tt in range(2):
                            t = t0 + tt
                            bt, btk = (bankT[:], "bankT") if tt == 0 else (bankT2, "bank6")
                            for k in range(8):
                                add("pe", lambda e, slot=slot, tt=tt, k=k, bt=bt: e.transpose(
                                    out=bt[:, k * 128:(k + 1) * 128], in_=xb[slot][:, tt, k * 128:(k + 1) * 128], identity=ident[:]),
                                    reads=["xb%d" % slot, "ident"], writes=[btk])
                            copy_op(evac_eng(), dst[:, :, t * 128:(t + 1) * 128], bt.rearrange("p (k c) -> p k c", k=8),
                                    [], ["%s:%d" % (dst_key, t), btk])

                zero_jobs = [(qpad[:, c], "qpad_z%d" % c) for c in range(2)] + [(vpad[:, 8 * i:8 * (i + 1)], "vpad_z%d" % i) for i in range(4)]
                def load_wqkv(g):
                    ws_ = g % 2
                    for part, c0 in enumerate((g * 256, 768 + g * 256, 1536 + g * 256)):
                        add("pool", lambda e, part=part, c0=c0, ws_=ws_: e.dma_start(
                            out=wqkv[ws_][:, :, part * 256:(part + 1) * 256], in_=w_in_v[:, :, c0:c0 + 256]),
                            writes=["wqkv%d:%d" % (ws_, part)], dma="wqkv%d_%d" % (ws_, part))
                load_wqkv(2)
                load_xT(lambda t0, n: xh_v[:, 16 + t0:16 + t0 + n, :], 16, xT, "xT")


                add("dve", lambda e: e.memset(xdb, 0.0), writes=["xdb"])
                add("pool", lambda e: e.dma_start(out=xdb[0:8, :], in_=dram["xd"]), reads=["xdb"], writes=["xdb2"], dma="xdb")
                add("pool", lambda e: e.memset(dqpad[:], 0.0), writes=["dqpad_z"])
                add("dve", lambda e: e.memset(cvpad[:], 0.0), writes=["cvpad_z"])
                add("pool", lambda e: e.memset(headsel[:], 0.0), writes=["headsel"])
                add("pool", lambda e: e.memset(headsel[0:64, 0:64], 1.0), reads=["headsel"], writes=["headsel"])
                add("pool", lambda e: e.memset(headsel[64:128, 64:128], 1.0), reads=["headsel"], writes=["headsel"])
                add("sp", lambda e: e.dma_start(out=bkm[:], in_=dram["bkmask"]), writes=["bkm"], dma="c_bkm")
                for k in range(8):
                    add("pe", lambda e, k=k: e.transpose(out=bankT[:, k * 128:(k + 1) * 128], in_=xdb[:, k * 128:(k + 1) * 128], identity=ident[:]),
                        reads=["xdb", "xdb2", "ident"], writes=["bankT"])
                copy_op("dve", xdTp[:], bankT[:].rearrange("p (k c) -> p k c", k=8), [], ["xdTp", "bankT"])
                add("pool", lambda e: e.tensor_copy(out=xT[:, :, T:TW], in_=xdTp[:, :, 0:8]), reads=["xdTp"], writes=["xTd"])

                S_.stop_at("xT")
                rotP = Rot([0, 1, 2, 3])
                wq_i = [0]
                first_group = [True]
                kv_i = [0]
                pt_i = [0]

                for g in (2, 1, 0):
                    win, dil = GROUPS[g]
                    ws = g % 2
                    W = wqkv[ws]
                    if g > 0:
                        load_wqkv(g - 1)
                    wkeys = ["wqkv%d:%d" % (ws, p) for p in range(3)]
                    S_.stop_at("wload%d" % g)
                    nblk_halo = 1
                    if g == 0:
                        halo_tok = 128
                    elif g == 1:
                        halo_tok = 512
                    else:
                        halo_tok = 2048
                    nb = T // win
                    kstride = (nb + 1) * 128

                    def kcol(r, n):
                        return r * kstride + (n + 1) * 128

                    def qcol(r, n):
                        return r * nb * 128 + n * 128

                    def vblk(r, n):
                        return r * (nb + 1) + (n + 1)

                    nh_tiles = halo_tok // 128
                    hbase = T - halo_tok

                    def halo_rows(t0, n, dil=dil, hbase=hbase):
                        if dil == 1:
                            return xh_v[:, (hbase // 128) + t0:(hbase // 128) + t0 + n, :]
                        v = xh[hbase:hbase + 128 * dil, :].rearrange("(i r) c -> i r c", r=dil)
                        return v[:, t0:t0 + n, :]
                    S_.stop_at("halo%d" % g)
                    def perm_dst(buf2d, tt, width_per_r, col_of):
                        if g == 0:
                            return buf2d[:, col_of(0, 0) + tt * 512:col_of(0, 0) + (tt + 1) * 512], None
                        if g == 1:
                            v = buf2d.rearrange("p (r m) -> p r m", r=4)
                            off = col_of(0, tt)
                            return v[:, :, off:off + 128], 4
                        v = buf2d.rearrange("p (r m) -> p r m", r=16)
                        off = col_of(0, 0) + 32 * tt
                        return v[:, :, off:off + 32], 16

                    for c in range(2):
                        for what in ("k", "q"):
                            for tt in range(4):
                                b = rotP.next()
                                wc0 = (256 if what == "k" else 0) + c * 128
                                for k in range(8):
                                    add("pe", lambda e, b=b, k=k, tt=tt, wc0=wc0, W=W: e.matmul(
                                        banks[b][:], lhsT=W[:, k, wc0:wc0 + 128], rhs=xT[:, k, tt * 512:(tt + 1) * 512],
                                        start=(k == 0), stop=(k == 7)),
                                        reads=wkeys + ["xT:%d" % t for t in range(tt * 4, tt * 4 + 4)], writes=["bank%d" % b])
                                if what == "k":
                                    dstv, rr = perm_dst(kT[:, c, 0:dil * kstride], tt, kstride, kcol)
                                    srcv = banks[b][:] if rr is None else banks[b][:].rearrange("p (i r) -> p r i", r=rr)
                                    copy_op(evac_eng(), dstv, srcv, ["bank%d" % b], ["kT:%d:own%d" % (c, tt)])
                                else:
                                    ve = evac_eng()
                                    for hp in range(2):
                                        ps = slice(hp * 64, (hp + 1) * 64)
                                        dstv, rr = perm_dst(qpad[:, c, hp, :], tt, nb * 128, qcol)
                                        dstv = dstv[ps]
                                        srcv = banks[b][ps, :] if rr is None else banks[b][ps, :].rearrange("p (i r) -> p r i", r=rr)
                                        copy_op(ve, dstv, srcv, ["bank%d" % b, "qpad_z0", "qpad_z1"], ["qpad:%d:%d:%d" % (c, hp, tt)])
                    kown = ["kT:%d:own%d" % (c, tt) for c in range(2) for tt in range(4)]
                    qown = ["qpad:%d:%d:%d" % (c, hp, tt) for c in range(2) for hp in range(2) for tt in range(4)]

                    S_.stop_at("kq%d" % g)
                    for r in range(dil):
                        for n in range(nb):
                            b = rotP.next()
                            base = n * win + r
                            for k in range(8):
                                add("pe", lambda e, b=b, k=k, base=base, W=W, dil=dil: e.matmul(
                                    banks[b][:, 0:256], lhsT=xT[:, k, base:base + 127 * dil + 1:dil], rhs=W[:, k, 512:768],
                                    start=(k == 0), stop=(k == 7)),
                                    reads=wkeys + ["xT:%d" % t for t in range(n * win // 128, (n + 1) * win // 128)], writes=["bank%d" % b])
                            blk = vblk(r, n)
                            ve = evac_eng()
                            for hp in range(2):
                                dstv = vpad[:, blk, hp::2, hp * 64:(hp + 1) * 64]
                                srcv = banks[b][:, 0:256].rearrange("p (c q x) -> p c q x", c=2, q=2)[:, :, hp, :]
                                copy_op(ve, dstv, srcv, ["bank%d" % b, "vpad_z0", "vpad_z1", "vpad_z2", "vpad_z3"], ["vpad:%d:%d" % (blk, hp)])

                    S_.stop_at("v%d" % g)
                    for t in range(NT - win // 128, NT):
                        b = rotP.next()
                        for k in range(8):
                            add("pe", lambda e, b=b, k=k, t=t, W=W: e.matmul(
                                banks[b][:, 0:256], lhsT=xT[:, k, t * 128:(t + 1) * 128], rhs=W[:, k, 256:512],
                                start=(k == 0), stop=(k == 7)), reads=wkeys + ["xT:%d" % t], writes=["bank%d" % b])
                        for k in range(8):
                            add("pe", lambda e, b=b, k=k, t=t, W=W: e.matmul(
                                banks[b][:, 256:512], lhsT=xT[:, k, t * 128:(t + 1) * 128], rhs=W[:, k, 512:768],
                                start=(k == 0), stop=(k == 7)), reads=wkeys + ["xT:%d" % t], writes=["bank%d" % b])
                        ks = kv_i[0] % 2
                        kv_i[0] += 1
                        copy_op(evac_eng(), kvst_ap(ks), banks[b][:], ["bank%d" % b], ["kvst%d" % ks])
                        row0 = (t - (NT - win // 128)) * 128
                        add("sp", lambda e, ks=ks, g=g, row0=row0: e.dma_start(out=out_win[g][row0:row0 + 128, :], in_=kvst_ap(ks)),
                            reads=["kvst%d" % ks], writes=["out_kvst%d" % ks], dma="kvst%d" % ks)

                    for hc0 in range(0, nh_tiles, 4):
                        hn = min(4, nh_tiles - hc0)
                        if hn >= 2:
                            load_xT(lambda t0, n, hc0=hc0: halo_rows(hc0 + t0, n), hn, xTh, "xTh")
                        else:
                            slot = xb_i[0] % 2
                            xb_i[0] += 1
                            src = halo_rows(0, 1)
                            add("pool", lambda e, slot=slot, src=src: e.dma_start(out=xb[slot][:, 0:1, :], in_=src),
                                writes=["xb%d" % slot], dma="xb%d" % slot)
                            for k in range(8):
                                add("pe", lambda e, slot=slot, k=k: e.transpose(
                                    out=bankT[:, k * 128:(k + 1) * 128], in_=xb[slot][:, 0, k * 128:(k + 1) * 128], identity=ident[:]),
                                    reads=["xb%d" % slot, "ident"], writes=["bankT"])
                            copy_op(evac_eng(), xTh[:, :, 0:128], bankT[:].rearrange("p (k c) -> p k c", k=8), ["bankT"], ["xTh:0"])
                        S_.stop_at("hload%d" % g)
                        for c in range(2):
                            for h0 in range(0, hn * 128, 512):
                                n = min(512, hn * 128 - h0)
                                b = rotP.next()
                                for k in range(8):
                                    add("pe", lambda e, b=b, c=c, k=k, h0=h0, n=n, W=W: e.matmul(
                                        banks[b][:, 0:n], lhsT=W[:, k, 256 + c * 128:256 + (c + 1) * 128], rhs=xTh[:, k, h0:h0 + n],
                                        start=(k == 0), stop=(k == 7)),
                                        reads=wkeys + ["xTh:%d" % t for t in range(h0 // 128, (h0 + n) // 128)], writes=["bank%d" % b])
                                nr = n // 128
                                r0 = hc0 + h0 // 128
                                dstv = kT[:, c, 0:dil * kstride].rearrange("p (r m) -> p r m", m=kstride)[:, r0:r0 + nr, 0:128]
                                copy_op(evac_eng(), dstv, banks[b][:, 0:n].rearrange("p (r i) -> p r i", i=128),
                                        ["bank%d" % b], ["kT:%d:%d" % (c, r) for r in range(r0, r0 + nr)])
                        S_.stop_at("hk%d" % g)
                        for rl in range(hn):
                            r = hc0 + rl
                            b = rotP.next()
                            for k in range(8):
                                add("pe", lambda e, b=b, k=k, rl=rl, W=W: e.matmul(
                                    banks[b][:, 0:256], lhsT=xTh[:, k, rl * 128:(rl + 1) * 128], rhs=W[:, k, 512:768],
                                    start=(k == 0), stop=(k == 7)), reads=wkeys + ["xTh:%d" % rl], writes=["bank%d" % b])
                            blk = vblk(r, -1)
                            ve = evac_eng()
                            for hp in range(2):
                                dstv = vpad[:, blk, hp::2, hp * 64:(hp + 1) * 64]
                                srcv = banks[b][:, 0:256].rearrange("p (c q x) -> p c q x", c=2, q=2)[:, :, hp, :]
                                copy_op(ve, dstv, srcv, ["bank%d" % b, "vpad_z0", "vpad_z1", "vpad_z2", "vpad_z3"], ["vpad:%d:%d" % (blk, hp)])
                                S_.stop_at("hv%d_%d_%d" % (g, rl, hp))
                        S_.stop_at("hchunk%d" % g)

                    ckk = ["ck:s", "ck:4", "ck:5"]
                    dpk = ["dP:s", "dP:4", "dP:5"]
                    def dec_p1():
                        for part in range(3):
                            for c in range(2):
                                pc = part * 2 + c
                                for k in range(8):
                                    add("pe", lambda e, pc=pc, part=part, c=c, k=k, W=W: e.matmul(
                                        banks[6][:, pc * 8:(pc + 1) * 8], lhsT=W[:, k, part * 256 + c * 128:part * 256 + (c + 1) * 128], rhs=xT[:, k, T:TW],
                                        start=(k == 0), stop=(k == 7)), reads=wkeys + ["xTd"], writes=["bank6"])
                        add("dve", lambda e, g=g: e.tensor_copy(out=dfm[:, g, :, :], in_=banks[6][:, 0:48].rearrange("p (a n) -> p a n", a=6)),
                            writes=["dfm:%d" % g, "bank6"])
                        for c in range(2):
                            for hp in range(2):
                                ps = slice(hp * 64, (hp + 1) * 64)
                                add("pool", lambda e, g=g, c=c, hp=hp, ps=ps: e.tensor_copy(out=dqpad[ps, g, c, hp, :], in_=dfm[ps, g, c, :]),
                                    reads=["dfm:%d" % g, "dqpad_z"], writes=["dqpad:%d" % g])
                    def dec_p2():
                        b = rotP.next()
                        for (part, c0) in ((1, 0), (2, 256)):
                            for k in range(8):
                                add("pe", lambda e, b=b, k=k, part=part, c0=c0, W=W: e.matmul(
                                    banks[b][:, c0:c0 + 256], lhsT=xdTp[:, k, :], rhs=W[:, k, part * 256:(part + 1) * 256],
                                    start=(k == 0), stop=(k == 7)), reads=wkeys + ["xdTp"], writes=["bank%d" % b])
                        ks = 0
                        copy_op(evac_eng(), kvst_ap(ks), banks[b][:], [], ["kvst%d" % ks, "bank%d" % b])
                        add("sp", lambda e, ks=ks, g=g: e.dma_start(out=dram["win%d_s" % g], in_=kvst_ap(ks)[0:4, :]),
                            reads=["kvst%d" % ks], writes=["out_kvst%d" % ks], dma="kvst%d" % ks)
                    def dec_ck():
                        csrc = dram["cwin%d" % g][:, 0:win - dil + 1:dil, :].rearrange("b r x -> r b x")
                        add("pool", lambda e, csrc=csrc: e.dma_start(out=ck[:, 0:4, :], in_=csrc), writes=["ck:s"], dma="ck_s")
                    bslots = []

                    def dec_p3a():
                        for nb_ in range(2):
                            slot = xb_i[0] % 2
                            xb_i[0] += 1
                            bslots.append(slot)
                            add("pool", lambda e, slot=slot, nb_=nb_, g=g: e.dma_start(out=xb[slot][:, 0, :], in_=dram["xbk"][nb_, g]),
                                writes=["xb%d" % slot], dma="xb%d" % slot)

                    def dec_p3():
                        for nb_ in range(2):
                            slot = bslots[nb_]
                            for k in range(8):
                                add("pe", lambda e, slot=slot, k=k: e.transpose(
                                    out=bankT[:, k * 128:(k + 1) * 128], in_=xb[slot][:, 0, k * 128:(k + 1) * 128], identity=ident[:]),
                                    reads=["xb%d" % slot, "ident"], writes=["bankT"])
                            copy_op(evac_eng(), xTh[:, :, 0:128], bankT[:].rearrange("p (k c) -> p k c", k=8), [], ["xTh:0", "bankT"])
                            b = rotP.next()
                            for (part, c0) in ((1, 0), (2, 256)):
                                for k in range(8):
                                    add("pe", lambda e, b=b, k=k, part=part, c0=c0, W=W: e.matmul(
                                        banks[b][:, c0:c0 + 256], lhsT=xTh[:, k, 0:128], rhs=W[:, k, part * 256:(part + 1) * 256],
                                        start=(k == 0), stop=(k == 7)), reads=wkeys + ["xTh:0"], writes=["bank%d" % b])
                            copy_op(evac_eng(), ck[:, 4 + nb_, :], banks[b][:], [], ["ck:%d" % (4 + nb_), "bank%d" % b])
                    def dec_p4():
                        for i8 in range(0, 12, 8):
                            cnt = min(8, 12 - i8)
                            for ii in range(cnt):
                                idx = i8 + ii
                                n_, c_ = idx // 2, idx % 2
                                add("pe", lambda e, ii=ii, n_=n_, c_=c_: e.transpose(
                                    out=bankT[:, ii * 128:(ii + 1) * 128], in_=ck[:, n_, c_ * 128:(c_ + 1) * 128], identity=ident[:]),
                                    reads=ckk + ["ident"], writes=["bankT"])
                            copy_op("dve", ckT[:, i8:i8 + cnt, :], bankT[:, 0:cnt * 128].rearrange("p (k c) -> p k c", k=cnt), [], ["ckT:%d" % i8, "bankT"])
                        for hp in range(2):
                            add("pool", lambda e, hp=hp: e.tensor_copy(
                                out=cvpad[:, :, hp::2, hp * 64:(hp + 1) * 64],
                                in_=ck[:, :, 256:512].rearrange("p n (c q x) -> p n c q x", c=2, q=2)[:, :, :, hp, :]),
                                reads=ckk + ["cvpad_z"], writes=["cvpad:%d" % hp])
                    def dec_p5():
                        for n_ in range(6):
                            for c_ in range(2):
                                for hp in range(2):
                                    col = 64 + n_ * 4 + c_ * 2 + hp
                                    add("pe", lambda e, n_=n_, c_=c_, hp=hp, col=col, g=g: e.matmul(
                                        banks[6][:, col:col + 1], lhsT=ckT[:, n_ * 2 + c_, :], rhs=dqpad[:, g, c_, hp, n_:n_ + 1], start=True, stop=True),
                                        reads=["ckT:0", "ckT:8", "dqpad:%d" % g, "dqpad_z"], writes=["bank6"])
                        add("act", lambda e: e.activation(out=dP[:, 0:16], in_=banks[6][:, 64:80], func=AF.Exp, scale=0.125), writes=["dP:s", "bank6"])
                        for n_ in (4, 5):
                            add("act", lambda e, n_=n_, g=g: e.activation(out=dP[:, n_ * 4:(n_ + 1) * 4], in_=banks[6][:, 64 + n_ * 4:64 + (n_ + 1) * 4], func=AF.Exp,
                                                                           scale=0.125, bias=bkm[:, (n_ - 4) * 3 + g:(n_ - 4) * 3 + g + 1]),
                                reads=["bkm"], writes=["dP:%d" % n_, "bank6"])
                    def dec_p6():
                        for n_ in range(6):
                            for c_ in range(2):
                                for (base, which) in ((128, "o"), (160, "d")):
                                    for hp in range(2):
                                        col = n_ * 4 + c_ * 2 + hp
                                        lhs = cvpad[:, n_, 2 * c_ + hp, :] if which == "o" else onespad[:, hp, :]
                                        add("pe", lambda e, n_=n_, c_=c_, hp=hp, col=col, base=base, lhs=lhs: e.matmul(
                                            banks[6][:, base + c_ * 8 + n_:base + c_ * 8 + n_ + 1], lhsT=lhs, rhs=dP[:, col:col + 1],
                                            start=(hp == 0), stop=(hp == 1)), reads=dpk + ["cvpad:0", "cvpad:1", "cvpad_z", "onespad"], writes=["bank6"])
                    def dec_p7():
                        add("dve", lambda e, g=g: e.tensor_tensor(out=dprod[:], in0=dfm[:, g, 0:2, :], in1=dfm[:, g, 2:4, :], op=ALU.mult),
                            reads=["dfm:%d" % g], writes=["dprod"])
                        add("pe", lambda e: e.matmul(banks[5][:, 0:16], lhsT=headsel[:], rhs=dprod[:].rearrange("p c n -> p (c n)"), start=True, stop=True),
                            reads=["headsel", "dprod"], writes=["bank5"])
                        add("act", lambda e: e.activation(out=dpo[:].rearrange("p c n -> p (c n)"), in_=banks[5][:, 0:16], func=AF.Exp, scale=0.125),
                            writes=["dpo", "bank5"])
                        add("dve", lambda e, g=g: e.tensor_tensor(out=dov[:], in0=dpo[:], in1=dfm[:, g, 4:6, :], op=ALU.mult),
                            reads=["dpo", "dfm:%d" % g], writes=["dov"])
                        add("dve", lambda e: e.tensor_tensor(out=dov[:, :, 0:6], in0=dov[:, :, 0:6], in1=banks[6][:, 128:144].rearrange("p (c n) -> p c n", c=2)[:, :, 0:6], op=ALU.add),
                            reads=["dov"], writes=["dov", "bank6"])
                        add("dve", lambda e: e.tensor_tensor(out=dpo[:, :, 0:6], in0=dpo[:, :, 0:6], in1=banks[6][:, 160:176].rearrange("p (c n) -> p c n", c=2)[:, :, 0:6], op=ALU.add),
                            reads=["dpo"], writes=["dpo", "bank6"])
                        if first_group[0]:
                            add("dve", lambda e: e.tensor_copy(out=dacc[:, 0], in_=dov[:]), reads=["dov"], writes=["dacc"])
                            add("dve", lambda e: e.tensor_copy(out=dacc[:, 1], in_=dpo[:]), reads=["dpo"], writes=["dacc"])
                        else:
                            add("dve", lambda e: e.tensor_tensor(out=dacc[:, 0], in0=dacc[:, 0], in1=dov[:], op=ALU.add), reads=["dov", "dacc"], writes=["dacc"])
                            add("dve", lambda e: e.tensor_tensor(out=dacc[:, 1], in0=dacc[:, 1], in1=dpo[:], op=ALU.add), reads=["dpo", "dacc"], writes=["dacc"])
                    dec_ck()
                    dec_sched = {1: dec_p1, 2: (lambda: (dec_p2(), dec_p3a())), 4: dec_p3, 7: dec_p4, 9: dec_p5, 11: dec_p6, 13: dec_p7}
                    S_.stop_at("kvout%d" % g)
                    rotS = Rot([0, 1, 2, 3])
                    rotO = Rot([4, 5])
                    tiles = [(r, n) for r in range(dil) for n in range(nb)]
                    tinfo = {}

                    def emit_S(ti):
                        r, n = tiles[ti]
                        sbk = [rotS.next(), rotS.next()]
                        ob = rotO.next()
                        pslot = pt_i[0] % 2
                        pt_i[0] += 1
                        tinfo[ti] = (ob, pslot)
                        halo_prev = (n == 0)
                        for bi, (kn, mask) in enumerate(((n - 1, maskPH if halo_prev else maskP), (n, maskC))):
                            b = sbk[bi]
                            mk = "maskPH" if (bi == 0 and halo_prev) else ("maskP" if bi == 0 else "maskC")
                            add("pe", lambda e, b=b, mask=mask: e.matmul(banks[b][:], lhsT=ident[:], rhs=mask[:], start=True, stop=False),
                                reads=["ident", mk], writes=["bank%d" % b])
                            kc0 = kcol(r, kn)
                            kkeys = kown + (["kT:%d:%d" % (c, r) for c in range(2)] if kn < 0 else [])
                            for hp in range(2):
                                for c in range(2):
                                    j = hp * 2 + c
                                    add("pe", lambda e, b=b, c=c, hp=hp, j=j, kc0=kc0, q0=qcol(r, n): e.matmul(
                                        banks[b][:, j * 128:(j + 1) * 128], lhsT=kT[:, c, kc0:kc0 + 128], rhs=qpad[:, c, hp, q0:q0 + 128],
                                        start=False, stop=(j == 3)), reads=kkeys + qown + ["qpad_z0", "qpad_z1"], writes=["bank%d" % b])
                            add("act", lambda e, b=b, pslot=pslot, bi=bi: e.activation(
                                out=PT[pslot][:, bi, :], in_=banks[b][:], func=AF.Exp, scale=0.125),
                                reads=["bank%d" % b] + (["out_kvst1", "kvst1"] if pslot == 1 else []), writes=["PT%d:%d" % (pslot, bi)])

                    def emit_PV(ti):
                        r, n = tiles[ti]
                        ob, pslot = tinfo[ti]
                        pkeys = ["PT%d:0" % pslot, "PT%d:1" % pslot]
                        for c in range(2):
                            i = 0
                            for hp in range(2):
                                h = 2 * c + hp
                                j = hp * 2 + c
                                for bi, kn in enumerate((n - 1, n)):
                                    blk = vblk(r, kn)
                                    add("pe", lambda e, ob=ob, c=c, h=h, j=j, bi=bi, blk=blk, pslot=pslot, i=i: e.matmul(
                                        banks[ob][:, c * 128:(c + 1) * 128], lhsT=vpad[:, blk, h, :], rhs=PT[pslot][:, bi, j * 128:(j + 1) * 128],
                                        start=(i == 0), stop=(i == 3)),
                                        reads=pkeys + ["vpad_z0", "vpad_z1", "vpad_z2", "vpad_z3", "vpad:%d:%d" % (blk, hp)], writes=["bank%d" % ob])
                                    i += 1
                        i = 0
                        for hp in range(2):
                            for bi in range(2):
                                add("pe", lambda e, ob=ob, hp=hp, bi=bi, pslot=pslot, i=i: e.matmul(
                                    banks[ob][:, 256:512], lhsT=onespad[:, hp, :], rhs=PT[pslot][:, bi, hp * 256:(hp + 1) * 256],
                                    start=(i == 0), stop=(i == 3)), reads=pkeys + ["onespad"], writes=["bank%d" % ob])
                                i += 1
                        p0 = n * win + r
                        accv = acc[:, :, p0:p0 + 127 * dil + 1:dil]
                        srcv = banks[ob][:].rearrange("p (a q) -> p a q", a=4)
                        tkeys = ["acc:%d" % t for t in range(n * win // 128, (n + 1) * win // 128)]
                        if first_group[0]:
                            add("dve", lambda e, accv=accv, srcv=srcv: e.tensor_copy(out=accv, in_=srcv),
                                reads=["bank%d" % ob], writes=tkeys)
                        else:
                            add("dve", lambda e, accv=accv, srcv=srcv: e.tensor_tensor(out=accv, in0=accv, in1=srcv, op=ALU.add),
                                reads=["bank%d" % ob] + tkeys, writes=tkeys)
                        if g == 0:
                            sl = slice(n * 128, (n + 1) * 128)
                            add("dve", lambda e, sl=sl: e.reciprocal(out=acc[:, 2:4, sl], in_=acc[:, 2:4, sl]), reads=tkeys, writes=tkeys)
                            add("pool", lambda e, sl=sl: e.tensor_tensor(out=o_aT[:, :, sl], in0=acc[:, 0:2, sl], in1=acc[:, 2:4, sl], op=ALU.mult),
                                reads=tkeys, writes=["o_aT:%d" % n])
                    emit_S(0)
                    for ti in range(len(tiles)):
                        if ti + 1 < len(tiles):
                            emit_S(ti + 1)
                        emit_PV(ti)
                        if ti in dec_sched:
                            dec_sched[ti]()
                    first_group[0] = False
                    S_.stop_at("attn%d" % g)
                    S_.flush(barrier=True)

                add("dve", lambda e: e.reciprocal(out=dacc[:, 1], in_=dacc[:, 1]), reads=["dacc"], writes=["dacc"])
                add("dve", lambda e: e.tensor_tensor(out=o_aT[:, :, T:TW], in0=dacc[:, 0], in1=dacc[:, 1], op=ALU.mult), reads=["dacc"], writes=["o_aT:d"])
                if dbg_d is not None:
                    add("dve", lambda e: e.tensor_copy(out=dprod[:], in_=o_aT[:, :, T:TW]), reads=["o_aT:d"], writes=["dprod"])
                    add("sp", lambda e: e.dma_start(out=dbg_d[:, 0:16], in_=dprod[:].rearrange("p c n -> p (c n)")), reads=["dprod"], writes=["dbg_d"], dma="dbgd")
                if dbg_oa is not None:
                    add("dve", lambda e: e.tensor_copy(out=acc[:, 0:2, :], in_=o_aT[:, :, 0:T]),
                        reads=["o_aT:%d" % n for n in range(16)],
                        writes=["acc:%d" % t for t in range(16)])
                    add("sp", lambda e: e.dma_start(out=dbg_oa.rearrange("(c p) t -> p c t", p=128), in_=acc[:, 0:2, :]),
                        reads=["acc:%d" % t for t in range(16)], writes=["dbg_oa"], dma="dbg")
                S_.flush(barrier=True)
            S_.stop_at("gmlp_start")
            with ExitStack() as l2b:
                o_bT = sb("o_bT", [128, 8, TW], BF16, l2b)
                onesb = sb("onesb", [128, 128], BF16, l2b)
                add("pool", lambda e: e.memset(onesb[:], 1.0), writes=["onesb"])
                UST = [0, 128, 192, 320, 384, 512, 576, 704]
                with ExitStack() as l3:
                    vtok = sb("vtok", [128, NT, 832], BF16, l3)
                    wvb = sb("wvb", [128, 8, 768], BF16, l3)
                    wub = sb("wub", [128, 8, 896], BF16, l3)
                    gtmp = [sb("gtmp%d" % i, [128, 768], F32, l3) for i in range(2)]
                    gtmp3 = gtmp + [sb("gtmp2", [128, 768], F32, l3)]
                    stats3 = [sb("stats3_%d" % i, [128, 2, 6], F32, l3) for i in range(3)]
                    mv3 = [sb("mv3_%d" % i, [128, 4], F32, l3) for i in range(3)]
                    ntmp = [sb("ntmp%d" % i, [128, 768], F32, l3) for i in range(2)]
                    lnvg = sb("lnvg", [128, 768], F32, l3)
                    lnvb = sb("lnvb", [128, 768], F32, l3)
                    stats = [sb("stats%d" % i, [128, 2, 6], F32, l3) for i in range(2)]
                    mv_ = [sb("mv%d" % i, [128, 4], F32, l3) for i in range(2)]
                    wsn = sb("wsn", [128, 4, 128], F32, l3)
                    wsm = sb("wsm", [128, 4, 128], BF16, l3)
                    wsT = sb("wsT", [128, 4, 128], BF16, l3)
                    bs2 = sb("bs2", [128, 512], F32, l3)
                    bshi = sb("bshi", [128, 512], BF16, l3)
                    bshf = sb("bshf", [128, 512], F32, l3)
                    bsrep = sb("bsrep", [128, 512], BF16, l3)
                    ug = [sb("ug%d" % i, [128, 512], F32, l3) for i in range(2)]

                    add("pool", lambda e: e.memset(vtok[:, :, 768:832], 0.0), writes=["vtok_z"])
                    add("pool", lambda e: e.dma_start(out=wvb[:], in_=w_in_v[:, :, 3072:3840]), writes=["wvb"], dma="wvb")
                    add("pool", lambda e: e.dma_start(out=wub[:], in_=w_in_v[:, :, 2304:3200]), writes=["wub"], dma="wub")
                    add("sp", lambda e: e.dma_start(out=lnvg[:], in_=dram["ln_v_g"].partition_broadcast(128)), writes=["lnvg"], dma="c_lnvg")
                    add("sp", lambda e: e.dma_start(out=lnvb[:], in_=dram["ln_v_b"].partition_broadcast(128)), writes=["lnvb"], dma="c_lnvb")
                    add("sp", lambda e: e.dma_start(out=wsn[:], in_=dram["w_spatial"].rearrange("g i j -> i g j")), writes=["wsn"], dma="c_wsn")
                    add("pool", lambda e: e.affine_select(out=wsm[:], in_=wsn[:], pattern=[[0, 4], [-1, 128]], compare_op=ALU.is_ge,
                                                          fill=0.0, base=0, channel_multiplier=1), reads=["wsn"], writes=["wsm"])
                    for gi in range(4):
                        add("pe", lambda e, gi=gi: e.transpose(out=bankT[:, gi * 128:(gi + 1) * 128], in_=wsm[:, gi, :], identity=ident[:]),
                            reads=["wsm", "ident"], writes=["bankT"])
                    copy_op("dve", wsT[:], bankT[:, 0:512].rearrange("p (g i) -> p g i", g=4), [], ["wsT", "bankT"])
                    add("pool", lambda e: e.memset(bs2[:], 0.0), writes=["bs2"])
                    bsrc = dram["b_spatial"].rearrange("(o g) i -> o (g i)", o=1)
                    add("sp", lambda e: e.dma_start(out=bs2[0:1, :], in_=bsrc), reads=["bs2"], writes=["bs2a"], dma="c_bs2a")
                    add("sp", lambda e: e.dma_start(out=bs2[1:2, :], in_=bsrc), reads=["bs2"], writes=["bs2b"], dma="c_bs2b")
                    add("dve", lambda e: e.tensor_copy(out=bshi[:], in_=bs2[:]), reads=["bs2", "bs2a", "bs2b"], writes=["bshi"])
                    add("dve", lambda e: e.tensor_copy(out=bshf[:], in_=bshi[:]), reads=["bshi"], writes=["bshf"])
                    add("dve", lambda e: e.tensor_tensor(out=bs2[:], in0=bs2[:], in1=bshf[:], op=ALU.subtract),
                        reads=["bshf", "bs2a", "bs2b"], writes=["bs2", "bs2a", "bs2b"])
                    add("dve", lambda e: e.tensor_scalar(out=bshf[:], in0=bshf[:], scalar1=identf[:, 0:1], scalar2=None, op0=ALU.mult),
                        reads=["bshf", "identf"], writes=["bshf"])
                    add("dve", lambda e: e.scalar_tensor_tensor(out=bsrep[:], in0=bs2[:], scalar=identf[:, 1:2], in1=bshf[:], op0=ALU.mult, op1=ALU.add),
                        reads=["bshf", "bs2", "identf"], writes=["bsrep"])

                    rotV = Rot([0, 1, 2, 3, 4, 5])

                    def vt_s1(t):
                        b0, b1 = rotV.next(), rotV.next()
                        for (b, c0, cn) in ((b0, 0, 512), (b1, 512, 256)):
                            for k in range(8):
                                add("pe", lambda e, b=b, k=k, t=t, c0=c0, cn=cn: e.matmul(
                                    banks[b][:, 0:cn], lhsT=xT[:, k, t * 128:(t + 1) * 128], rhs=wvb[:, k, c0:c0 + cn],
                                    start=(k == 0), stop=(k == 7)), reads=["wvb", "xT:%d" % t], writes=["bank%d" % b])
                        i3 = t % 3
                        add("act", lambda e, i3=i3, b0=b0: e.activation(out=gtmp3[i3][:, 0:512], in_=banks[b0][:], func=AF.Gelu_apprx_tanh),
                            writes=["gtmp%d" % i3, "bank%d" % b0])
                        add("act", lambda e, i3=i3, b1=b1: e.activation(out=gtmp3[i3][:, 512:768], in_=banks[b1][:, 0:256], func=AF.Gelu_apprx_tanh),
                            reads=["gtmp%d" % i3], writes=["gtmp%d" % i3, "bank%d" % b1])
                        for ci in range(2):
                            add("dve", lambda e, i3=i3, ci=ci: e.bn_stats(out=stats3[i3][:, ci, :], in_=gtmp3[i3][:, ci * 384:(ci + 1) * 384]),
                                reads=["gtmp%d" % i3], writes=["stats3_%d:%d" % (i3, ci)])
                        add("dve", lambda e, i3=i3: e.bn_aggr(out=mv3[i3][:, 0:2], in_=stats3[i3][:].rearrange("p a b -> p (a b)")),
                            reads=["stats3_%d:0" % i3, "stats3_%d:1" % i3], writes=["mv3_%d" % i3])
                        add("dve", lambda e, i3=i3: e.tensor_scalar(out=mv3[i3][:, 2:3], in0=mv3[i3][:, 1:2], scalar1=EPS, scalar2=None, op0=ALU.add),
                            reads=["mv3_%d" % i3], writes=["mv3_%d" % i3])
                        add("pool", lambda e, i3=i3: e.tensor_tensor(out=mv3[i3][:, 2:3], in0=mv3[i3][:, 2:3], in1=mhalf[:, 0:1], op=ALU.pow),
                            reads=["mv3_%d" % i3, "mhalf"], writes=["mv3_%d" % i3])
                        add("dve", lambda e, i3=i3: e.scalar_tensor_tensor(out=mv3[i3][:, 3:4], in0=mv3[i3][:, 0:1], scalar=-1.0, in1=mv3[i3][:, 2:3], op0=ALU.mult, op1=ALU.mult),
                            reads=["mv3_%d" % i3], writes=["mv3_%d" % i3])

                    def vt_s2(t):
                        i3 = t % 3
                        i2 = t % 2
                        add("act", lambda e, i2=i2, i3=i3: e.activation(out=ntmp[i2][:], in_=gtmp3[i3][:], func=AF.Identity, scale=mv3[i3][:, 2:3], bias=mv3[i3][:, 3:4]),
                            reads=["gtmp%d" % i3, "mv3_%d" % i3], writes=["ntmp%d" % i2])
                        add("dve", lambda e, i2=i2: e.tensor_tensor(out=ntmp[i2][:], in0=ntmp[i2][:], in1=lnvg[:], op=ALU.mult),
                            reads=["lnvg"], writes=["ntmp%d" % i2])
                        add("pool", lambda e, i2=i2, t=t: e.tensor_tensor(out=vtok[:, t, 0:768], in0=ntmp[i2][:], in1=lnvb[:], op=ALU.add),
                            reads=["ntmp%d" % i2, "lnvb"], writes=["vtok:%d" % t])
                    vt_s1(0)
                    vt_s1(1)
                    for t in range(NT):
                        if t + 2 < NT:
                            vt_s1(t + 2)
                        vt_s2(t)
                    S_.stop_at("vtok")
                    xdTp2 = sb("xdTp2", [128, 8, 128], BF16, l3)
                    xTH = sb("xTH", [128, 8, 128], BF16, l3)
                    xHb = sb("xHb", [128, D], BF16, l3)
                    dv = sb("dv", [128, 832], F32, l3)
                    vtokH = sb("vtokH", [128, 832], BF16, l3)
                    du = sb("du", [128, 8, 8], F32, l3)
                    dvT = sb("dvT", [128, 8, 8], F32, l3)
                    dmx = sb("dmx", [128, 8, 8], F32, l3)
                    ws00 = sb("ws00", [128, 4], F32, l3)
                    bs00 = sb("bs00", [128, 4], F32, l3)
                    add("pool", lambda e: e.memset(xdTp2[:], 0.0), writes=["xdTp2"])
                    add("pool", lambda e: e.tensor_copy(out=xdTp2[:, :, 0:8], in_=xT[:, :, T:TW]), reads=["xdTp2"], writes=["xdTp2"])
                    add("pool", lambda e: e.memset(dv[:, 768:832], 0.0), writes=["dv_z"])
                    add("pool", lambda e: e.memset(vtokH[:, 768:832], 0.0), writes=["vtokH_z"])
                    add("pool", lambda e: e.memset(dmx[:], 0.0), writes=["dmx"])
                    add("pool", lambda e: e.dma_start(out=xHb[:], in_=xh[T - 128:T, :]), writes=["xHb"], dma="xHb")
                    wsv = dram["w_spatial"]
                    bsv = dram["b_spatial"]
                    add("sp", lambda e: e.dma_start(out=ws00[:], in_=bass.AP(wsv.tensor, 0, [[0, 128], [128 * 128, 4]]), allow_slow_non_contiguous=True),
                        writes=["ws00"], dma="c_ws00")
                    add("sp", lambda e: e.dma_start(out=bs00[:], in_=bass.AP(bsv.tensor, 0, [[0, 128], [128, 4]]), allow_slow_non_contiguous=True),
                        writes=["bs00"], dma="c_bs00")

                    def vb_ln(lhs, lkeys, out_ap, okeys, i2):
                        b0, b1 = rotV.next(), rotV.next()
                        for (b, c0, cn) in ((b0, 0, 512), (b1, 512, 256)):
                            for k in range(8):
                                add("pe", lambda e, b=b, k=k, c0=c0, cn=cn: e.matmul(
                                    banks[b][:, 0:cn], lhsT=lhs[:, k, :], rhs=wvb[:, k, c0:c0 + cn],
                                    start=(k == 0), stop=(k == 7)), reads=["wvb"] + lkeys, writes=["bank%d" % b])
                        add("act", lambda e: e.activation(out=gtmp[i2][:, 0:512], in_=banks[b0][:], func=AF.Gelu_apprx_tanh),
                            writes=["gtmp%d" % i2, "bank%d" % b0])
                        add("act", lambda e: e.activation(out=gtmp[i2][:, 512:768], in_=banks[b1][:, 0:256], func=AF.Gelu_apprx_tanh),
                            reads=["gtmp%d" % i2], writes=["gtmp%d" % i2, "bank%d" % b1])
                        for ci in range(2):
                            add("dve", lambda e, ci=ci: e.bn_stats(out=stats[i2][:, ci, :], in_=gtmp[i2][:, ci * 384:(ci + 1) * 384]),
                                reads=["gtmp%d" % i2], writes=["stats%d:%d" % (i2, ci)])
                        add("dve", lambda e: e.bn_aggr(out=mv_[i2][:, 0:2], in_=stats[i2][:].rearrange("p a b -> p (a b)")),
                            reads=["stats%d:0" % i2, "stats%d:1" % i2], writes=["mv%d" % i2])
                        add("dve", lambda e: e.tensor_scalar(out=mv_[i2][:, 2:3], in0=mv_[i2][:, 1:2], scalar1=EPS, scalar2=None, op0=ALU.add),
                            reads=["mv%d" % i2], writes=["mv%d" % i2])
                        add("pool", lambda e: e.tensor_tensor(out=mv_[i2][:, 2:3], in0=mv_[i2][:, 2:3], in1=mhalf[:, 0:1], op=ALU.pow),
                            reads=["mv%d" % i2, "mhalf"], writes=["mv%d" % i2])
                        add("dve", lambda e: e.scalar_tensor_tensor(out=mv_[i2][:, 3:4], in0=mv_[i2][:, 0:1], scalar=-1.0, in1=mv_[i2][:, 2:3], op0=ALU.mult, op1=ALU.mult),
                            reads=["mv%d" % i2], writes=["mv%d" % i2])
                        add("act", lambda e: e.activation(out=ntmp[i2][:], in_=gtmp[i2][:], func=AF.Identity, scale=mv_[i2][:, 2:3], bias=mv_[i2][:, 3:4]),
                            reads=["gtmp%d" % i2, "mv%d" % i2], writes=["ntmp%d" % i2])
                        add("dve", lambda e: e.tensor_tensor(out=gtmp[i2][:], in0=ntmp[i2][:], in1=lnvg[:], op=ALU.mult),
                            reads=["ntmp%d" % i2, "lnvg"], writes=["gtmp%d" % i2])
                        add("pool", lambda e: e.tensor_tensor(out=out_ap, in0=gtmp[i2][:], in1=lnvb[:], op=ALU.add),
                            reads=["gtmp%d" % i2, "lnvb"], writes=okeys)
                    def gdec_p1():
                        for k in range(8):
                            add("pe", lambda e, k=k: e.transpose(out=bankT[:, k * 128:(k + 1) * 128], in_=xHb[:, k * 128:(k + 1) * 128], identity=ident[:]),
                                reads=["xHb", "ident"], writes=["bankT"])
                        copy_op("dve", xTH[:], bankT[:].rearrange("p (k c) -> p k c", k=8), [], ["xTH", "bankT"])
                    def gdec_p2():
                        vb_ln(xdTp2, ["xdTp2"], dv[:, 0:768], ["dv"], 0)
                    def gdec_p3():
                        vb_ln(xTH, ["xTH"], vtokH[:, 0:768], ["vtokH"], 1)
                        add("sp", lambda e: e.dma_start(out=dram["gmlp_v_s"], in_=dv[0:4, 0:768]), reads=["dv"], writes=["o_gv"], dma="o_gv")
                    def gdec_p4():
                        for cu in range(8):
                            for k in range(8):
                                add("pe", lambda e, cu=cu, k=k: e.matmul(
                                    banks[4][:, cu * 8:(cu + 1) * 8], lhsT=wub[:, k, UST[cu]:UST[cu] + 128], rhs=xT[:, k, T:TW],
                                    start=(k == 0), stop=(k == 7)), reads=["wub", "xTd"], writes=["bank4"])
                        add("act", lambda e: e.activation(out=du[:].rearrange("p a n -> p (a n)"), in_=banks[4][:, 0:64], func=AF.Gelu_apprx_tanh),
                            writes=["du", "bank4"])
                    def gdec_p5():
                        for hb in range(2):
                            for q4 in range(4):
                                cu = hb * 4 + q4
                                add("pe", lambda e, hb=hb, q4=q4, cu=cu: e.transpose(
                                    out=banks[5][:, q4 * 128:(q4 + 1) * 128], in_=dv[:, UST[cu]:UST[cu] + 128], identity=identf[:]),
                                    reads=["dv", "dv_z", "identf"], writes=["bank5"])
                            add("dve", lambda e, hb=hb: e.tensor_copy(out=dvT[:, hb * 4:(hb + 1) * 4, :], in_=banks[5][:].rearrange("p (a t) -> p a t", a=4)[:, :, 0:8]),
                                writes=["dvT:%d" % hb, "bank5"])
                    def gdec_p6():
                        for cu in range(8):
                            add("dve", lambda e, cu=cu: e.tensor_scalar(out=dmx[:, cu, 0:4], in0=dvT[:, cu, 0:4], scalar1=ws00[:, cu // 2:cu // 2 + 1],
                                                                       scalar2=bs00[:, cu // 2:cu // 2 + 1], op0=ALU.mult, op1=ALU.add),
                                reads=["dvT:0", "dvT:1", "ws00", "bs00", "dmx"], writes=["dmx"])
                        for cu in range(8):
                            gi = cu // 2
                            add("pe", lambda e, cu=cu, gi=gi: e.matmul(banks[6][:, cu * 2:cu * 2 + 2], lhsT=onesb[:], rhs=bsrep[:, gi * 128 + 126:gi * 128 + 128], start=True, stop=False),
                                reads=["onesb", "bsrep"], writes=["bank6"])
                            add("pe", lambda e, cu=cu, gi=gi: e.matmul(banks[6][:, cu * 2:cu * 2 + 2], lhsT=vtokH[:, UST[cu]:UST[cu] + 128], rhs=wsT[:, gi, 126:128], start=False, stop=True),
                                reads=["vtokH", "vtokH_z", "wsT"], writes=["bank6"])
                        add("dve", lambda e: e.tensor_copy(out=dmx[:, :, 4:6], in_=banks[6][:, 0:16].rearrange("p (a n) -> p a n", a=8)), reads=["dmx"], writes=["dmx", "bank6"])
                        add("dve", lambda e: e.tensor_tensor(out=o_bT[:, :, T:TW], in0=du[:], in1=dmx[:], op=ALU.mult), reads=["du", "dmx"], writes=["o_bT:d"])
                        if dbg_d is not None:
                            add("dve", lambda e: e.tensor_copy(out=du[:], in_=o_bT[:, :, T:TW]), reads=["o_bT:d"], writes=["du"])
                            add("sp", lambda e: e.dma_start(out=dbg_d[:, 0:64], in_=du[:].rearrange("p a n -> p (a n)")), reads=["du"], writes=["dbg_d"], dma="dbgd")
                    gdec_sched = {2: gdec_p1, 5: gdec_p2, 9: gdec_p3, 13: gdec_p4, 17: gdec_p5, 22: gdec_p6}
                    rotU = Rot([0, 1])
                    rotM = Rot([2, 3])
                    ui = 0
                    for cu in range(8):
                        st0 = UST[cu]
                        gi = cu // 2
                        for tt in range(4):
                            bu, bm = rotU.next(), rotM.next()
                            for k in range(8):
                                add("pe", lambda e, bu=bu, k=k, tt=tt, st0=st0: e.matmul(
                                    banks[bu][:], lhsT=wub[:, k, st0:st0 + 128], rhs=xT[:, k, tt * 512:(tt + 1) * 512],
                                    start=(k == 0), stop=(k == 7)), reads=["wub"] + ["xT:%d" % t for t in range(tt * 4, tt * 4 + 4)], writes=["bank%d" % bu])
                            for n4 in range(4):
                                n = tt * 4 + n4
                                add("pe", lambda e, bm=bm, n4=n4, gi=gi: e.matmul(
                                    banks[bm][:, n4 * 128:(n4 + 1) * 128], lhsT=onesb[:], rhs=bsrep[:, gi * 128:(gi + 1) * 128], start=True, stop=False),
                                    reads=["onesb", "bsrep"], writes=["bank%d" % bm])
                                add("pe", lambda e, bm=bm, n4=n4, gi=gi, n=n, st0=st0: e.matmul(
                                    banks[bm][:, n4 * 128:(n4 + 1) * 128], lhsT=vtok[:, n, st0:st0 + 128], rhs=wsT[:, gi, :], start=False, stop=True),
                                    reads=["vtok:%d" % n, "vtok_z", "wsT"], writes=["bank%d" % bm])
                            u2 = ui % 2
                            ui += 1
                            add("act", lambda e, u2=u2, bu=bu: e.activation(out=ug[u2][:], in_=banks[bu][:], func=AF.Gelu_apprx_tanh),
                                writes=["ug%d" % u2, "bank%d" % bu])
                            add("dve", lambda e, u2=u2, bm=bm, cu=cu, tt=tt: e.tensor_tensor(
                                out=o_bT[:, cu, tt * 512:(tt + 1) * 512], in0=ug[u2][:], in1=banks[bm][:], op=ALU.mult),
                                reads=["ug%d" % u2], writes=["o_bT:%d:%d" % (cu, tt), "bank%d" % bm])
                            if (ui - 1) in gdec_sched:
                                gdec_sched[ui - 1]()
                    S_.flush(barrier=True)
                S_.stop_at("gmlp")
                o_mT = sb("o_mT", [128, 4, TW], BF16, l2b)
                with ExitStack() as l3:
                    memb = sb("memb", [128, 2, D], BF16, l3)
                    memT = sb("memT", [128, 8, 256], BF16, l3)
                    wmem = sb("wmem", [128, 8, D], BF16, l3)
                    wqm = sb("wqm", [128, 8, 512], BF16, l3)
                    w_mem_v = dram["w_mem_kv"].rearrange("(k p) c -> p k c", p=128)
                    for hf in range(2):
                        add("pool", lambda e, hf=hf: e.dma_start(out=wmem[:, :, hf * 512:(hf + 1) * 512], in_=w_mem_v[:, :, hf * 512:(hf + 1) * 512]),
                            writes=["wmem:%d" % hf], dma="wmem%d" % hf)
                    add("pool", lambda e: e.dma_start(out=wqm[:], in_=w_in_v[:, :, 3840:4352]), writes=["wqm"], dma="wqm")
                    mkT = sb("mkT", [128, 4, 256], BF16, l3)
                    mvv = sb("mvv", [128, 2, 512], BF16, l3)
                    mst = [sb("mst%d" % i, [128, 512], F32, l3) for i in range(2)]
                    qmT = sb("qmT", [128, 4, T], BF16, l3)
                    PTm = [sb("PTm%d" % i, [128, 2, 512], BF16, l3) for i in range(2)]
                    rec = [sb("rec%d" % i, [128, 512], F32, l3) for i in range(2)]
                    add("pool", lambda e: e.dma_start(out=memb[:], in_=dram["mem"].rearrange("(t p) c -> p t c", p=128)), writes=["memb"], dma="memb")
                    for mt in range(2):
                        for k in range(8):
                            add("pe", lambda e, mt=mt, k=k: e.transpose(out=bankT[:, k * 128:(k + 1) * 128], in_=memb[:, mt, k * 128:(k + 1) * 128], identity=ident[:]),
                                reads=["memb", "ident"], writes=["bankT"])
                        copy_op("dve", memT[:, :, mt * 128:(mt + 1) * 128], bankT[:].rearrange("p (k c) -> p k c", k=8), [], ["memT:%d" % mt, "bankT"])
                    rotA = Rot([0, 1, 2, 3])
                    mi = 0
                    for mt in range(2):
                        for hf in range(2):
                            b = rotA.next()
                            for k in range(8):
                                add("pe", lambda e, b=b, k=k, mt=mt, hf=hf: e.matmul(
                                    banks[b][:], lhsT=memT[:, k, mt * 128:(mt + 1) * 128], rhs=wmem[:, k, hf * 512:(hf + 1) * 512],
                                    start=(k == 0), stop=(k == 7)), reads=["memT:%d" % mt, "wmem:%d" % hf], writes=["bank%d" % b])
                            m2 = mi % 2
                            mi += 1
                            add("act", lambda e, m2=m2, b=b: e.activation(out=mst[m2][:], in_=banks[b][:], func=AF.Identity),
                                writes=["mst%d" % m2, "bank%d" % b])
                            add("sp", lambda e, m2=m2, mt=mt, hf=hf: e.dma_start(
                                out=dram["new_mem_kv"][mt * 128:(mt + 1) * 128, hf * 512:(hf + 1) * 512], in_=mst[m2][:]),
                                reads=["mst%d" % m2], writes=["o_mst%d" % m2], dma="mst%d" % m2)
                            if hf == 1:
                                add("dve", lambda e, m2=m2, mt=mt: e.tensor_copy(out=mvv[:, mt, :], in_=mst[m2][:]),
                                    reads=["mst%d" % m2], writes=["mvv:%d" % mt])
                    for h in range(4):
                        b = rotA.next()
                        for k in range(8):
                            add("pe", lambda e, b=b, k=k, h=h: e.matmul(
                                banks[b][:, 0:256], lhsT=wmem[:, k, h * 128:(h + 1) * 128], rhs=memT[:, k, :],
                                start=(k == 0), stop=(k == 7)), reads=["memT:0", "memT:1", "wmem:0"], writes=["bank%d" % b])
                        copy_op(evac_eng(), mkT[:, h, :], banks[b][:, 0:256], [], ["mkT:%d" % h, "bank%d" % b])
                    for h in range(4):
                        for tt in range(4):
                            b = rotA.next()
                            for k in range(8):
                                add("pe", lambda e, b=b, k=k, h=h, tt=tt: e.matmul(
                                    banks[b][:], lhsT=wqm[:, k, h * 128:(h + 1) * 128], rhs=xT[:, k, tt * 512:(tt + 1) * 512],
                                    start=(k == 0), stop=(k == 7)), reads=["wqm"] + ["xT:%d" % t for t in range(tt * 4, tt * 4 + 4)], writes=["bank%d" % b])
                            copy_op(evac_eng(), qmT[:, h, tt * 512:(tt + 1) * 512], banks[b][:], [], ["qmT:%d:%d" % (h, tt), "bank%d" % b])
                    its = [(h, tt) for h in range(4) for tt in range(4)]
                    sbanks = [(0, 1), (2, 3)]
                    obanks = [4, 5]

                    def emit_S(i):
                        h, tt = its[i]
                        p2 = i % 2
                        for mt in range(2):
                            b = sbanks[p2][mt]
                            add("pe", lambda e, b=b, h=h, tt=tt, mt=mt: e.matmul(
                                banks[b][:], lhsT=mkT[:, h, mt * 128:(mt + 1) * 128], rhs=qmT[:, h, tt * 512:(tt + 1) * 512], start=True, stop=True),
                                reads=["mkT:%d" % h, "qmT:%d:%d" % (h, tt)], writes=["bank%d" % b])
                            add("act", lambda e, b=b, p2=p2, mt=mt: e.activation(out=PTm[p2][:, mt, :], in_=banks[b][:], func=AF.Exp, scale=float(128 ** -0.5)),
                                writes=["PTm%d:%d" % (p2, mt), "bank%d" % b])

                    def emit_O(i):
                        h, tt = its[i]
                        p2 = i % 2
                        ob = obanks[p2]
                        for mt in range(2):
                            add("pe", lambda e, h=h, mt=mt, p2=p2: e.matmul(
                                banks[6][:], lhsT=onesb[:], rhs=PTm[p2][:, mt, :], start=(mt == 0), stop=(mt == 1)),
                                reads=["onesb", "PTm%d:%d" % (p2, mt)], writes=["bank6"])
                        for mt in range(2):
                            add("pe", lambda e, ob=ob, h=h, mt=mt, p2=p2: e.matmul(
                                banks[ob][:], lhsT=mvv[:, mt, h * 128:(h + 1) * 128], rhs=PTm[p2][:, mt, :], start=(mt == 0), stop=(mt == 1)),
                                reads=["mvv:%d" % mt, "PTm%d:%d" % (p2, mt)], writes=["bank%d" % ob])
                        add("act", lambda e, p2=p2: e.activation(out=rec[p2][:], in_=banks[6][:], func=AF.Ln), writes=["rec%d" % p2, "bank6"])
                        add("act", lambda e, p2=p2: e.activation(out=rec[p2][:], in_=rec[p2][:], func=AF.Exp, scale=-1.0), reads=["rec%d" % p2], writes=["rec%d" % p2])
                        add("dve", lambda e, p2=p2, ob=ob, h=h, tt=tt: e.tensor_tensor(
                            out=o_mT[:, h, tt * 512:(tt + 1) * 512], in0=rec[p2][:], in1=banks[ob][:], op=ALU.mult),
                            reads=["rec%d" % p2], writes=["o_mT:%d:%d" % (h, tt), "bank%d" % ob])
                    dqm = sb("dqm", [128, 4, 8], BF16, l3)
                    cmk = sb("cmk", [128, 2, 4, D], BF16, l3)
                    cmkT = sb("cmkT", [128, 32, 128], BF16, l3)
                    dPm = sb("dPm", [128, 64], BF16, l3)
                    dden = sb("dden", [128, 4, 8], F32, l3)
                    cmkTk = ["cmkT:%d" % i8 for i8 in range(0, 32, 8)]
                    for mt_ in range(2):
                        add("pool", lambda e, mt_=mt_: e.dma_start(out=cmk[:, mt_], in_=dram["cmem"][:, mt_ * 128:(mt_ + 1) * 128, :].rearrange("n p x -> p n x")),
                            writes=["cmk:%d" % mt_], dma="cmk%d" % mt_)
                    add("pool", lambda e: e.memset(o_mT[:, :, T:TW], 0.0), writes=["o_mT:dz"])
                    def mdec_p1():
                        for h in range(4):
                            for k in range(8):
                                add("pe", lambda e, h=h, k=k: e.matmul(banks[4][:, h * 8:(h + 1) * 8], lhsT=wqm[:, k, h * 128:(h + 1) * 128], rhs=xT[:, k, T:TW],
                                                                      start=(k == 0), stop=(k == 7)), reads=["wqm", "xTd"], writes=["bank4"])
                        copy_op("dve", dqm[:], banks[4][:, 0:32].rearrange("p (h n) -> p h n", h=4), [], ["dqm", "bank4"])
                    def mdec_p2(i8):
                        for ii in range(8):
                            idx = i8 + ii
                            n_, h_, mt_ = idx // 8, (idx // 2) % 4, idx % 2
                            add("pe", lambda e, ii=ii, n_=n_, h_=h_, mt_=mt_: e.transpose(
                                out=bankT[:, ii * 128:(ii + 1) * 128], in_=cmk[:, mt_, n_, h_ * 128:(h_ + 1) * 128], identity=ident[:]),
                                reads=["cmk:0", "cmk:1", "ident"], writes=["bankT"])
                        copy_op(evac_eng(), cmkT[:, i8:i8 + 8, :], bankT[:].rearrange("p (k c) -> p k c", k=8), [], ["cmkT:%d" % i8, "bankT"])
                    def mdec_p3():
                        for n_ in range(6):
                            for h_ in range(4):
                                for mt_ in range(2):
                                    col = (n_ * 4 + h_) * 2 + mt_
                                    lhs = cmkT[:, col, :] if n_ < 4 else mkT[:, h_, mt_ * 128:(mt_ + 1) * 128]
                                    add("pe", lambda e, col=col, lhs=lhs, h_=h_, n_=n_: e.matmul(
                                        banks[5][:, col:col + 1], lhsT=lhs, rhs=dqm[:, h_, n_:n_ + 1], start=True, stop=True),
                                        reads=cmkTk + ["dqm"] + ["mkT:%d" % h for h in range(4)], writes=["bank5"])
                        add("act", lambda e: e.activation(out=dPm[:, 0:48], in_=banks[5][:, 0:48], func=AF.Exp, scale=float(128 ** -0.5)), writes=["dPm", "bank5"])
                    def mdec_p4():
                        for n_ in range(6):
                            for h_ in range(4):
                                for (base, which) in ((0, "o"), (64, "d")):
                                    for mt_ in range(2):
                                        scol = (n_ * 4 + h_) * 2 + mt_
                                        if which == "d":
                                            lhs = onesb[:]
                                        elif n_ < 4:
                                            lhs = cmk[:, mt_, n_, 512 + h_ * 128:512 + (h_ + 1) * 128]
                                        else:
                                            lhs = mvv[:, mt_, h_ * 128:(h_ + 1) * 128]
                                        add("pe", lambda e, base=base, h_=h_, n_=n_, mt_=mt_, scol=scol, lhs=lhs: e.matmul(
                                            banks[6][:, base + h_ * 8 + n_:base + h_ * 8 + n_ + 1], lhsT=lhs, rhs=dPm[:, scol:scol + 1],
                                            start=(mt_ == 0), stop=(mt_ == 1)), reads=["dPm", "cmk:0", "cmk:1", "onesb", "mvv:0", "mvv:1"], writes=["bank6"])
                        add("dve", lambda e: e.reciprocal(out=dden[:, :, 0:6], in_=banks[6][:, 64:96].rearrange("p (h n) -> p h n", h=4)[:, :, 0:6]), writes=["dden", "bank6"])
                        add("dve", lambda e: e.tensor_tensor(out=o_mT[:, :, T:T + 6], in0=dden[:, :, 0:6], in1=banks[6][:, 0:32].rearrange("p (h n) -> p h n", h=4)[:, :, 0:6], op=ALU.mult),
                            reads=["dden", "o_mT:dz"], writes=["o_mT:d", "bank6"])
                        if dbg_d is not None:
                            add("dve", lambda e: e.tensor_copy(out=dden[:], in_=o_mT[:, :, T:TW]), reads=["o_mT:d"], writes=["dden"])
                            add("sp", lambda e: e.dma_start(out=dbg_d[:, 0:32], in_=dden[:].rearrange("p a n -> p (a n)")), reads=["dden"], writes=["dbg_d"], dma="dbgd")

                    mdec_sched = {1: mdec_p1, 3: (lambda: mdec_p2(0)), 5: (lambda: mdec_p2(8)), 7: (lambda: mdec_p2(16)), 9: (lambda: mdec_p2(24)), 11: mdec_p3, 13: mdec_p4}
                    emit_S(0)
                    for i in range(len(its)):
                        if i + 1 < len(its):
                            emit_S(i + 1)
                        emit_O(i)
                        if i in mdec_sched:
                            mdec_sched[i]()
                    S_.flush(barrier=True)
                S_.stop_at("mem")
                mixedT = sb("mixedT", [128, 8, TW], BF16, l2b)
                w_out_v = dram["w_out"].rearrange("(k p) c -> p k c", p=128)

                def prefetch_wout():
                    for hf in range(2):
                        add("pool", lambda e, hf=hf: e.dma_start(out=wout[:, :, hf * 512:(hf + 1) * 512], in_=w_out_v[:, :, hf * 512:(hf + 1) * 512]),
                            writes=["wout:%d" % hf], dma="wout%d" % hf)
                    add("sp", lambda e: e.dma_start(out=ln1g[:], in_=dram["ln1_g"].partition_broadcast(128)), writes=["ln1g"], dma="c_ln1g")
                    add("sp", lambda e: e.dma_start(out=ln1b[:], in_=dram["ln1_b"].partition_broadcast(128)), writes=["ln1b"], dma="c_ln1b")
                with ExitStack() as l3:
                    wg = [sb("wg%d" % i, [128, 8, 3, 128], BF16, l3) for i in range(2)]
                    wba = sb("wba", [128, 2, D], BF16, l3)
                    wbb = sb("wbb", [128, 8, D], BF16, l3)
                    wbm = sb("wbm", [128, 4, D], BF16, l3)
                    bgn = sb("bgn", [24, 128], F32, l3)
                    bg = sb("bg", [128, 24], F32, l3)
                    gt = [[sb("gt%d_%d" % (i, j), [128, 512], F32, l3) for j in range(3)] for i in range(2)]
                    tm = [[sb("tm%d_%d" % (i, j), [128, 512], F32, l3) for j in range(3)] for i in range(2)]
                    def load_wg(f):
                        s2 = f % 2
                        for bi in range(3):
                            c0 = 4352 + bi * 1024 + f * 128
                            add("pool", lambda e, s2=s2, bi=bi, c0=c0: e.dma_start(out=wg[s2][:, :, bi, :], in_=w_in_v[:, :, c0:c0 + 128]),
                                writes=["wg%d:%d" % (s2, bi)], dma="wg%d_%d" % (s2, bi))
                    load_wg(0)
                    add("pool", lambda e: e.dma_start(out=wba[:], in_=dram["w_branch_a"].rearrange("(k p) c -> p k c", p=128)), writes=["wba"], dma="wba")
                    add("pool", lambda e: e.dma_start(out=wbm[:], in_=dram["w_branch_m"].rearrange("(k p) c -> p k c", p=128)), writes=["wbm"], dma="wbm")
                    add("pool", lambda e: e.memset(wbb[:], 0.0), writes=["wbb_z"])
                    for cu in range(8):
                        rows = 128 if cu % 2 == 0 else 64
                        add("pool", lambda e, cu=cu, rows=rows: e.dma_start(out=wbb[0:rows, cu, :], in_=dram["w_branch_b"][UST[cu]:UST[cu] + rows, :]),
                            reads=["wbb_z"], writes=["wbb:%d" % cu], dma="wbb%d" % cu)
                    add("sp", lambda e: e.dma_start(out=bgn[:], in_=dram["b_gate"].rearrange("b (f p) -> (b f) p", p=128)), writes=["bgn"], dma="c_bgn")
                    add("pe", lambda e: e.transpose(out=banks[6][:, 0:24], in_=bgn[:], identity=identf[0:24, 0:24]), reads=["bgn", "identf"], writes=["bank6"])
                    copy_op("dve", bg[:], banks[6][:, 0:24], [], ["bg", "bank6"])
                    wbkeys = ["wba", "wbm", "wbb_z"] + ["wbb:%d" % cu for cu in range(8)]
                    obk = ["o_bT:%d:%d" % (cu, tt) for cu in range(8) for tt in range(4)]
                    omk = ["o_mT:%d:%d" % (h, tt) for h in range(4) for tt in range(4)]
                    oak = ["o_aT:%d:%d" % (c, tt) for c in range(2) for tt in range(4)]

                    it = 0
                    for f in range(8):
                        if f + 1 < 8:
                            load_wg(f + 1)
                        s2 = f % 2
                        for tt in range(5):
                            i2 = it % 2
                            it += 1
                            tsl = slice(tt * 512, (tt + 1) * 512) if tt < 4 else slice(T, TW)
                            nn = 512 if tt < 4 else 8
                            xk = ["xT:%d" % t for t in range(tt * 4, tt * 4 + 4)] if tt < 4 else ["xTd"]
                            for bi in range(3):
                                for k in range(8):
                                    add("pe", lambda e, bi=bi, k=k, s2=s2, tsl=tsl, nn=nn: e.matmul(
                                        banks[bi][:, 0:nn], lhsT=wg[s2][:, k, bi, :], rhs=xT[:, k, tsl], start=(k == 0), stop=(k == 7)),
                                        reads=["wg%d:%d" % (s2, bi)] + xk, writes=["bank%d" % bi])
                                add("act", lambda e, bi=bi, i2=i2, f=f, nn=nn: e.activation(
                                    out=gt[i2][bi][:, 0:nn], in_=banks[bi][:, 0:nn], func=AF.Sigmoid, bias=bg[:, bi * 8 + f:bi * 8 + f + 1]),
                                    reads=["bg"], writes=["gt%d_%d" % (i2, bi), "bank%d" % bi])
                            fs = slice(f * 128, (f + 1) * 128)
                            for (bi, wt, nk, src, keys) in ((0, wba, 2, o_aT, oak), (1, wbb, 8, o_bT, obk), (2, wbm, 4, o_mT, omk)):
                                for k in range(nk):
                                    add("pe", lambda e, bi=bi, k=k, wt=wt, src=src, nk=nk, fs=fs, tsl=tsl, nn=nn: e.matmul(
                                        banks[3 + bi][:, 0:nn], lhsT=wt[:, k, fs], rhs=src[:, k, tsl], start=(k == 0), stop=(k == nk - 1)),
                                        reads=wbkeys + keys, writes=["bank%d" % (3 + bi)])
                                add("dve", lambda e, bi=bi, i2=i2, nn=nn: e.tensor_tensor(out=tm[i2][bi][:, 0:nn], in0=gt[i2][bi][:, 0:nn], in1=banks[3 + bi][:, 0:nn], op=ALU.mult),
                                    reads=["gt%d_%d" % (i2, bi)], writes=["tm%d_%d" % (i2, bi), "bank%d" % (3 + bi)])
                            add("pool", lambda e, i2=i2, nn=nn: e.tensor_tensor(out=tm[i2][0][:, 0:nn], in0=tm[i2][0][:, 0:nn], in1=tm[i2][1][:, 0:nn], op=ALU.add),
                                reads=["tm%d_1" % i2], writes=["tm%d_0" % i2])
                            add("pool", lambda e, i2=i2, f=f, tsl=tsl, nn=nn: e.tensor_tensor(out=mixedT[:, f, tsl], in0=tm[i2][0][:, 0:nn], in1=tm[i2][2][:, 0:nn], op=ALU.add),
                                reads=["tm%d_0" % i2, "tm%d_2" % i2], writes=["mixedT:%d:%d" % (f, tt)])
                    S_.flush(barrier=True)
                S_.stop_at("merge")
                with ExitStack() as l3:
                    wout = sb("wout", [128, 8, D], BF16, l3)
                    ln1g = sb("ln1g", [128, D], F32, l3)
                    ln1b = sb("ln1b", [128, D], F32, l3)
                    prefetch_wout()
                    xres = [sb("xres%d" % i, [128, D], F32, l3) for i in range(3)]
                    zt = [sb("zt%d" % i, [128, D], F32, l3) for i in range(3)]
                    zn = [sb("zn%d" % i, [128, D], F32, l3) for i in range(3)]
                    x1f = [sb("x1f%d" % i, [128, D], F32, l3) for i in range(3)]
                    x1b = [sb("x1b%d" % i, [128, D], BF16, l3) for i in range(3)]
                    stats = [sb("stats1_%d" % i, [128, 2, 6], F32, l3) for i in range(3)]
                    mv_ = [sb("mv1_%d" % i, [128, 4], F32, l3) for i in range(3)]
                    rotZ = Rot([0, 1, 2, 3, 4, 5])
                    mixk = ["mixedT:%d:%d" % (f, tt) for f in range(8) for tt in range(5)]
                    mxdp = sb("mxdp", [128, 8, 128], BF16, l3)
                    add("pool", lambda e: e.memset(mxdp[:], 0.0), writes=["mxdp"])
                    add("pool", lambda e: e.tensor_copy(out=mxdp[:, :, 0:8], in_=mixedT[:, :, T:TW]), reads=["mxdp"] + ["mixedT:%d:4" % f for f in range(8)], writes=["mxdp"])
                    bzs = {}

                    def ln1_mm(t):
                        bz = [rotZ.next(), rotZ.next()]
                        bzs[t] = bz
                        i2 = t % 3
                        if t < NT:
                            add("sp", lambda e, i2=i2, t=t: e.dma_start(out=xres[i2][:], in_=xh[T + t * 128:T + (t + 1) * 128, :]), writes=["xres%d" % i2], dma="xres%d" % i2)
                        else:
                            add("pool", lambda e, i2=i2: e.memset(xres[i2][:], 0.0), writes=["xres%d" % i2])
                            add("sp", lambda e, i2=i2: e.dma_start(out=xres[i2][0:8, :], in_=dram["xd"]), reads=["xres%d" % i2], writes=["xresd"], dma="xresd")
                        for hf in range(2):
                            for k in range(8):
                                lhs = mixedT[:, k, t * 128:(t + 1) * 128] if t < NT else mxdp[:, k, :]
                                add("pe", lambda e, hf=hf, k=k, lhs=lhs, bz=bz: e.matmul(
                                    banks[bz[hf]][:], lhsT=lhs, rhs=wout[:, k, hf * 512:(hf + 1) * 512],
                                    start=(k == 0), stop=(k == 7)), reads=["mixedT:%d:%d" % (k, t // 4), "wout:%d" % hf, "mxdp"], writes=["bank%d" % bz[hf]])

                    def ln1_A1(t):
                        i2 = t % 3
                        bz = bzs[t]
                        for hf in range(2):
                            add("dve", lambda e, hf=hf, i2=i2, bz=bz: e.scalar_tensor_tensor(
                                out=zt[i2][:, hf * 512:(hf + 1) * 512], in0=xres[i2][:, hf * 512:(hf + 1) * 512], scalar=ALPHA, in1=banks[bz[hf]][:],
                                op0=ALU.mult, op1=ALU.add), reads=["xres%d" % i2, "xresd"], writes=["zt%d:%d" % (i2, hf), "bank%d" % bz[hf]])
                            add("dve", lambda e, hf=hf, i2=i2: e.bn_stats(out=stats[i2][:, hf, :], in_=zt[i2][:, hf * 512:(hf + 1) * 512]),
                                reads=["zt%d:%d" % (i2, hf)], writes=["st1_%d:%d" % (i2, hf)])
                        add("dve", lambda e, i2=i2: e.bn_aggr(out=mv_[i2][:, 0:2], in_=stats[i2][:].rearrange("p a b -> p (a b)")),
                            reads=["st1_%d:0" % i2, "st1_%d:1" % i2], writes=["mv1_%d" % i2])
                        add("dve", lambda e, i2=i2: e.tensor_scalar(out=mv_[i2][:, 2:3], in0=mv_[i2][:, 1:2], scalar1=EPS, scalar2=None, op0=ALU.add),
                            reads=["mv1_%d" % i2], writes=["mv1_%d" % i2])

                    def ln1_A2a(t):
                        i2 = t % 3
                        add("act", lambda e, i2=i2: e.activation(out=mv_[i2][:, 2:3], in_=mv_[i2][:, 2:3], func=AF.Sqrt),
                            reads=["mv1_%d" % i2], writes=["mv1_%d" % i2])

                    def ln1_A2b(t):
                        i2 = t % 3
                        add("dve", lambda e, i2=i2: e.reciprocal(out=mv_[i2][:, 2:3], in_=mv_[i2][:, 2:3]),
                            reads=["mv1_%d" % i2], writes=["mv1_%d" % i2])
                        add("dve", lambda e, i2=i2: e.scalar_tensor_tensor(out=mv_[i2][:, 3:4], in0=mv_[i2][:, 0:1], scalar=-1.0, in1=mv_[i2][:, 2:3], op0=ALU.mult, op1=ALU.mult),
                            reads=["mv1_%d" % i2], writes=["mv1_%d" % i2])

                    def ln1_B1(t):
                        i2 = t % 3
                        add("act", lambda e, i2=i2: e.activation(out=zn[i2][:], in_=zt[i2][:], func=AF.Identity, scale=mv_[i2][:, 2:3], bias=mv_[i2][:, 3:4]),
                            reads=["zt%d:0" % i2, "zt%d:1" % i2, "mv1_%d" % i2], writes=["zn%d" % i2])

                    def ln1_B2a(t):
                        i2 = t % 3
                        add("dve", lambda e, i2=i2: e.tensor_tensor(out=zn[i2][:], in0=zn[i2][:], in1=ln1g[:], op=ALU.mult),
                            reads=["ln1g"], writes=["zn%d" % i2])

                    def ln1_B2b(t):
                        i2 = t % 3
                        add("pool", lambda e, i2=i2: e.tensor_tensor(out=x1f[i2][:], in0=zn[i2][:], in1=ln1b[:], op=ALU.add),
                            reads=["zn%d" % i2, "ln1b"], writes=["x1f%d" % i2])
                        add("sp", lambda e, i2=i2, t=t: e.dma_start(out=x1s[t * 128:(t + 1) * 128, :], in_=x1f[i2][:]),
                            reads=["x1f%d" % i2], writes=["x1s:%d" % t], dma="x1f%d" % i2)
                        add("act", lambda e, i2=i2: e.activation(out=x1b[i2][:], in_=x1f[i2][:], func=AF.Identity), reads=["x1f%d" % i2], writes=["x1b%d" % i2])

                    def ln1_stage(tA, tB):
                        if tB is not None:
                            ln1_B1(tB)
                        if tA is not None:
                            ln1_A1(tA)
                            ln1_A2a(tA)
                        if tB is not None:
                            ln1_B2a(tB)
                        if tA is not None:
                            ln1_A2b(tA)
                        if tB is not None:
                            ln1_B2b(tB)

                    def ln1_tr(t):
                        i2 = t % 3
                        bt, btk = (bankT[:], "bankT") if t % 2 == 0 else (bankT2, "bank6")
                        for k in range(8):
                            add("pe", lambda e, i2=i2, k=k, bt=bt: e.transpose(out=bt[:, k * 128:(k + 1) * 128], in_=x1b[i2][:, k * 128:(k + 1) * 128], identity=ident[:]),
                                reads=["x1b%d" % i2, "ident"], writes=[btk])
                        if t < NT:
                            copy_op("dve", xT[:, :, t * 128:(t + 1) * 128], bt.rearrange("p (k c) -> p k c", k=8), mixk, ["x1T:%d" % t, btk])
                        else:
                            copy_op("dve", xT[:, :, T:TW], bt.rearrange("p (k c) -> p k c", k=8)[:, :, 0:8], mixk + ["mxdp"], ["x1T:d", btk])

                    ln1_mm(0)
                    ln1_mm(1)
                    ln1_mm(2)
                    ln1_stage(0, None)
                    ln1_stage(1, 0)
                    for t in range(NT + 1):
                        if t + 3 <= NT:
                            ln1_mm(t + 3)
                        ln1_stage(t + 2 if t + 2 <= NT else None, t + 1 if t + 1 <= NT else None)
                        ln1_tr(t)
                    if dbg_x1 is not None:
                        for t in range(NT):
                            add("sp", lambda e, t=t: e.dma_start(out=dbg_x1[t * 128:(t + 1) * 128, :], in_=x1s[t * 128:(t + 1) * 128, :]),
                                reads=["x1s:%d" % t], writes=["dbgx1:%d" % t], dma="dbgx1")
                    S_.flush(barrier=True)
                S_.stop_at("ln1")
        S_.stop_at("pre_ffn")


        with ExitStack() as l4:
            hT = sb("hT", [128, NJ, TW], BF16, l4)
            cwn = sb("cwn", [88, 128], F32, l4)
            cw = sb("cw", [128, 88], F32, l4)
            alast = sb("alast", [128, 2, NJ], F32, l4)
            dav = sb("dav", [128, NJ, 2, 8], F32, l4)
            bflag = sb("bflag_sb", [128, 1], F32, l4)
            cstA = sb("cstA", [128, 128], F32, l4)
            cstB = sb("cstB", [128, 128], F32, l4)
            dst_ = sb("dst_", [128, NJ, 8], F32, l4)
            tcv = sb("tcv", [128, NJ, 4], F32, l4)
            tgd = sb("tgd", [128, NJ, 4], F32, l4)
            dtmp = sb("dtmp", [128, 4, NJ], F32, l4)
            dat88 = sb("dat88", [128, 128], F32, l4)
            add("sp", lambda e: e.dma_start(out=bflag[:], in_=dram["bflag"]), writes=["bflag"], dma="c_bflag")

            def halo_fn(j, AB, a2):
                add("pool", lambda e, AB=AB, j=j: e.tensor_scalar(out=AB[:, 0:2], in0=dav[:, j, 0, 4:6], scalar1=bflag[:, 0:1], scalar2=None, op0=ALU.mult),
                    reads=["dav:%d" % j, "bflag"], writes=["abuf%d:h" % a2])
            add("sp", lambda e: e.dma_start(out=cwn[0:66, :], in_=dram["conv_w"].rearrange("k (j p) -> (k j) p", p=128)), writes=["cwn:0"], dma="c_cw0")
            add("sp", lambda e: e.dma_start(out=cwn[66:88, :], in_=dram["conv_b"].rearrange("(j p) -> j p", p=128)), writes=["cwn:1"], dma="c_cw1")
            add("pe", lambda e: e.transpose(out=banks[6][:, 0:88], in_=cwn[:], identity=identf[0:88, 0:88]), reads=["cwn:0", "cwn:1", "identf"], writes=["bank6"])
            copy_op("dve", cw[:], banks[6][:, 0:88], [], ["cw", "bank6"])
            w_up_v = dram["w_up"].rearrange("(k p) c -> p k c", p=128)
            w_dn_v = dram["w_down"].rearrange("(j p) c -> p j c", p=128)
            wdnA = sb("wdnA", [128, 12, D], BF16, l4)
            with ExitStack() as l5:
                wup = [sb("wup%d" % i, [128, 8, 2, 256], BF16, l5) for i in range(2)]
                abuf = [sb("abuf%d" % i, [128, 2 + T], F32, l5) for i in range(2)]
                t0 = [sb("t0_%d" % i, [128, 512], F32, l5) for i in range(2)]
                t1 = [sb("t1_%d" % i, [128, 512], F32, l5) for i in range(2)]
                gl = [sb("gl%d" % i, [128, 512], F32, l5) for i in range(2)]

                def load_wup(jp):
                    s2 = jp % 2
                    for hv in range(2):
                        c0 = hv * DFF + jp * 256
                        add("pool", lambda e, s2=s2, hv=hv, c0=c0: e.dma_start(out=wup[s2][:, :, hv, :], in_=w_up_v[:, :, c0:c0 + 256]),
                            writes=["wup%d:%d" % (s2, hv)], dma="wup%d_%d" % (s2, hv))
                load_wup(0)
                rotA = Rot([0, 1, 2])
                rotB = Rot([3, 4, 5])
                it = 0
                for jp in range(NJ // 2):
                    if jp + 1 < NJ // 2:
                        load_wup(jp + 1)
                    if jp in (1, 3):
                        j0 = 0 if jp == 1 else 6
                        add("pool", lambda e, j0=j0: e.dma_start(out=wdnA[:, j0:j0 + 6, :], in_=w_dn_v[:, j0:j0 + 6, :]), writes=["wdnA:%d" % j0], dma="wdnA%d" % j0)
                    s2 = jp % 2
                    for jj in range(2):
                        j = 2 * jp + jj
                        a2 = j % 2
                        AB = abuf[a2]
                        for hv in range(2):
                            for k in range(8):
                                add("pe", lambda e, hv=hv, k=k, s2=s2, jj=jj: e.matmul(
                                    banks[6][:, hv * 8:(hv + 1) * 8], lhsT=wup[s2][:, k, hv, jj * 128:(jj + 1) * 128], rhs=xT[:, k, T:TW],
                                    start=(k == 0), stop=(k == 7)), reads=["wup%d:%d" % (s2, hv), "x1T:d"], writes=["bank6"])
                        add("dve", lambda e, j=j: e.tensor_copy(out=dav[:, j], in_=banks[6][:, 0:16].rearrange("p (a n) -> p a n", a=2)),
                            writes=["dav:%d" % j, "bank6"])
                        halo_fn(j, AB, a2)
                        for tt in range(4):
                            i2 = it % 2
                            it += 1
                            ba, bv = rotA.next(), rotB.next()
                            xk = ["x1T:%d" % t for t in range(tt * 4, tt * 4 + 4)]
                            for (b, hv) in ((ba, 0), (bv, 1)):
                                for k in range(8):
                                    add("pe", lambda e, b=b, hv=hv, k=k, s2=s2, jj=jj, tt=tt: e.matmul(
                                        banks[b][:], lhsT=wup[s2][:, k, hv, jj * 128:(jj + 1) * 128], rhs=xT[:, k, tt * 512:(tt + 1) * 512],
                                        start=(k == 0), stop=(k == 7)), reads=["wup%d:%d" % (s2, hv)] + xk, writes=["bank%d" % b])
                            c0 = 2 + tt * 512
                            add("act", lambda e, AB=AB, ba=ba, c0=c0: e.activation(out=AB[:, c0:c0 + 512], in_=banks[ba][:], func=AF.Identity),
                                writes=["abuf%d:%d" % (a2, tt), "bank%d" % ba])
                            add("act", lambda e, i2=i2, ba=ba, j=j: e.activation(out=t0[i2][:], in_=banks[ba][:], func=AF.Identity,
                                                                               scale=cw[:, 44 + j:45 + j], bias=cw[:, 66 + j:67 + j]),
                                reads=["cw"], writes=["t0_%d" % i2, "bank%d" % ba])
                            prevk = ["abuf%d:%d" % (a2, tt - 1)] if tt > 0 else ["abuf%d:h" % a2]
                            add("dve", lambda e, i2=i2, AB=AB, c0=c0, j=j: e.scalar_tensor_tensor(
                                out=t1[i2][:], in0=AB[:, c0 - 1:c0 + 511], scalar=cw[:, 22 + j:23 + j], in1=t0[i2][:], op0=ALU.mult, op1=ALU.add),
                                reads=["cw", "t0_%d" % i2, "abuf%d:%d" % (a2, tt)] + prevk, writes=["t1_%d" % i2])
                            add("dve", lambda e, i2=i2, AB=AB, c0=c0, j=j: e.scalar_tensor_tensor(
                                out=t1[i2][:], in0=AB[:, c0 - 2:c0 + 510], scalar=cw[:, j:j + 1], in1=t1[i2][:], op0=ALU.mult, op1=ALU.add),
                                reads=["cw", "abuf%d:%d" % (a2, tt)] + prevk, writes=["t1_%d" % i2])
                            add("act", lambda e, i2=i2: e.activation(out=gl[i2][:], in_=t1[i2][:], func=AF.Gelu_apprx_tanh),
                                reads=["t1_%d" % i2], writes=["gl%d" % i2])
                            add("dve", lambda e, i2=i2, bv=bv, j=j, tt=tt: e.tensor_tensor(
                                out=hT[:, j, tt * 512:(tt + 1) * 512], in0=gl[i2][:], in1=banks[bv][:], op=ALU.mult),
                                reads=["gl%d" % i2], writes=["hT:%d:%d" % (j, tt), "bank%d" % bv])
                        add("pool", lambda e, AB=AB, j=j: e.tensor_copy(out=alast[:, :, j], in_=AB[:, T:T + 2]),
                            reads=["abuf%d:3" % a2], writes=["alast:%d" % j])
                add("pe", lambda e: e.transpose(out=banks[6][0:44, 0:128], in_=alast[:].rearrange("p t j -> p (t j)"), identity=identf[:]),
                    reads=["alast:%d" % j for j in range(NJ)] + ["identf"], writes=["bank6"])
                add("act", lambda e: e.activation(out=t0[0][0:44, 0:128], in_=banks[6][0:44, 0:128], func=AF.Identity), writes=["t0_0", "bank6"])
                add("sp", lambda e: e.dma_start(out=dram["conv_p"].rearrange("t (j p) -> (t j) p", p=128), in_=t0[0][0:44, 0:128]),
                    reads=["t0_0"], writes=["o_convp"], dma="convp")
                S_.flush(barrier=True)
            S_.stop_at("ffn_up")
            with ExitStack() as l5:
                wdnB = sb("wdnB", [128, NJ - 12, D], BF16, l5)
                ln2g = sb("ln2g", [128, D], F32, l5)
                ln2b = sb("ln2b", [128, D], F32, l5)
                xres = [sb("x1r%d" % i, [128, D], F32, l5) for i in range(1)]
                zt = sb("ztB", [128, D], F32, l5)
                zn = zt
                yst = [sb("yst%d" % i, [128, D], F32, l5) for i in range(2)]
                stats = [sb("stats2_%d" % i, [128, 2, 6], F32, l5) for i in range(2)]
                mv_ = [sb("mv2_%d" % i, [128, 4], F32, l5) for i in range(2)]
                for (j0, j1) in ((12, 17), (17, NJ)):
                    add("pool", lambda e, j0=j0, j1=j1: e.dma_start(out=wdnB[:, j0 - 12:j1 - 12, :], in_=w_dn_v[:, j0:j1, :]), writes=["wdn:%d" % j0], dma="wdn%d" % j0)
                wdk = ["wdn:12", "wdn:17"]
                cstv = dram["cstate"].rearrange("n s (j p) -> (n s j) p", p=128)
                add("sp", lambda e: e.dma_start(out=cstA[:], in_=cstv[0:128, :]), writes=["cstA"], dma="c_cstA")
                add("sp", lambda e: e.dma_start(out=cstB[0:48, :], in_=cstv[128:176, :]), writes=["cstB"], dma="c_cstB")
                add("sp", lambda e: e.dma_start(out=dram["conv_s"][:, 0, :], in_=dram["cstate"][:, 1, :]), writes=["o_convs0"], dma="o_convs0")
                add("pe", lambda e: e.transpose(out=banks[5][:, 0:128], in_=cstA[:], identity=identf[:]), reads=["cstA", "identf"], writes=["bank5"])
                add("pe", lambda e: e.transpose(out=banks[5][:, 128:176], in_=cstB[0:48, :], identity=identf[0:48, 0:48]), reads=["cstB", "identf"], writes=["bank5"])
                add("dve", lambda e: e.tensor_copy(out=dst_[:], in_=banks[5][:, 0:176].rearrange("p (a j) -> p j a", a=8)), writes=["dst_", "bank5"])
                davk = ["dav:%d" % j for j in range(NJ)]
                for j in range(NJ):
                    add("dve", lambda e, j=j: e.tensor_scalar(out=tcv[:, j, :], in0=dst_[:, j, 0:8:2], scalar1=cw[:, j:j + 1], scalar2=cw[:, 66 + j:67 + j],
                                                             op0=ALU.mult, op1=ALU.add), reads=["dst_", "cw"], writes=["tcv:%d" % j])
                    add("dve", lambda e, j=j: e.scalar_tensor_tensor(out=tcv[:, j, :], in0=dst_[:, j, 1:8:2], scalar=cw[:, 22 + j:23 + j], in1=tcv[:, j, :],
                                                                    op0=ALU.mult, op1=ALU.add), reads=["dst_", "cw", "tcv:%d" % j], writes=["tcv:%d" % j])
                    add("dve", lambda e, j=j: e.scalar_tensor_tensor(out=tcv[:, j, :], in0=dav[:, j, 0, 0:4], scalar=cw[:, 44 + j:45 + j], in1=tcv[:, j, :],
                                                                    op0=ALU.mult, op1=ALU.add), reads=["cw", "tcv:%d" % j], writes=["tcv:%d" % j])
                add("act", lambda e: e.activation(out=tgd[:].rearrange("p j n -> p (j n)"), in_=tcv[:].rearrange("p j n -> p (j n)"), func=AF.Gelu_apprx_tanh),
                    reads=["tcv:%d" % j for j in range(NJ)], writes=["tgd"])
                add("pool", lambda e: e.memset(hT[:, :, T:TW], 0.0), writes=["hT:dz"])
                add("dve", lambda e: e.tensor_tensor(out=hT[:, :, T:T + 4], in0=tgd[:], in1=dav[:, :, 1, 0:4], op=ALU.mult),
                    reads=["tgd", "hT:dz"], writes=["hT:d"])
                add("dve", lambda e: e.tensor_copy(out=dtmp[:], in_=dav[:, :, 0, 0:4].rearrange("p j n -> p n j")), writes=["dtmp"])
                add("pe", lambda e: e.transpose(out=banks[4][0:88, 0:128], in_=dtmp[:].rearrange("p n j -> p (n j)"), identity=identf[:]),
                    reads=["dtmp", "identf"], writes=["bank4"])
                add("act", lambda e: e.activation(out=dat88[0:88, :], in_=banks[4][0:88, 0:128], func=AF.Identity), writes=["dat88", "bank4"])
                for n_ in range(4):
                    add("sp", lambda e, n_=n_: e.dma_start(out=dram["conv_s"][n_, 1, :].rearrange("(j p) -> j p", p=128), in_=dat88[n_ * NJ:(n_ + 1) * NJ, :]),
                        reads=["dat88"], writes=["o_convs1:%d" % n_], dma="o_convs1")
                add("sp", lambda e: e.dma_start(out=ln2g[:], in_=dram["ln2_g"].partition_broadcast(128)), writes=["ln2g"], dma="c_ln2g")
                add("sp", lambda e: e.dma_start(out=ln2b[:], in_=dram["ln2_b"].partition_broadcast(128)), writes=["ln2b"], dma="c_ln2b")
                rotZ = Rot([0, 1, 2, 3, 4, 5])
                hdp = sb("hdp", [128, NJ, 128], BF16, l5)
                add("pool", lambda e: e.memset(hdp[:], 0.0), writes=["hdp"])
                add("pool", lambda e: e.tensor_copy(out=hdp[:, :, 0:8], in_=hT[:, :, T:TW]), reads=["hdp", "hT:d", "hT:dz"], writes=["hdp"])
                def ln2_load(t):
                    add("sp", lambda e, t=t: e.dma_start(out=xres[0][:], in_=x1s[t * 128:(t + 1) * 128, :]), writes=["x1r0"], dma="x1r0")
                ln2_load(0)
                for t in range(NT + 1):
                    i2 = t % 2
                    bz = [rotZ.next(), rotZ.next()]
                    for hf in range(2):
                        for j in range(NJ):
                            lhs = hT[:, j, t * 128:(t + 1) * 128] if t < NT else hdp[:, j, :]
                            add("pe", lambda e, hf=hf, j=j, lhs=lhs, bz=bz: e.matmul(
                                banks[bz[hf]][:], lhsT=lhs, rhs=(wdnA[:, j, hf * 512:(hf + 1) * 512] if j < 12 else wdnB[:, j - 12, hf * 512:(hf + 1) * 512]),
                                start=(j == 0), stop=(j == NJ - 1)), reads=wdk + ["hT:%d:%d" % (j, t // 4), "hdp"], writes=["bank%d" % bz[hf]])
                        add("dve", lambda e, hf=hf, i2=i2, bz=bz: e.scalar_tensor_tensor(
                            out=zt[:, hf * 512:(hf + 1) * 512], in0=xres[0][:, hf * 512:(hf + 1) * 512], scalar=ALPHA, in1=banks[bz[hf]][:],
                            op0=ALU.mult, op1=ALU.add), reads=["x1r0"], writes=["zt2:%d" % hf, "bank%d" % bz[hf]])
                        add("dve", lambda e, hf=hf, i2=i2: e.bn_stats(out=stats[i2][:, hf, :], in_=zt[:, hf * 512:(hf + 1) * 512]),
                            reads=["zt2:%d" % hf], writes=["st2_%d:%d" % (i2, hf)])
                    if t + 1 <= NT:
                        ln2_load(t + 1)
                    add("dve", lambda e, i2=i2: e.bn_aggr(out=mv_[i2][:, 0:2], in_=stats[i2][:].rearrange("p a b -> p (a b)")),
                        reads=["st2_%d:0" % i2, "st2_%d:1" % i2], writes=["mv2_%d" % i2])
                    add("dve", lambda e, i2=i2: e.tensor_scalar(out=mv_[i2][:, 2:3], in0=mv_[i2][:, 1:2], scalar1=EPS, scalar2=None, op0=ALU.add),
                        reads=["mv2_%d" % i2], writes=["mv2_%d" % i2])
                    add("pool", lambda e, i2=i2: e.tensor_tensor(out=mv_[i2][:, 2:3], in0=mv_[i2][:, 2:3], in1=mhalf[:, 0:1], op=ALU.pow),
                        reads=["mv2_%d" % i2, "mhalf"], writes=["mv2_%d" % i2])
                    add("dve", lambda e, i2=i2: e.scalar_tensor_tensor(out=mv_[i2][:, 3:4], in0=mv_[i2][:, 0:1], scalar=-1.0, in1=mv_[i2][:, 2:3], op0=ALU.mult, op1=ALU.mult),
                        reads=["mv2_%d" % i2], writes=["mv2_%d" % i2])
                    add("act", lambda e, i2=i2: e.activation(out=zn[:], in_=zt[:], func=AF.Identity, scale=mv_[i2][:, 2:3], bias=mv_[i2][:, 3:4]),
                        reads=["mv2_%d" % i2], writes=["zn2", "zt2:0", "zt2:1"])
                    add("dve", lambda e: e.tensor_tensor(out=zn[:], in0=zn[:], in1=ln2g[:], op=ALU.mult), reads=["ln2g"], writes=["zn2"])
                    add("pool", lambda e, i2=i2: e.tensor_tensor(out=yst[i2][:], in0=zn[:], in1=ln2b[:], op=ALU.add),
                        reads=["ln2b"], writes=["yst%d" % i2, "zn2", "zt2:0", "zt2:1"])
                    if t < NT:
                        add("sp", lambda e, i2=i2, t=t: e.dma_start(out=dram["y_p"][t * 128:(t + 1) * 128, :], in_=yst[i2][:]),
                            reads=["yst%d" % i2], writes=["o_y:%d" % t], dma="yst%d" % i2)
                    else:
                        add("sp", lambda e, i2=i2: e.dma_start(out=dram["y_s"], in_=yst[i2][0:4, :]),
                            reads=["yst%d" % i2], writes=["o_ys"], dma="yst%d" % i2)
                S_.flush(barrier=True)
        S_.flush(barrier=True)
    print("instructions emitted:", S_.nins)
    return nc


_CACHE = {}


def kernel(**inputs):
    x = np.ascontiguousarray(inputs["x_prompt"][0])
    if "nc" not in _CACHE:
        _CACHE["nc"] = build_program()
    nc = _CACHE["nc"]
    in_maps = []
    xs = np.ascontiguousarray(inputs["x_sample"][:, 0, :])
    caches = [np.ascontiguousarray(inputs[k][0]).reshape(32, -1, 512) for k in ("cache_win128_kv", "cache_win512_kv", "cache_win2048_kv")]
    cmem = np.ascontiguousarray(inputs["cache_mem_kv"][0]).reshape(32, 256, 1024)
    cstate = np.ascontiguousarray(inputs["state_ffn_conv"][0])
    for c in range(NCORES):
        xh = np.zeros((2 * T, D), np.float32)
        if c > 0:
            xh[:T] = x[(c - 1) * T:c * T]
        xh[T:] = x[c * T:(c + 1) * T]
        hb = np.full((128, 1), NEG if c == 0 else 0.0, np.float32)
        xd = np.zeros((8, D), np.float32)
        xd[0:4] = xs[4 * c:4 * c + 4]
        xbk = np.zeros((2, 3, 128, D), np.float32)
        bkmask = np.zeros((128, 6), np.float32)
        if c > 0:
            for nb in range(2):
                p = c * T - 2 + nb
                xd[4 + nb] = x[p]
                for g, (win, dil) in enumerate(GROUPS):
                    pos = p - dil * np.arange(1, 129)
                    ok = pos >= 0
                    xbk[nb, g, ok] = x[pos[ok]]
                    bkmask[~ok, nb * 3 + g] = NEG
        m = {"xh": xh, "hbias": hb, "mem": np.ascontiguousarray(inputs["mem_prompt"][0]),
             "xd": xd, "xbk": xbk, "bkmask": bkmask, "bflag": np.full((128, 1), 0.0 if c == 0 else 1.0, np.float32),
             "cwin0": caches[0][4 * c:4 * c + 4], "cwin1": caches[1][4 * c:4 * c + 4], "cwin2": caches[2][4 * c:4 * c + 4],
             "cmem": cmem[4 * c:4 * c + 4], "cstate": cstate[4 * c:4 * c + 4]}
        for nm in ("w_in", "w_mem_kv", "ln_v_g", "ln_v_b", "w_spatial", "b_spatial", "w_branch_a", "w_branch_b", "w_branch_m",
                   "b_gate", "w_out", "ln1_g", "ln1_b", "w_up", "conv_w", "conv_b", "w_down", "ln2_g", "ln2_b"):
            m[nm] = np.ascontiguousarray(inputs[nm][0])
        in_maps.append(m)
    sel = os.environ.get("MK_CORES")
    if sel:
        ids = [int(v) for v in sel.split(",")]
        res = run_bass_kernel_spmd(nc, [in_maps[i] for i in ids], core_ids=list(range(len(ids))))
        return res
    res = run_bass_kernel_spmd(nc, in_maps, core_ids=list(range(NCORES)))
    R = res.results
    f32 = np.float32
    y_prompt = np.concatenate([R[c]["y_p"] for c in range(NCORES)], axis=0).reshape(1, S, D).astype(f32)
    y_sample = np.concatenate([R[c]["y_s"] for c in range(NCORES)], axis=0).reshape(32, 1, D).astype(f32)
    win_p = [np.asarray(R[NCORES - 1]["win%d_p" % g]).reshape(1, 1, GROUPS[g][0], 2, 4, 64).astype(f32) for g in range(3)]
    mem_p = np.asarray(R[0]["new_mem_kv"]).reshape(1, 1, 256, 2, 4, 128).astype(f32)
    conv_p = np.asarray(R[NCORES - 1]["conv_p"]).reshape(1, 1, 2, DFF).astype(f32)
    win_s = [np.concatenate([R[c]["win%d_s" % g] for c in range(NCORES)], axis=0).reshape(1, 32, 1, 2, 4, 64).astype(f32) for g in range(3)]
    gv_s = np.concatenate([R[c]["gmlp_v_s"] for c in range(NCORES)], axis=0).reshape(1, 32, 1, 768).astype(f32)
    conv_s = np.concatenate([R[c]["conv_s"] for c in range(NCORES)], axis=0).reshape(1, 32, 2, DFF).astype(f32)
    return (y_prompt, y_sample, win_p[0], win_p[1], win_p[2], mem_p, conv_p, win_s[0], win_s[1], win_s[2], gv_s, conv_s)
```
